# Optimizing a Trainium2 kernel written in Bass

```python
import jax, jax.numpy as jnp
from jax import lax
import numpy as np

D_MODEL = 1024
BATCH = 8
SEQ = 2048
DEPTH = 2

N_HEADS = 8
HEAD_DIM = 64
D_ATT = N_HEADS * HEAD_DIM
IDX_HEADS = 8
IDX_DIM = 32
TOPK_MAX = 256
Q_BLOCK = 128
D_RNN = 1024
RNN_BLOCKS = 16
RNN_BW = D_RNN // RNN_BLOCKS
CONV_W = 4
LRU_C = 8.0
D_FF = 2816
EPS = 1e-6

SPLITS = (D_ATT, HEAD_DIM, HEAD_DIM, IDX_HEADS * IDX_DIM, IDX_DIM, IDX_HEADS,
          D_RNN, D_RNN, D_MODEL, D_MODEL)
N_IN = sum(SPLITS)

kernel_name = "hybrid_dsa_rglru_macaron"


def rmsnorm(x, g):
    x32 = x.astype(jnp.float32)
    y = x32 * lax.rsqrt(jnp.mean(x32 * x32, axis=-1, keepdims=True) + EPS)
    return (y * g.astype(jnp.float32)).astype(x.dtype)


def swiglu(x, w_gate, w_up, w_down):
    return (jax.nn.silu(x @ w_gate) * (x @ w_up)) @ w_down


def split_columns(z):
    offsets = [int(o) for o in np.cumsum(SPLITS)[:-1]]
    return jnp.split(z, offsets, axis=-1)


def causal_dwconv(x, w, b):
    c = x.shape[-1]
    y = lax.conv_general_dilated(
        x, w[:, None, :], window_strides=(1,), padding=[(CONV_W - 1, 0)],
        dimension_numbers=("NWC", "WIO", "NWC"), feature_group_count=c)
    return y + b


def block_diag_linear(x, w, b):
    bsz, s, _ = x.shape
    xb = x.reshape(bsz, s, RNN_BLOCKS, RNN_BW)
    return jnp.einsum("bsni,nij->bsnj", xb, w).reshape(bsz, s, D_RNN) + b


def rg_lru(x, w_a, b_a, w_x, b_x, lam):
    r = jax.nn.sigmoid(block_diag_linear(x, w_a, b_a).astype(jnp.float32))
    i = jax.nn.sigmoid(block_diag_linear(x, w_x, b_x).astype(jnp.float32))
    log_a = -LRU_C * r * jax.nn.softplus(-lam.astype(jnp.float32))
    a = jnp.exp(log_a)
    u = jnp.sqrt(-jnp.expm1(2.0 * log_a)) * i * x.astype(jnp.float32)

    def combine(left, right):
        a1, b1 = left
        a2, b2 = right
        return a1 * a2, a2 * b1 + b2

    _, h = lax.associative_scan(combine, (a, u), axis=1)
    return h.astype(x.dtype)


def dsa_attention(q, k, v, qi, ki, wi):
    bsz, s = k.shape[0], k.shape[1]
    topk = min(TOPK_MAX, s // 4)
    nb = s // Q_BLOCK
    key_pos = jnp.arange(s)

    def to_blocks(t):
        return jnp.moveaxis(t.reshape((bsz, nb, Q_BLOCK) + t.shape[2:]), 1, 0)

    def one_block(args):
        q_b, qi_b, wi_b, start = args
        q_pos = start + jnp.arange(Q_BLOCK)
        causal = key_pos[None, :] <= q_pos[:, None]
        logits = jnp.einsum("bqhd,bsd->bqhs", qi_b.astype(jnp.float32),
                            ki.astype(jnp.float32)) * (IDX_DIM ** -0.5)
        score = jnp.einsum("bqhs,bqh->bqs", jax.nn.relu(logits), wi_b.astype(jnp.float32))
        score = jnp.where(causal[None], score, -jnp.inf)
        _, idx = lax.top_k(score, topk)
        valid = idx <= q_pos[None, :, None]
        k_sel = jax.vmap(lambda kk, ii: kk[ii])(k, idx)
        v_sel = jax.vmap(lambda vv, ii: vv[ii])(v, idx)
        sc = jnp.einsum("bqhd,bqkd->bqhk", q_b, k_sel).astype(jnp.float32) * (HEAD_DIM ** -0.5)
        sc = jnp.where(valid[:, :, None, :], sc, -jnp.inf)
        p = jax.nn.softmax(sc, axis=-1)
        return jnp.einsum("bqhk,bqkd->bqhd", p.astype(v.dtype), v_sel)

    starts = jnp.arange(nb) * Q_BLOCK
    out = lax.map(one_block, (to_blocks(q), to_blocks(qi), to_blocks(wi), starts))
    return jnp.moveaxis(out, 0, 1).reshape(bsz, s, N_HEADS * HEAD_DIM)


def hybrid_mixer(h, w_in, conv_w, conv_b, rg_wa, rg_ba, rg_wx, rg_bx, rg_lam,
                 w_att_proj, w_rnn_proj, w_out):
    bsz, s, _ = h.shape
    z = h @ w_in
    q, k, v, qi, ki, wi, xr, gr, ga_logit, gr_logit = split_columns(z)
    att = dsa_attention(q.reshape(bsz, s, N_HEADS, HEAD_DIM), k, v,
                        qi.reshape(bsz, s, IDX_HEADS, IDX_DIM), ki,
                        wi * (IDX_HEADS ** -0.5))
    xr = causal_dwconv(xr, conv_w, conv_b)
    rnn = rg_lru(xr, rg_wa, rg_ba, rg_wx, rg_bx, rg_lam) * jax.nn.gelu(gr)
    merged = (jax.nn.sigmoid(ga_logit) * (att @ w_att_proj)
              + jax.nn.sigmoid(gr_logit) * (rnn @ w_rnn_proj))
    return merged @ w_out


def setup_inputs(seed: int = 0) -> dict:
    key = jax.random.key(seed)
    ks = jax.random.split(key, 24)
    f32 = jnp.float32

    def w(k, shape, fan_in):
        return jax.random.normal(k, shape, f32) * (fan_in ** -0.5)

    def gain(k, shape):
        return 1.0 + 0.01 * jax.random.normal(k, shape, f32)

    u = jax.random.uniform(ks[13], (DEPTH, D_RNN), f32, minval=0.9, maxval=0.999)
    a0 = u ** (1.0 / LRU_C)
    lam = jnp.log(a0) - jnp.log1p(-a0)

    return {
        "x": jax.random.normal(ks[0], (BATCH, SEQ, D_MODEL), f32),
        "ffn1_norm": gain(ks[1], (DEPTH, D_MODEL)),
        "ffn1_wg": w(ks[2], (DEPTH, D_MODEL, D_FF), D_MODEL),
        "ffn1_wu": w(ks[3], (DEPTH, D_MODEL, D_FF), D_MODEL),
        "ffn1_wd": w(ks[4], (DEPTH, D_FF, D_MODEL), D_FF),
        "mix_norm": gain(ks[5], (DEPTH, D_MODEL)),
        "w_in": w(ks[6], (DEPTH, D_MODEL, N_IN), D_MODEL),
        "conv_w": w(ks[7], (DEPTH, CONV_W, D_RNN), CONV_W),
        "conv_b": 0.01 * jax.random.normal(ks[8], (DEPTH, D_RNN), f32),
        "rg_wa": w(ks[9], (DEPTH, RNN_BLOCKS, RNN_BW, RNN_BW), RNN_BW),
        "rg_ba": 0.01 * jax.random.normal(ks[10], (DEPTH, D_RNN), f32),
        "rg_wx": w(ks[11], (DEPTH, RNN_BLOCKS, RNN_BW, RNN_BW), RNN_BW),
        "rg_bx": 0.01 * jax.random.normal(ks[12], (DEPTH, D_RNN), f32),
        "rg_lam": lam,
        "w_att_proj": w(ks[14], (DEPTH, D_ATT, D_MODEL), D_ATT),
        "w_rnn_proj": w(ks[15], (DEPTH, D_RNN, D_MODEL), D_RNN),
        "w_out": w(ks[16], (DEPTH, D_MODEL, D_MODEL), D_MODEL),
        "ffn2_norm": gain(ks[17], (DEPTH, D_MODEL)),
        "ffn2_wg": w(ks[18], (DEPTH, D_MODEL, D_FF), D_MODEL),
        "ffn2_wu": w(ks[19], (DEPTH, D_MODEL, D_FF), D_MODEL),
        "ffn2_wd": w(ks[20], (DEPTH, D_FF, D_MODEL), D_FF),
        "final_norm": gain(ks[21], (D_MODEL,)),
    }


def reference(x, ffn1_norm, ffn1_wg, ffn1_wu, ffn1_wd, mix_norm, w_in, conv_w, conv_b,
              rg_wa, rg_ba, rg_wx, rg_bx, rg_lam, w_att_proj, w_rnn_proj, w_out,
              ffn2_norm, ffn2_wg, ffn2_wu, ffn2_wd, final_norm):
    for l in range(DEPTH):
        x = x + 0.5 * swiglu(rmsnorm(x, ffn1_norm[l]), ffn1_wg[l], ffn1_wu[l], ffn1_wd[l])
        x = x + hybrid_mixer(rmsnorm(x, mix_norm[l]), w_in[l], conv_w[l], conv_b[l],
                             rg_wa[l], rg_ba[l], rg_wx[l], rg_bx[l], rg_lam[l],
                             w_att_proj[l], w_rnn_proj[l], w_out[l])
        x = x + 0.5 * swiglu(rmsnorm(x, ffn2_norm[l]), ffn2_wg[l], ffn2_wu[l], ffn2_wd[l])
    return rmsnorm(x, final_norm)
```

```python
import numpy as np
from contextlib import ExitStack
import concourse.bass as bass
import concourse.mybir as mybir
from concourse.bass_utils import run_bass_kernel_spmd

F32 = mybir.dt.float32
BF16 = mybir.dt.bfloat16
AF = mybir.ActivationFunctionType
ALU = mybir.AluOpType
AX = mybir.AxisListType

S = 2048
D = 1024
DFF = 2816
NT = 4
TT = 512
EPS = 1e-6
DEPTH = 2
N_IN = 5032
IT = 12
TOPK = 256
NEG = -30000.0
PL = 88
P_F1, P_MX, P_F2, P_CW, P_CB, P_BA, P_BX, P_LAM = 0, 8, 16, 24, 56, 64, 72, 80
P_FIN = DEPTH * PL
NP = P_FIN + 8
C_Q, C_K, C_V, C_QI, C_KI, C_WI, C_XR, C_GR, C_GA, C_GL = 0, 512, 576, 640, 896, 928, 936, 1960, 2984, 4008

SEM_ROLL = 30000
N_DMA_SEMS = 24


class Buf:
    __slots__ = ("name", "last_w", "readers")

    def __init__(self, name=""):
        self.name = name
        self.last_w = None
        self.readers = {}


class Ctx:
    def __init__(self, nc, stack):
        self.nc = nc
        self.stack = stack
        self.eng = {"pe": nc.tensor, "act": nc.scalar, "dve": nc.vector,
                    "pool": nc.gpsimd, "sp": nc.sync}
        self.sems = {}
        self.cur = {}
        self.cnt = {}
        self.gen = {}
        self.known = {e: {} for e in self.eng}
        for e in self.eng:
            self.gen[e] = 0
            self._new_sem(e)
        self.dma_sems = []
        for i in range(N_DMA_SEMS):
            key = f"dma{i}"
            self.sems[key] = stack.enter_context(nc.semaphore(key))
            self.dma_sems.append([key, 0])
        self.dma_rr = 0
        self.dma_rr2 = 0
        self.n_wait = 0
        self.n_ins = 0

    def _new_sem(self, e):
        key = f"{e}_{self.gen[e]}"
        self.gen[e] += 1
        self.sems[key] = self.stack.enter_context(self.nc.semaphore(key))
        self.cur[e] = key
        self.cnt[e] = 0

    def _wait(self, e, ev):
        key, val, src = ev
        if self.known[e].get(key, 0) >= val:
            return
        if not key.startswith("dma") and key == self.cur[src]:
            assert self.cnt[src] >= val, f"wait on future signal {key} {val} > {self.cnt[src]}"
        self.eng[e].wait_ge(self.sems[key], val)
        self.known[e][key] = val
        self.n_wait += 1

    def _deps(self, e, reads, writes):
        evs = []
        for b in reads:
            if b.last_w is not None:
                evs.append(b.last_w)
        for b in writes:
            if b.last_w is not None and b.last_w[2] != e:
                evs.append(b.last_w)
            for r, ev in b.readers.items():
                if r != e:
                    evs.append(ev)
        for ev in evs:
            self._wait(e, ev)

    def op(self, e, fn, reads=(), writes=(), sig=True):
        self._deps(e, reads, writes)
        ins = fn()
        self.n_ins += 1
        if sig:
            if self.cnt[e] >= SEM_ROLL:
                self._new_sem(e)
            self.cnt[e] += 1
            ins.then_inc(self.sems[self.cur[e]], 1)
            ev = (self.cur[e], self.cnt[e], e)
        else:
            ev = (self.cur[e], self.cnt[e] + 1, e)
        for b in reads:
            b.readers[e] = ev
        for b in writes:
            b.last_w = ev
            b.readers = {}
        return ins

    def dma(self, q, out, in_, reads=(), writes=(), **kw):
        self._deps(q, reads, writes)
        half = len(self.dma_sems) // 2
        if q == "pool":
            slot = self.dma_sems[self.dma_rr % half]
            self.dma_rr += 1
        else:
            slot = self.dma_sems[half + self.dma_rr2 % half]
            self.dma_rr2 += 1
        key, val = slot
        if val > 0:
            self._wait(q, (key, val, "dma"))
        ins = self.eng[q].dma_start(out=out, in_=in_, **kw)
        slot[1] = val + 16
        ins.then_inc(self.sems[key], 16)
        ev = (key, val + 16, "dma:" + q)
        for b in reads:
            b.readers["dma:" + q + key] = ev
        for b in writes:
            b.last_w = ev
            b.readers = {}
        self.n_ins += 1
        return ev

    def barrier_all(self):
        evs = [(self.cur[e], self.cnt[e], e) for e in self.eng if self.cnt[e] > 0]
        evs += [(key, val, "dma") for key, val in self.dma_sems if val > 0]
        for e in self.eng:
            for ev in evs:
                if ev[2] != e:
                    self._wait(e, ev)


class G:
    pass


def _grid(a, b):
    return [[Buf() for _ in range(b)] for _ in range(a)]


def emit_norm(g, st, x_sb, xb, gcol, out_fn, tag):
    cx, nc = g.cx, g.nc
    sq = st.enter_context(nc.sbuf_tensor(f"sq{tag}", [128, 2, 8, TT], BF16))
    sqb = [Buf(), Buf()]
    rs = st.enter_context(nc.sbuf_tensor(f"rs{tag}", [128, 2, TT], F32))
    rsb = [Buf(), Buf()]
    ps = st.enter_context(nc.psum_tensor(f"psn{tag}", [128, 2, TT], F32))
    psb = [Buf(), Buf()]
    for tt in range(NT):
        s = tt % 2
        tsl = slice(tt * TT, (tt + 1) * TT)
        for c in range(8):
            cx.op("act", lambda c=c: nc.scalar.activation(out=sq[:, s, c, :], in_=x_sb[:, c, tsl], func=AF.Square),
                  reads=[xb[c][tt]], writes=[sqb[s]], sig=(c == 7))
        for c in range(8):
            cx.op("pe", lambda c=c: nc.tensor.matmul(ps[:, s, :], lhsT=g.ones_bf[:, :], rhs=sq[:, s, c, :],
                                                      start=(c == 0), stop=(c == 7)),
                  reads=[sqb[s], g.constb], writes=[psb[s]], sig=(c == 7))
        cx.op("act", lambda: nc.scalar.activation(out=rs[:, s, :], in_=ps[:, s, :], func=AF.Ln,
                                                   scale=1.0 / D, bias=g.eps_c[:, 0:1]),
              reads=[psb[s], g.constb], writes=[rsb[s]])
        cx.op("act", lambda: nc.scalar.activation(out=rs[:, s, :], in_=rs[:, s, :], func=AF.Exp, scale=-0.5),
              reads=[rsb[s]], writes=[rsb[s]])
        for c in range(8):
            out_fn(tt, c, rs[:, s, :], rsb[s])


def norm_to_bf16(g, x_sb, xb, xn_sb, xnb, gcol):
    cx, nc = g.cx, g.nc

    def fn(tt, c, rs_ap, rs_buf):
        tsl = slice(tt * TT, (tt + 1) * TT)
        cx.op("dve", lambda: nc.vector.scalar_tensor_tensor(
            out=xn_sb[:, c, tsl], in0=x_sb[:, c, tsl], scalar=g.prm_sb[:, gcol + c:gcol + c + 1], in1=rs_ap,
            op0=ALU.mult, op1=ALU.mult),
            reads=[xb[c][tt], rs_buf, g.prmb], writes=[xnb[c][tt]])
    return fn


GROUPS = [(0, 4), (4, 8), (8, 12), (12, 16), (16, 19), (19, 22)]


def emit_ffn(g, st, x_sb, xb, xn_sb, xnb, wg, wu, wd, tag):
    cx, nc = g.cx, g.nc
    GM = 4
    wg_s = st.enter_context(nc.sbuf_tensor(f"wg_s{tag}", [128, 2, 8, GM * 128], BF16))
    wu_s = st.enter_context(nc.sbuf_tensor(f"wu_s{tag}", [128, 2, 8, GM * 128], BF16))
    wd_s = st.enter_context(nc.sbuf_tensor(f"wd_s{tag}", [128, 2, GM, D], BF16))
    wgb = [Buf(), Buf()]; wub = [Buf(), Buf()]; wdb = [Buf(), Buf()]
    a_s = st.enter_context(nc.sbuf_tensor(f"a_s{tag}", [128, 2, GM, TT], BF16))
    ab = [Buf(), Buf()]
    sg_s = st.enter_context(nc.sbuf_tensor(f"sg_s{tag}", [128, 2, TT], F32))
    sgb = [Buf(), Buf()]
    psg = st.enter_context(nc.psum_tensor(f"psg{tag}", [128, 2, TT], F32))
    psu = st.enter_context(nc.psum_tensor(f"psu{tag}", [128, 2, TT], F32))
    psy = st.enter_context(nc.psum_tensor(f"psy{tag}", [128, 2, TT], F32))
    psgb = [Buf(), Buf()]; psub = [Buf(), Buf()]; psyb = [Buf(), Buf()]
    wgv = wg.rearrange("(kc p) n -> p kc n", p=128)
    wuv = wu.rearrange("(kc p) n -> p kc n", p=128)
    wdv = wd.rearrange("(j p) d -> p j d", p=128)

    def load(gi):
        c0, c1 = GROUPS[gi]
        G_ = c1 - c0
        s = gi % 2
        cx.dma("pool", wg_s[:, s, :, 0:G_ * 128], wgv[:, :, c0 * 128:c1 * 128], writes=[wgb[s]])
        cx.dma("pool", wu_s[:, s, :, 0:G_ * 128], wuv[:, :, c0 * 128:c1 * 128], writes=[wub[s]])
        cx.dma("pool", wd_s[:, s, 0:G_, :], wdv[:, c0:c1, :], writes=[wdb[s]])

    load(0)
    jj = 0
    yy = 0
    for gi, (c0, c1) in enumerate(GROUPS):
        G_ = c1 - c0
        s = gi % 2
        if gi + 1 < len(GROUPS):
            load(gi + 1)
        for tt in range(NT):
            tsl = slice(tt * TT, (tt + 1) * TT)
            sa = (gi * NT + tt) % 2
            for j in range(G_):
                p = jj % 2
                jj += 1
                for kc in range(8):
                    cx.op("pe", lambda kc=kc: nc.tensor.matmul(
                        psg[:, p, :], lhsT=wg_s[:, s, kc, j * 128:(j + 1) * 128], rhs=xn_sb[:, kc, tsl],
                        start=(kc == 0), stop=(kc == 7)),
                        reads=[wgb[s], xnb[kc][tt]], writes=[psgb[p]], sig=(kc == 7))
                for kc in range(8):
                    cx.op("pe", lambda kc=kc: nc.tensor.matmul(
                        psu[:, p, :], lhsT=wu_s[:, s, kc, j * 128:(j + 1) * 128], rhs=xn_sb[:, kc, tsl],
                        start=(kc == 0), stop=(kc == 7)),
                        reads=[wub[s], xnb[kc][tt]], writes=[psub[p]], sig=(kc == 7))
                cx.op("act", lambda: nc.scalar.activation(out=sg_s[:, p, :], in_=psg[:, p, :], func=AF.Silu),
                      reads=[psgb[p]], writes=[sgb[p]])
                cx.op("dve", lambda: nc.vector.tensor_tensor(out=a_s[:, sa, j, :], in0=psu[:, p, :],
                                                             in1=sg_s[:, p, :], op=ALU.mult),
                      reads=[psub[p], sgb[p]], writes=[ab[sa]])
            for dc in range(8):
                p = yy % 2
                yy += 1
                for j in range(G_):
                    cx.op("pe", lambda j=j: nc.tensor.matmul(
                        psy[:, p, :], lhsT=wd_s[:, s, j, dc * 128:(dc + 1) * 128], rhs=a_s[:, sa, j, :],
                        start=(j == 0), stop=(j == G_ - 1)),
                        reads=[wdb[s], ab[sa]], writes=[psyb[p]], sig=(j == G_ - 1))
                cx.op("dve", lambda: nc.vector.scalar_tensor_tensor(
                    out=x_sb[:, dc, tsl], in0=psy[:, p, :], scalar=0.5, in1=x_sb[:, dc, tsl],
                    op0=ALU.mult, op1=ALU.add),
                    reads=[psyb[p], xb[dc][tt]], writes=[xb[dc][tt]])


def emit_mixer(g, l, W, have_xn=False):
    cx, nc = g.cx, g.nc
    pb = l * PL
    w_in_v = W["w_in"][l].rearrange("(kc p) n -> p kc n", p=128)
    xres_v = g.xres.rearrange("(c p) t -> p c t", p=128)
    with ExitStack() as ms:
        xn_sb, xnb = g.xn_sb, g.xnb
        x_sb, xb = g.x_sb, g.xb
        if not have_xn:
            with ExitStack() as st:
                for tt in range(NT):
                    cx.dma("sp", x_sb[:, :, tt * TT:(tt + 1) * TT], xres_v[:, :, tt * TT:(tt + 1) * TT],
                           reads=[g.xresb[c][tt] for c in range(8)], writes=[xb[c][tt] for c in range(8)])
                emit_norm(g, st, x_sb, xb, pb + P_MX, norm_to_bf16(g, x_sb, xb, xn_sb, xnb, pb + P_MX), f"m{l}")
        cx.barrier_all()
        if getattr(g, "stop", 9) == 0:
            return
        attT = ms.enter_context(nc.sbuf_tensor(f"attT{l}", [128, 4, S], BF16))
        attTb = [Buf() for _ in range(16)]
        with ExitStack() as st:
            qT = st.enter_context(nc.sbuf_tensor(f"qT{l}", [128, 4, S], BF16)); qTb = Buf()
            kT2 = st.enter_context(nc.sbuf_tensor(f"kT2{l}", [128, 2, S], BF16)); kT2b = Buf()
            qiT3 = st.enter_context(nc.sbuf_tensor(f"qiT3{l}", [128, 3, S], BF16)); qiT3b = Buf()
            kiT3 = st.enter_context(nc.sbuf_tensor(f"kiT3{l}", [128, 3, S], BF16)); kiT3b = Buf()
            vaug2 = st.enter_context(nc.sbuf_tensor(f"vaug{l}", [128, 16, 2, 128], BF16)); vb = Buf()
            wi_sb = st.enter_context(nc.sbuf_tensor(f"wi{l}", [128, 16, 8], F32)); wib = Buf()
            wabs = st.enter_context(nc.sbuf_tensor(f"wabs{l}", [128, 16, 8], F32)); wabsb = Buf()
            wsgn = st.enter_context(nc.sbuf_tensor(f"wsgn{l}", [128, 16, 8], F32)); wsgnb = Buf()
            with ExitStack() as s1:
                wt = s1.enter_context(nc.sbuf_tensor(f"m1w{l}", [128, 3, 8, 128], BF16))
                wtb = [Buf(), Buf(), Buf()]
                ps1 = s1.enter_context(nc.psum_tensor(f"m1ps{l}", [128, 2, TT], F32))
                ps1b = [Buf(), Buf()]
                psv = s1.enter_context(nc.psum_tensor(f"m1pv{l}", [128, 2, TT], F32))
                psvb = [Buf(), Buf()]
                chunks = []
                for c in range(4):
                    chunks.append(([(C_Q + c * 128, 128, 0)], 128, qT[:, c, :], qTb))
                chunks.append(([(C_K, 64, 0), (C_K, 64, 64)], 128, "K", kT2b))
                cx.op("dve", lambda: nc.vector.memset(kT2[:, :, :], 0.0), writes=[kT2b])
                chunks.append(([(C_QI, 96, 0)], 96, qiT3[0:96, 0, :], qiT3b))
                chunks.append(([(C_QI + 96, 96, 0)], 96, qiT3[0:96, 1, :], qiT3b))
                chunks.append(([(C_QI + 192, 64, 0)], 64, qiT3[0:64, 2, :], qiT3b))
                chunks.append(([(C_KI, 32, 0), (C_KI, 32, 32), (C_KI, 32, 64)], 96, "KI", kiT3b))
                cx.op("dve", lambda: nc.vector.memset(kiT3[:, :, :], 0.0), writes=[kiT3b])
                cx.op("dve", lambda: nc.vector.memset(qiT3[:, :, :], 0.0), writes=[qiT3b])
                chunks.append(([(C_V, 64, 0), (C_KI, 40, 64)], 104, None, None))

                def loadw(i):
                    cols, M, _, _ = chunks[i]
                    s = i % 3
                    for (c0, n, off) in cols:
                        cx.dma("pool", wt[:, s, :, off:off + n], w_in_v[:, :, c0:c0 + n], writes=[wtb[s]])
                loadw(0)
                loadw(1)
                pp = 0
                for i, (cols, M, dst, dstb) in enumerate(chunks):
                    s = i % 3
                    if i + 2 < len(chunks):
                        loadw(i + 2)
                    if dst is not None:
                        for tt in range(NT):
                            tsl = slice(tt * TT, (tt + 1) * TT)
                            p = pp % 2
                            pp += 1
                            for kc in range(8):
                                cx.op("pe", lambda kc=kc: nc.tensor.matmul(
                                    ps1[0:M, p, :], lhsT=wt[:, s, kc, 0:M], rhs=xn_sb[:, kc, tsl],
                                    start=(kc == 0), stop=(kc == 7)),
                                    reads=[wtb[s], xnb[kc][tt]], writes=[ps1b[p]], sig=(kc == 7))
                            if dst == "KI":
                                for v in range(3):
                                    cx.op("dve", lambda v=v: nc.vector.tensor_copy(out=kiT3[32 * v:32 * v + 32, v, tsl],
                                                                                   in_=ps1[32 * v:32 * v + 32, p, :]),
                                          reads=[ps1b[p]], writes=[dstb])
                            elif isinstance(dst, str):
                                cx.op("dve", lambda: nc.vector.tensor_copy(out=kT2[0:64, 0, tsl], in_=ps1[0:64, p, :]),
                                      reads=[ps1b[p]], writes=[dstb])
                                cx.op("dve", lambda: nc.vector.tensor_copy(out=kT2[64:128, 1, tsl], in_=ps1[64:128, p, :]),
                                      reads=[ps1b[p]], writes=[dstb])
                            elif pp % 2 == 0:
                                cx.op("act", lambda: nc.scalar.copy(out=dst[:, tsl], in_=ps1[0:M, p, :]),
                                      reads=[ps1b[p]], writes=[dstb])
                            else:
                                cx.op("dve", lambda: nc.vector.tensor_copy(out=dst[:, tsl], in_=ps1[0:M, p, :]),
                                      reads=[ps1b[p]], writes=[dstb])
                    else:
                        cx.op("dve", lambda: nc.vector.memset(vaug2[:, :, :, :], 1.0), writes=[vb])
                        for tb in range(16):
                            p = tb % 2
                            for kc in range(8):
                                cx.op("pe", lambda kc=kc: nc.tensor.matmul(
                                    psv[:, p, 0:104], lhsT=xn_sb[:, kc, tb * 128:(tb + 1) * 128], rhs=wt[:, s, kc, 0:104],
                                    start=(kc == 0), stop=(kc == 7)),
                                    reads=[wtb[s], xnb[kc][tb // 4]], writes=[psvb[p]], sig=(kc == 7))
                            cx.op("dve", lambda: nc.vector.tensor_copy(out=vaug2[:, tb, 0, 0:64], in_=psv[:, p, 0:64]),
                                  reads=[psvb[p]], writes=[vb])
                            cx.op("dve", lambda: nc.vector.tensor_copy(out=vaug2[:, tb, 1, 64:128], in_=psv[:, p, 0:64]),
                                  reads=[psvb[p]], writes=[vb])
                            cx.op("dve", lambda: nc.vector.tensor_copy(out=wi_sb[:, tb, :], in_=psv[:, p, 96:104]),
                                  reads=[psvb[p]], writes=[wib])
                        cx.op("dve", lambda: nc.vector.scalar_tensor_tensor(
                            out=wabs[:, :, :], in0=wi_sb[:, :, :], scalar=-1.0, in1=wi_sb[:, :, :],
                            op0=ALU.mult, op1=ALU.max), reads=[wib], writes=[wabsb])
                        cx.op("act", lambda: nc.scalar.activation(out=wsgn[:, :, :], in_=wi_sb[:, :, :], func=AF.Sign),
                              reads=[wib], writes=[wsgnb])
                cx.barrier_all()
            if getattr(g, "stop", 9) == 1:
                return
            with ExitStack() as s2:
                score = g.x_sb; scb = [Buf() for _ in range(4)]
                Rt = s2.enter_context(nc.sbuf_tensor(f"Rt{l}", [128, 3, TT], F32)); Rb = [Buf(), Buf(), Buf()]
                bias = s2.enter_context(nc.sbuf_tensor(f"bias{l}", [128, 4, S], BF16)); biasb = [Buf() for _ in range(4)]
                _bv = g.x_sb.bitcast(BF16)[:, 4:8, :].rearrange("p a (k q) -> p a k q", q=TT)

                def bT(par, kb, lo, hi):
                    return _bv[:, par * 2 + kb // 8, kb % 8, lo:hi]
                biasTb = [[Buf() for _ in range(16)] for _ in range(2)]
                PT = s2.enter_context(nc.sbuf_tensor(f"PT{l}", [128, 3, TT], BF16)); PTb = [Buf(), Buf(), Buf()]
                sm = s2.enter_context(nc.sbuf_tensor(f"sm{l}", [128, 2, 64, 4], F32)); smb = [Buf(), Buf()]
                rdn = s2.enter_context(nc.sbuf_tensor(f"rdn{l}", [128, 2, TT], F32)); rdnb = [Buf(), Buf()]
                smm = s2.enter_context(nc.sbuf_tensor(f"smm{l}", [128, 2, 4], F32)); smmb = [Buf(), Buf()]
                sma = s2.enter_context(nc.sbuf_tensor(f"sma{l}", [128, 2, 4], F32)); smab = [Buf(), Buf()]
                ACTJ = (0, 3)
                junk, junkb = bias[:, 0, :], biasb[0]
                junkA, junkAb = bias[:, 1, :], biasb[1]
                psL = s2.enter_context(nc.psum_tensor(f"psL{l}", [128, 2, TT], F32)); psLb = [Buf(), Buf()]
                accP = s2.enter_context(nc.psum_tensor(f"accP{l}", [128, TT], F32)); accPb = Buf()
                psS = s2.enter_context(nc.psum_tensor(f"psS{l}", [128, 2, TT], F32)); psSb = [Buf(), Buf()]
                psO = s2.enter_context(nc.psum_tensor(f"psO{l}", [128, 2, TT], F32)); psOb = [Buf(), Buf()]
                psB = s2.enter_context(nc.psum_tensor(f"psB{l}", [128, TT], F32)); psBb = Buf()
                st_ = {"L": 0, "R": 0, "S": 0, "PT": 0, "O": 0}
                V = nc.vector
                cx.op("dve", lambda: V.memset(sm[:, :, :, :], 0.0), writes=smb)

                def indexer(qb, j):
                    n = (qb + 1) * 128
                    qsl = slice(qb * 128, (qb + 1) * 128)
                    for kt in range((n + TT - 1) // TT):
                        w = min(TT, n - kt * TT)
                        for h in range(8):
                            jj, po = h // 3, (h % 3) * 32
                            pL = st_["L"] % 2; st_["L"] += 1
                            r = st_["R"] % 3; st_["R"] += 1
                            cx.op("pe", lambda: nc.tensor.matmul(
                                psL[:, pL, 0:w], lhsT=qiT3[:, jj, qsl], rhs=kiT3[:, h % 3, kt * TT:kt * TT + w],
                                start=True, stop=True), reads=[qiT3b, kiT3b], writes=[psLb[pL]])
                            cx.op("act", lambda: nc.scalar.activation(
                                out=Rt[:, r, 0:w], in_=psL[:, pL, 0:w], func=AF.Relu, scale=wabs[:, qb, h:h + 1]),
                                reads=[psLb[pL], wabsb], writes=[Rb[r]])
                            if h == 0:
                                cx.op("dve", lambda: V.tensor_scalar(
                                    out=accP[:, 0:w], in0=Rt[:, r, 0:w], scalar1=wsgn[:, qb, 0:1], scalar2=None,
                                    op0=ALU.mult), reads=[Rb[r], wsgnb], writes=[accPb])
                            elif h < 7:
                                cx.op("dve", lambda: V.scalar_tensor_tensor(
                                    out=accP[:, 0:w], in0=Rt[:, r, 0:w], scalar=wsgn[:, qb, h:h + 1], in1=accP[:, 0:w],
                                    op0=ALU.mult, op1=ALU.add), reads=[Rb[r], wsgnb, accPb], writes=[accPb])
                            else:
                                cx.op("dve", lambda: V.scalar_tensor_tensor(
                                    out=score[:, j, kt * TT:kt * TT + w], in0=Rt[:, r, 0:w], scalar=wsgn[:, qb, h:h + 1],
                                    in1=accP[:, 0:w], op0=ALU.mult, op1=ALU.add),
                                    reads=[Rb[r], wsgnb, accPb], writes=[scb[j]])
                            yield
                    cx.op("dve", lambda: V.tensor_tensor(
                        out=score[:, j, qsl], in0=score[:, j, qsl], in1=g.negm[:, :], op=ALU.add),
                        reads=[scb[j], g.constb], writes=[scb[j]])

                def bisect(sb):
                    s = sb % 2
                    B = smb[s]
                    Bm, Ba = smmb[s], smab[s]
                    js = [j for j in range(4) if 4 * sb + j >= 2]
                    actj = [j for j in js if j in ACTJ]

                    def sv(fn, extra_r=(), extra_w=()):
                        cx.op("dve", fn, reads=[B] + list(extra_r), writes=[B] + list(extra_w))
                    for j in range(4):
                        n = (4 * sb + j + 1) * 128
                        val = (2.0 * TOPK - 1.0 - n) if j in actj else (TOPK - 0.5)
                        sv(lambda: V.memset(sm[:, s, 6, j:j + 1], val))
                    for j in js:
                        qb = 4 * sb + j
                        n = (qb + 1) * 128
                        sv(lambda: V.tensor_reduce(out=sm[:, s, 0, j:j + 1], in_=score[:, j, 0:n], axis=AX.X, op=ALU.max), extra_r=[scb[j]])
                        sv(lambda: V.tensor_reduce(out=sm[:, s, 1, j:j + 1], in_=score[:, j, 0:qb * 128], axis=AX.X, op=ALU.min), extra_r=[scb[j]])
                    sv(lambda: V.tensor_tensor(out=sm[:, s, 2, :], in0=sm[:, s, 0, :], in1=sm[:, s, 1, :], op=ALU.subtract))
                    sv(lambda: V.tensor_tensor(out=sm[:, s, 8:8 + IT, :], in0=g.pw3[:, 0:IT, :],
                                               in1=sm[:, s, 2:3, :].to_broadcast([128, IT, 4]), op=ALU.mult), extra_r=[g.constb])
                    for i in range(IT):
                        sv(lambda: V.tensor_tensor(out=sm[:, s, 5, :], in0=sm[:, s, 1, :], in1=sm[:, s, 8 + i, :], op=ALU.add))
                        if actj:
                            sv(lambda: V.scalar_tensor_tensor(out=smm[:, s, :], in0=sm[:, s, 1, :], scalar=-1.0, in1=sm[:, s, 8 + i, :],
                                                              op0=ALU.mult, op1=ALU.subtract), extra_w=[Bm])
                        for j in actj:
                            n = (4 * sb + j + 1) * 128
                            cx.op("act", lambda: nc.scalar.activation(
                                out=junkA[:, 0:n], in_=score[:, j, 0:n], func=AF.Sign, bias=smm[:, s, j:j + 1], scale=1.0,
                                accum_out=sma[:, s, j:j + 1]), reads=[Bm, scb[j]], writes=[Ba, junkAb])
                        for j in js:
                            if j in actj:
                                continue
                            n = (4 * sb + j + 1) * 128
                            sv(lambda: V.tensor_scalar(out=junk[:, 0:n], in0=score[:, j, 0:n], scalar1=sm[:, s, 5, j:j + 1],
                                                       scalar2=None, op0=ALU.is_ge, op1=ALU.add, accum_out=sm[:, s, 3, j:j + 1]),
                               extra_r=[scb[j]], extra_w=[junkb])
                        for j in actj:
                            sv(lambda: V.tensor_copy(out=sm[:, s, 3, j:j + 1], in_=sma[:, s, j:j + 1]), extra_r=[Ba])
                        sv(lambda: V.tensor_tensor(out=sm[:, s, 4, :], in0=sm[:, s, 3, :], in1=sm[:, s, 6, :], op=ALU.is_ge))
                        sv(lambda: V.tensor_tensor(out=sm[:, s, 4, :], in0=sm[:, s, 4, :], in1=sm[:, s, 8 + i, :], op=ALU.mult))
                        sv(lambda: V.tensor_tensor(out=sm[:, s, 1, :], in0=sm[:, s, 1, :], in1=sm[:, s, 4, :], op=ALU.add))
                        yield
                    for j in js:
                        n = (4 * sb + j + 1) * 128
                        cx.op("dve", lambda: V.tensor_scalar(out=bias[:, j, 0:n], in0=score[:, j, 0:n], scalar1=sm[:, s, 1, j:j + 1],
                                                             scalar2=NEG, op0=ALU.is_lt, op1=ALU.mult),
                              reads=[B, scb[j]], writes=[biasb[j]])
                    for j in range(4):
                        qb = 4 * sb + j
                        if qb < 2:
                            if qb > 0:
                                cx.op("dve", lambda: V.memset(bias[:, j, 0:qb * 128], 0.0), writes=[biasb[j]])
                            cx.op("dve", lambda: V.tensor_copy(out=bias[:, j, qb * 128:(qb + 1) * 128], in_=g.negb[:, :]),
                                  reads=[g.constb], writes=[biasb[j]])

                def make_biasT(sb):
                    par = sb % 2
                    for kb in range(4 * sb + 4):
                        j0 = max(0, kb - 4 * sb)
                        for j in range(j0, 4):
                            cx.op("pe", lambda: nc.tensor.matmul(
                                psB[:, j * 128:(j + 1) * 128], lhsT=bias[:, j, kb * 128:(kb + 1) * 128], rhs=g.ident[:, :],
                                start=(j == j0), stop=(j == 3), skip_group_check=True),
                                reads=[biasb[j], g.constb], writes=[psBb], sig=(j == 3))
                        cx.op("act", lambda: nc.scalar.copy(out=bT(par, kb, j0 * 128, TT), in_=psB[:, j0 * 128:TT]),
                              reads=[psBb], writes=[biasTb[par][kb]])

                def attention(sb):
                    par = sb % 2
                    nk = 4 * sb + 4
                    items = [(h, kb) for h in range(8) for kb in range(nk)]
                    slots = {}

                    def emit_sb(i):
                        h, kb = items[i]
                        c, hp = h // 2, (h % 2) * 64
                        qo = max(0, kb - 4 * sb) * 128
                        qs = slice(sb * TT + qo, (sb + 1) * TT)
                        ksl = slice(kb * 128, (kb + 1) * 128)
                        pS = st_["S"] % 2; st_["S"] += 1
                        slots[i] = pS
                        cx.op("pe", lambda: nc.tensor.matmul(
                            psS[:, pS, qo:TT], lhsT=kT2[:, h % 2, ksl], rhs=qT[:, c, qs],
                            start=True, stop=False), reads=[kT2b, qTb], writes=[psSb[pS]], sig=False)
                        cx.op("pe", lambda: nc.tensor.matmul(
                            psS[:, pS, qo:TT], lhsT=g.ident[:, :], rhs=bT(par, kb, qo, TT),
                            start=False, stop=True), reads=[biasTb[par][kb], g.constb], writes=[psSb[pS]])
                    emit_sb(0)
                    po = None
                    for i, (h, kb) in enumerate(items):
                        c = h // 2
                        if kb == 0:
                            po = st_["O"] % 2; st_["O"] += 1
                        qo = max(0, kb - 4 * sb) * 128
                        pS = slots.pop(i)
                        pt = st_["PT"] % 3; st_["PT"] += 1
                        cx.op("act", lambda: nc.scalar.activation(
                            out=PT[:, pt, qo:TT], in_=psS[:, pS, qo:TT], func=AF.Exp, scale=0.125),
                            reads=[psSb[pS]], writes=[PTb[pt]])
                        if i + 1 < len(items):
                            emit_sb(i + 1)
                        cx.op("pe", lambda: nc.tensor.matmul(
                            psO[:, po, qo:TT], lhsT=vaug2[:, kb, h % 2, :], rhs=PT[:, pt, qo:TT],
                            start=(kb == 0), stop=(kb == nk - 1), skip_group_check=True),
                            reads=[PTb[pt], vb], writes=[psOb[po]], sig=(kb == nk - 1))
                        if kb == nk - 1:
                            osl = slice(sb * TT, (sb + 1) * TT)
                            if h % 2 == 0:
                                num, den, dst_p = slice(0, 64), slice(64, 128), slice(0, 64)
                            else:
                                num, den, dst_p = slice(64, 128), slice(0, 64), slice(64, 128)
                            cx.op("act", lambda: nc.scalar.activation(out=rdn[dst_p, po, :], in_=psO[den, po, :], func=AF.Ln),
                                  reads=[psOb[po]], writes=[rdnb[po]])
                            cx.op("act", lambda: nc.scalar.activation(out=rdn[dst_p, po, :], in_=rdn[dst_p, po, :], func=AF.Exp, scale=-1.0),
                                  reads=[rdnb[po]], writes=[rdnb[po]])
                            cx.op("dve", lambda: V.tensor_tensor(out=attT[dst_p, c, osl], in0=psO[num, po, :], in1=rdn[dst_p, po, :],
                                                                 op=ALU.mult),
                                  reads=[psOb[po], rdnb[po]], writes=attTb[4 * sb:4 * sb + 4])
                        yield

                def run_all(gen):
                    for _ in gen:
                        pass

                def chain(gens):
                    for gg in gens:
                        for _ in gg:
                            yield

                def interleave(main, n_main, side, n_side):
                    done = 0
                    k = 0
                    for _ in main:
                        k += 1
                        tgt = (n_side * k) // max(n_main, 1)
                        while done < tgt:
                            next(side, None)
                            done += 1

                for sb in range(4):
                    qbs = [4 * sb + j for j in range(4) if 4 * sb + j >= 2]
                    n_units = sum(((qb + 1) * 128 + TT - 1) // TT * 8 for qb in qbs)
                    ig = chain([indexer(qb, qb - 4 * sb) for qb in qbs])
                    bg = bisect(sb)
                    if sb == 0:
                        run_all(ig)
                        run_all(bg)
                    else:
                        ag = attention(sb - 1)
                        n_items = 8 * 4 * sb
                        interleave(ig, n_units, ag, n_items // 2)
                        interleave(bg, IT, ag, n_items - n_items // 2)
                        run_all(ag)
                    make_biasT(sb)
                run_all(attention(3))
                cx.barrier_all()

        if getattr(g, "stop", 9) == 2:
            return
        rnnT = ms.enter_context(nc.sbuf_tensor(f"rnnT{l}", [128, 8, S], BF16))
        rnnb = [Buf() for _ in range(8)]
        with ExitStack() as st:
            T0, T1, T2, T3 = [g.x_sb[:, 2 * i_:2 * i_ + 2, :] for i_ in range(4)]
            T0b, T1b, T2b, T3b = [[Buf(), Buf()] for _ in range(4)]
            xrb = st.enter_context(nc.sbuf_tensor(f"xrb{l}", [128, 2, S + 8], BF16)); xrbb = [Buf(), Buf()]
            xrb2 = st.enter_context(nc.sbuf_tensor(f"xrc{l}", [128, 2, S + 8], BF16)); xrb2b = [Buf(), Buf()]
            xcb = st.enter_context(nc.sbuf_tensor(f"xcb{l}", [128, 2, S], BF16)); xcbb = [Buf(), Buf()]
            wxr = st.enter_context(nc.sbuf_tensor(f"wxr{l}", [128, 2, 8, 128], BF16)); wxrb = [Buf(), Buf()]
            wgr = st.enter_context(nc.sbuf_tensor(f"wgr{l}", [128, 2, 8, 128], BF16)); wgrb = [Buf(), Buf()]
            dg = st.enter_context(nc.sbuf_tensor(f"dg{l}", [128, 2, 4, 128], BF16)); dgb = [Buf(), Buf()]
            bdA = st.enter_context(nc.sbuf_tensor(f"bdA{l}", [128, 8, 128], BF16)); bdAb = Buf()
            bdX = st.enter_context(nc.sbuf_tensor(f"bdX{l}", [128, 8, 128], BF16)); bdXb = Buf()
            nsp = st.enter_context(nc.sbuf_tensor(f"nsp{l}", [128, 3, 8], F32)); nspb = Buf()
            ps3 = st.enter_context(nc.psum_tensor(f"ps3{l}", [128, 2, TT], F32)); ps3b = [Buf(), Buf()]
            psc = st.enter_context(nc.psum_tensor(f"psc{l}", [128, 2, TT], F32)); pscb = [Buf(), Buf()]
            psa = st.enter_context(nc.psum_tensor(f"psa{l}", [128, 2, TT], F32)); psab = [Buf(), Buf()]
            psx = st.enter_context(nc.psum_tensor(f"psx{l}", [128, 2, TT], F32)); psxb = [Buf(), Buf()]
            V = nc.vector
            cx.op("dve", lambda: V.memset(bdA[:, :, :], 0.0), writes=[bdAb])
            cx.op("dve", lambda: V.memset(bdX[:, :, :], 0.0), writes=[bdXb])
            cx.op("dve", lambda: V.memset(xrb[:, :, 0:8], 0.0), writes=xrbb)
            cx.op("dve", lambda: V.memset(xrb2[:, :, 0:8], 0.0), writes=xrb2b)
            for half in range(2):
                hs = slice(half * 64, (half + 1) * 64)
                srcA = W["rg_wa"][l].rearrange("(c two) i j -> two i c j", two=2)[half]
                srcX = W["rg_wx"][l].rearrange("(c two) i j -> two i c j", two=2)[half]
                cx.dma("pool", bdA[hs, :, hs], srcA, writes=[bdAb])
                cx.dma("pool", bdX[hs, :, hs], srcX, writes=[bdXb])
            lam = g.prm_sb[:, pb + P_LAM:pb + P_LAM + 8]
            cx.op("act", lambda: nc.scalar.activation(out=nsp[:, 0, :], in_=lam, func=AF.Exp, scale=-1.0),
                  reads=[g.prmb], writes=[nspb])
            cx.op("act", lambda: nc.scalar.activation(out=nsp[:, 0, :], in_=nsp[:, 0, :], func=AF.Ln, scale=1.0, bias=g.one_c[:, 0:1]),
                  reads=[nspb, g.constb], writes=[nspb])
            cx.op("dve", lambda: V.tensor_scalar(out=nsp[:, 1, :], in0=nsp[:, 0, :], scalar1=-8.0, scalar2=None, op0=ALU.mult),
                  reads=[nspb], writes=[nspb])
            cx.op("dve", lambda: V.tensor_scalar(out=nsp[:, 2, :], in0=nsp[:, 0, :], scalar1=-16.0, scalar2=None, op0=ALU.mult),
                  reads=[nspb], writes=[nspb])

            def load3(c):
                s = c % 2
                cx.dma("pool", wxr[:, s, :, :], w_in_v[:, :, C_XR + c * 128:C_XR + (c + 1) * 128], writes=[wxrb[s]])
                cx.dma("pool", wgr[:, s, :, :], w_in_v[:, :, C_GR + c * 128:C_GR + (c + 1) * 128], writes=[wgrb[s]])
            k3 = [0]

            def stageA(c):
                s = c % 2
                pcol = lambda base: g.prm_sb[:, pb + base + c:pb + base + c + 1]
                for j in range(4):
                    cx.op("act", lambda j=j: nc.scalar.activation(
                        out=dg[:, s, j, :], in_=g.ident[:, :], func=AF.Identity,
                        scale=g.prm_sb[:, pb + P_CW + j * 8 + c:pb + P_CW + j * 8 + c + 1]),
                        reads=[g.constb, g.prmb], writes=[dgb[s]])
                for tt in range(NT):
                    tsl = slice(tt * TT, (tt + 1) * TT)
                    p = k3[0] % 2; k3[0] += 1
                    for kc in range(8):
                        cx.op("pe", lambda kc=kc: nc.tensor.matmul(
                            ps3[:, p, :], lhsT=wxr[:, s, kc, :], rhs=xn_sb[:, kc, tsl], start=(kc == 0), stop=(kc == 7)),
                            reads=[wxrb[s], xnb[kc][tt]], writes=[ps3b[p]], sig=(kc == 7))
                    cx.op("dve", lambda: V.tensor_copy(out=xrb[:, s, 4 + tt * TT:4 + (tt + 1) * TT], in_=ps3[:, p, :]),
                          reads=[ps3b[p]], writes=[xrbb[s]])
                    cx.op("dve", lambda: V.tensor_copy(out=xrb2[:, s, 5 + tt * TT:5 + (tt + 1) * TT], in_=ps3[:, p, :]),
                          reads=[ps3b[p]], writes=[xrb2b[s]])
                for tt in range(NT):
                    tsl = slice(tt * TT, (tt + 1) * TT)
                    p = tt % 2
                    for j in range(4):
                        if j % 2 == 1:
                            rhs_ap = xrb[:, s, tt * TT + 1 + j:tt * TT + 1 + j + TT]
                        else:
                            rhs_ap = xrb2[:, s, tt * TT + 2 + j:tt * TT + 2 + j + TT]
                        cx.op("pe", lambda j=j, rhs_ap=rhs_ap: nc.tensor.matmul(
                            psc[:, p, :], lhsT=dg[:, s, j, :], rhs=rhs_ap,
                            start=(j == 0), stop=(j == 3)), reads=[dgb[s], xrbb[s], xrb2b[s]], writes=[pscb[p]], sig=(j == 3))
                    cx.op("dve", lambda: V.tensor_scalar(out=T0[:, s, tsl], in0=psc[:, p, :], scalar1=pcol(P_CB),
                                                         scalar2=None, op0=ALU.add),
                          reads=[pscb[p], g.prmb], writes=[T0b[s]])
                    cx.op("dve", lambda: V.tensor_scalar(out=xcb[:, s, tsl], in0=psc[:, p, :], scalar1=pcol(P_CB),
                                                         scalar2=None, op0=ALU.add),
                          reads=[pscb[p], g.prmb], writes=[xcbb[s]])
                for tt in range(NT):
                    tsl = slice(tt * TT, (tt + 1) * TT)
                    p = tt % 2
                    cx.op("pe", lambda: nc.tensor.matmul(psa[:, p, :], lhsT=bdA[:, c, :], rhs=xcb[:, s, tsl], start=True, stop=True),
                          reads=[bdAb, xcbb[s]], writes=[psab[p]])
                    cx.op("act", lambda: nc.scalar.activation(out=T1[:, s, tsl], in_=psa[:, p, :], func=AF.Sigmoid,
                                                               bias=pcol(P_BA), scale=1.0),
                          reads=[psab[p], g.prmb], writes=[T1b[s]])
                    cx.op("pe", lambda: nc.tensor.matmul(psx[:, p, :], lhsT=bdX[:, c, :], rhs=xcb[:, s, tsl], start=True, stop=True),
                          reads=[bdXb, xcbb[s]], writes=[psxb[p]])
                    cx.op("act", lambda: nc.scalar.activation(out=T2[:, s, tsl], in_=psx[:, p, :], func=AF.Sigmoid,
                                                               bias=pcol(P_BX), scale=1.0),
                          reads=[psxb[p], g.prmb], writes=[T2b[s]])

            def stageB(c):
                s = c % 2
                A = nc.scalar.activation
                cx.op("act", lambda: A(out=T3[:, s, :], in_=T1[:, s, :], func=AF.Exp, scale=nsp[:, 1, c:c + 1]),
                      reads=[T1b[s], nspb], writes=[T3b[s]])
                cx.op("act", lambda: A(out=T1[:, s, :], in_=T1[:, s, :], func=AF.Exp, scale=nsp[:, 2, c:c + 1]),
                      reads=[T1b[s], nspb], writes=[T1b[s]])
                cx.op("act", lambda: A(out=T1[:, s, :], in_=T1[:, s, :], func=AF.Sqrt, scale=-1.0, bias=g.one_c[:, 0:1]),
                      reads=[T1b[s], g.constb], writes=[T1b[s]])
                cx.op("dve", lambda: V.tensor_tensor(out=T2[:, s, :], in0=T2[:, s, :], in1=T0[:, s, :], op=ALU.mult),
                      reads=[T2b[s], T0b[s]], writes=[T2b[s]])
                cx.op("dve", lambda: V.tensor_tensor(out=T2[:, s, :], in0=T2[:, s, :], in1=T1[:, s, :], op=ALU.mult),
                      reads=[T2b[s], T1b[s]], writes=[T2b[s]])
                cx.op("dve", lambda: V.tensor_tensor_scan(out=T0[:, s, :], data0=T3[:, s, :], data1=T2[:, s, :], initial=0.0,
                                                          op0=ALU.mult, op1=ALU.add),
                      reads=[T3b[s], T2b[s]], writes=[T0b[s]])
                for tt in range(NT):
                    tsl = slice(tt * TT, (tt + 1) * TT)
                    p = k3[0] % 2; k3[0] += 1
                    for kc in range(8):
                        cx.op("pe", lambda kc=kc: nc.tensor.matmul(
                            ps3[:, p, :], lhsT=wgr[:, s, kc, :], rhs=xn_sb[:, kc, tsl], start=(kc == 0), stop=(kc == 7)),
                            reads=[wgrb[s], xnb[kc][tt]], writes=[ps3b[p]], sig=(kc == 7))
                    cx.op("act", lambda: A(out=T1[:, s, tsl], in_=ps3[:, p, :], func=AF.Gelu_apprx_tanh),
                          reads=[ps3b[p]], writes=[T1b[s]])
                cx.op("dve", lambda: V.tensor_tensor(out=rnnT[:, c, :], in0=T0[:, s, :], in1=T1[:, s, :], op=ALU.mult),
                      reads=[T0b[s], T1b[s]], writes=[rnnb[c]])

            load3(0)
            load3(1)
            stageA(0)
            for c in range(8):
                if c + 1 < 8:
                    stageA(c + 1)
                stageB(c)
                if c + 2 < 8:
                    load3(c + 2)
            cx.barrier_all()

        if getattr(g, "stop", 9) == 3:
            return
        with ExitStack() as st:
            merged = st.enter_context(nc.sbuf_tensor(f"merged{l}", [128, 8, S], BF16))
            mgb = _grid(8, NT)
            for c in range(8):
                cx.dma("sp" if c % 2 == 0 else "act", x_sb[:, c, :], xres_v[:, c, :],
                       reads=[g.xresb[c][tt] for tt in range(NT)], writes=[xb[c][tt] for tt in range(NT)])
            wga = st.enter_context(nc.sbuf_tensor(f"wga{l}", [128, 2, 8, 128], BF16)); wgab = [Buf(), Buf()]
            wgl = st.enter_context(nc.sbuf_tensor(f"wgl{l}", [128, 2, 8, 128], BF16)); wglb = [Buf(), Buf()]
            wpp = st.enter_context(nc.sbuf_tensor(f"wpp{l}", [128, 2, 4, 128], BF16)); wppb = [Buf(), Buf()]
            wrr = st.enter_context(nc.sbuf_tensor(f"wrr{l}", [128, 2, 8, 128], BF16)); wrrb = [Buf(), Buf()]
            sga = st.enter_context(nc.sbuf_tensor(f"sga{l}", [128, 2, TT], F32)); sgab = [Buf(), Buf()]
            sgl = st.enter_context(nc.sbuf_tensor(f"sgl{l}", [128, 2, TT], F32)); sglb = [Buf(), Buf()]
            wp_v = W["w_att_proj"][l].rearrange("(kc p) n -> p kc n", p=128)
            wr_v = W["w_rnn_proj"][l].rearrange("(kc p) n -> p kc n", p=128)
            wo_v = W["w_out"][l].rearrange("(kc p) n -> p kc n", p=128)
            with ExitStack() as s4:
                pga = s4.enter_context(nc.psum_tensor(f"pga{l}", [128, 2, TT], F32)); pgab = [Buf(), Buf()]
                pgl = s4.enter_context(nc.psum_tensor(f"pgl{l}", [128, 2, TT], F32)); pglb = [Buf(), Buf()]
                ppp = s4.enter_context(nc.psum_tensor(f"ppp{l}", [128, 2, TT], F32)); pppb = [Buf(), Buf()]
                prr = s4.enter_context(nc.psum_tensor(f"prr{l}", [128, 2, TT], F32)); prrb = [Buf(), Buf()]

                def load4(c):
                    s = c % 2
                    csl = slice(c * 128, (c + 1) * 128)
                    cx.dma("pool", wga[:, s, :, :], w_in_v[:, :, C_GA + c * 128:C_GA + (c + 1) * 128], writes=[wgab[s]])
                    cx.dma("pool", wgl[:, s, :, :], w_in_v[:, :, C_GL + c * 128:C_GL + (c + 1) * 128], writes=[wglb[s]])
                    cx.dma("pool", wpp[:, s, :, :], wp_v[:, :, csl], writes=[wppb[s]])
                    cx.dma("pool", wrr[:, s, :, :], wr_v[:, :, csl], writes=[wrrb[s]])
                load4(0)
                k4 = 0
                for c in range(8):
                    s = c % 2
                    if c + 1 < 8:
                        load4(c + 1)
                    for tt in range(NT):
                        tsl = slice(tt * TT, (tt + 1) * TT)
                        p = k4 % 2; k4 += 1
                        for kc in range(8):
                            cx.op("pe", lambda kc=kc: nc.tensor.matmul(
                                pga[:, p, :], lhsT=wga[:, s, kc, :], rhs=xn_sb[:, kc, tsl], start=(kc == 0), stop=(kc == 7)),
                                reads=[wgab[s], xnb[kc][tt]], writes=[pgab[p]], sig=(kc == 7))
                        cx.op("act", lambda: nc.scalar.activation(out=sga[:, p, :], in_=pga[:, p, :], func=AF.Sigmoid),
                              reads=[pgab[p]], writes=[sgab[p]])
                        for kc in range(4):
                            cx.op("pe", lambda kc=kc: nc.tensor.matmul(
                                ppp[:, p, :], lhsT=wpp[:, s, kc, :], rhs=attT[:, kc, tsl], start=(kc == 0), stop=(kc == 3)),
                                reads=[wppb[s]] + attTb[tt * 4:(tt + 1) * 4], writes=[pppb[p]], sig=(kc == 3))
                        cx.op("dve", lambda: nc.vector.tensor_tensor(out=sga[:, p, :], in0=ppp[:, p, :], in1=sga[:, p, :], op=ALU.mult),
                              reads=[pppb[p], sgab[p]], writes=[sgab[p]])
                        for kc in range(8):
                            cx.op("pe", lambda kc=kc: nc.tensor.matmul(
                                pgl[:, p, :], lhsT=wgl[:, s, kc, :], rhs=xn_sb[:, kc, tsl], start=(kc == 0), stop=(kc == 7)),
                                reads=[wglb[s], xnb[kc][tt]], writes=[pglb[p]], sig=(kc == 7))
                        cx.op("act", lambda: nc.scalar.activation(out=sgl[:, p, :], in_=pgl[:, p, :], func=AF.Sigmoid),
                              reads=[pglb[p]], writes=[sglb[p]])
                        for kc in range(8):
                            cx.op("pe", lambda kc=kc: nc.tensor.matmul(
                                prr[:, p, :], lhsT=wrr[:, s, kc, :], rhs=rnnT[:, kc, tsl], start=(kc == 0), stop=(kc == 7)),
                                reads=[wrrb[s]] + rnnb, writes=[prrb[p]], sig=(kc == 7))
                        cx.op("dve", lambda: nc.vector.tensor_tensor(out=sgl[:, p, :], in0=prr[:, p, :], in1=sgl[:, p, :], op=ALU.mult),
                              reads=[prrb[p], sglb[p]], writes=[sglb[p]])
                        cx.op("dve", lambda: nc.vector.tensor_tensor(out=merged[:, c, tsl], in0=sga[:, p, :], in1=sgl[:, p, :], op=ALU.add),
                              reads=[sgab[p], sglb[p]], writes=[mgb[c][tt]])
                cx.barrier_all()
            with ExitStack() as s5:
                wo = s5.enter_context(nc.sbuf_tensor(f"wo{l}", [128, 2, 8, 128], BF16)); wob = [Buf(), Buf()]
                pso = s5.enter_context(nc.psum_tensor(f"pso{l}", [128, 2, TT], F32)); psob = [Buf(), Buf()]

                def load5(c):
                    s = c % 2
                    cx.dma("pool", wo[:, s, :, :], wo_v[:, :, c * 128:(c + 1) * 128], writes=[wob[s]])
                load5(0)
                tiles5 = [(c, tt) for c in range(8) for tt in range(NT)]

                for k, (c, tt) in enumerate(tiles5):
                    s = c % 2
                    if tt == 0 and c + 1 < 8:
                        load5(c + 1)
                    tsl = slice(tt * TT, (tt + 1) * TT)
                    p = k % 2
                    for kc in range(8):
                        cx.op("pe", lambda kc=kc: nc.tensor.matmul(
                            pso[:, p, :], lhsT=wo[:, s, kc, :], rhs=merged[:, kc, tsl], start=(kc == 0), stop=(kc == 7)),
                            reads=[wob[s], mgb[kc][tt]], writes=[psob[p]], sig=(kc == 7))
                    cx.op("dve", lambda: nc.vector.tensor_tensor(out=x_sb[:, c, tsl], in0=pso[:, p, :], in1=x_sb[:, c, tsl], op=ALU.add),
                          reads=[psob[p], xb[c][tt]], writes=[xb[c][tt]])
                cx.barrier_all()


WNAMES = ["ffn1_wg", "ffn1_wu", "ffn1_wd", "w_in", "rg_wa", "rg_wx", "w_att_proj", "w_rnn_proj", "w_out",
          "ffn2_wg", "ffn2_wu", "ffn2_wd"]
WSHAPES = {"ffn1_wg": [DEPTH, D, DFF], "ffn1_wu": [DEPTH, D, DFF], "ffn1_wd": [DEPTH, DFF, D],
           "w_in": [DEPTH, D, N_IN], "rg_wa": [DEPTH, 16, 64, 64], "rg_wx": [DEPTH, 16, 64, 64],
           "w_att_proj": [DEPTH, 512, D], "w_rnn_proj": [DEPTH, D, D], "w_out": [DEPTH, D, D],
           "ffn2_wg": [DEPTH, D, DFF], "ffn2_wu": [DEPTH, D, DFF], "ffn2_wd": [DEPTH, DFF, D]}


def build_program(plan=None, stop=9):
    if plan is None:
        plan = ["ffn1_0", "mix_0", "ffn2_0+ffn1_1", "mix_1", "ffn2_1+final"]
    nc = bass.Bass("TRN2", target_bir_lowering=False)
    xT = nc.dram_tensor("xT", [D, S], F32, kind="ExternalInput").ap()
    prm = nc.dram_tensor("prm", [128, NP], F32, kind="ExternalInput").ap()
    W = {n: nc.dram_tensor(n, WSHAPES[n], F32, kind="ExternalInput").ap() for n in WNAMES}
    yT = nc.dram_tensor("yT", [D, S], F32, kind="ExternalOutput").ap()
    xres = nc.dram_tensor("xres", [D, S], F32, kind="Internal").ap()
    g = G()
    g.nc = nc
    g.stop = stop
    g.xres = xres
    g.xresb = _grid(8, NT)
    xT_v = xT.rearrange("(c p) t -> p c t", p=128)
    yT_v = yT.rearrange("(c p) t -> p c t", p=128)
    xres_v = xres.rearrange("(c p) t -> p c t", p=128)
    with ExitStack() as top:
        cx = Ctx(nc, top)
        g.cx = cx
        g.prm_sb = top.enter_context(nc.sbuf_tensor("prm_sb", [128, NP], F32)); g.prmb = Buf()
        g.ones_bf = top.enter_context(nc.sbuf_tensor("ones_bf", [128, 128], BF16))
        g.ident = top.enter_context(nc.sbuf_tensor("ident", [128, 128], BF16))
        g.negm = top.enter_context(nc.sbuf_tensor("negm", [128, 128], F32))
        g.negb = top.enter_context(nc.sbuf_tensor("negb", [128, 128], BF16))
        g.zer = top.enter_context(nc.sbuf_tensor("zer", [128, 128], F32))
        g.pw3 = top.enter_context(nc.sbuf_tensor("pw3", [128, 32, 4], F32))
        g.eps_c = top.enter_context(nc.sbuf_tensor("eps_c", [128, 1], F32))
        g.one_c = top.enter_context(nc.sbuf_tensor("one_c", [128, 1], F32))
        g.constb = Buf()
        P = nc.gpsimd
        cx.dma("sp", g.prm_sb[:, :], prm[:, :], writes=[g.prmb])
        cx.op("pool", lambda: P.memset(g.ones_bf[:, :], 1.0), writes=[g.constb])
        cx.op("pool", lambda: P.memset(g.zer[:, :], 0.0), writes=[g.constb])
        cx.op("pool", lambda: P.memset(g.eps_c[:, :], EPS), writes=[g.constb])
        cx.op("pool", lambda: P.memset(g.one_c[:, :], 1.0), writes=[g.constb])
        for i in range(IT):
            cx.op("pool", lambda i=i: P.memset(g.pw3[:, i, :], 2.0 ** -(i + 1)), writes=[g.constb])
        cx.op("pool", lambda: P.affine_select(out=g.ident[:, :], in_=g.ones_bf[:, :], pattern=[[1, 128]],
                                              compare_op=ALU.is_equal, fill=0.0, base=0, channel_multiplier=-1),
              reads=[g.constb], writes=[g.constb])
        cx.op("pool", lambda: P.affine_select(out=g.negm[:, :], in_=g.zer[:, :], pattern=[[-1, 128]],
                                              compare_op=ALU.is_ge, fill=-1e30, base=0, channel_multiplier=1),
              reads=[g.constb], writes=[g.constb])
        cx.op("pool", lambda: P.affine_select(out=g.negb[:, :], in_=g.zer[:, :], pattern=[[-1, 128]],
                                              compare_op=ALU.is_ge, fill=NEG, base=0, channel_multiplier=1),
              reads=[g.constb], writes=[g.constb])
        cx.barrier_all()

        def ffn_phase(src_v, steps, final, next_mix_l=None):
            with ExitStack() as st:
                g.uid = getattr(g, "uid", 0) + 1
                x_sb, xb, xn_sb, xnb = g.x_sb, g.xb, g.xn_sb, g.xnb
                if src_v is not None:
                    for tt in range(NT):
                        tsl = slice(tt * TT, (tt + 1) * TT)
                        cx.dma("sp" if tt % 2 == 0 else "act", x_sb[:, :, tsl], src_v[:, :, tsl],
                               writes=[xb[c][tt] for c in range(8)])
                for si, (l, which) in enumerate(steps):
                    with ExitStack() as s2:
                        gcol = l * PL + (P_F1 if which == 1 else P_F2)
                        emit_norm(g, s2, x_sb, xb, gcol, norm_to_bf16(g, x_sb, xb, xn_sb, xnb, gcol), f"f{l}{which}")
                        emit_ffn(g, s2, x_sb, xb, xn_sb, xnb, W[f"ffn{which}_wg"][l], W[f"ffn{which}_wu"][l],
                                 W[f"ffn{which}_wd"][l], f"f{l}{which}")
                        cx.barrier_all()
                if not final:
                    for tt in range(NT):
                        tsl = slice(tt * TT, (tt + 1) * TT)
                        cx.dma("sp" if tt % 2 == 0 else "act", xres_v[:, :, tsl], x_sb[:, :, tsl],
                               reads=[xb[c][tt] for c in range(8)], writes=[g.xresb[c][tt] for c in range(8)])
                    if next_mix_l is not None:
                        with ExitStack() as s2:
                            gcol = next_mix_l * PL + P_MX
                            emit_norm(g, s2, x_sb, xb, gcol, norm_to_bf16(g, x_sb, xb, xn_sb, xnb, gcol), f"mx{next_mix_l}")
                            cx.barrier_all()
                else:
                    with ExitStack() as s2:
                        yo = s2.enter_context(nc.sbuf_tensor(f"yo{g.uid}", [128, 2, 8, TT], F32))
                        yob = [Buf(), Buf()]

                        def fin(tt, c, rs_ap, rs_buf):
                            tsl = slice(tt * TT, (tt + 1) * TT)
                            cx.op("dve", lambda: nc.vector.scalar_tensor_tensor(
                                out=yo[:, tt % 2, c, :], in0=x_sb[:, c, tsl], scalar=g.prm_sb[:, P_FIN + c:P_FIN + c + 1],
                                in1=rs_ap, op0=ALU.mult, op1=ALU.mult),
                                reads=[xb[c][tt], rs_buf, g.prmb], writes=[yob[tt % 2]])
                            if c == 7:
                                cx.dma("sp", yT_v[:, :, tsl], yo[:, tt % 2, :, :], reads=[yob[tt % 2]])
                        emit_norm(g, s2, x_sb, xb, P_FIN, fin, "fin")
                        cx.barrier_all()
                cx.barrier_all()

        g.xn_sb = top.enter_context(nc.sbuf_tensor("xn_sb", [128, 8, S], BF16))
        g.xnb = _grid(8, NT)
        g.x_sb = top.enter_context(nc.sbuf_tensor("x_sb", [128, 8, S], F32))
        g.xb = _grid(8, NT)
        first = True
        have_xn = False
        for pi, ph in enumerate(plan):
            nxt = plan[pi + 1] if pi + 1 < len(plan) else None
            if ph.startswith("mix"):
                if first:
                    allx = [b for r in g.xb for b in r]
                    cx.dma("sp", g.x_sb[:, :, :], xT_v, writes=allx)
                    cx.dma("sp", xres_v, g.x_sb[:, :, :], reads=allx, writes=[b for r in g.xresb for b in r])
                    cx.barrier_all()
                emit_mixer(g, int(ph.split("_")[1]), W, have_xn=have_xn)
                have_xn = False
            else:
                steps = []
                final = False
                for part in ph.split("+"):
                    if part == "final":
                        final = True
                    else:
                        steps.append((int(part.split("_")[1]), 1 if part.startswith("ffn1") else 2))
                nml = int(nxt.split("_")[1]) if (nxt is not None and nxt.startswith("mix")) else None
                ffn_phase(xT_v if first else None, steps, final, next_mix_l=nml)
                have_xn = nml is not None
            first = False
        if not plan[-1].endswith("final"):
            cx.dma("sp", yT_v, g.x_sb[:, :, :], reads=[b for r in g.xb for b in r])
        cx.barrier_all()
        g.stats = (cx.n_ins, cx.n_wait)
    return nc


def pack_prm(inp):
    prm = np.zeros((128, NP), np.float32)

    def put(col, v):
        prm[:, col:col + 8] = np.asarray(v, np.float32).reshape(8, 128).T
    for l in range(DEPTH):
        b = l * PL
        put(b + P_F1, inp["ffn1_norm"][l]); put(b + P_MX, inp["mix_norm"][l]); put(b + P_F2, inp["ffn2_norm"][l])
        for j in range(4):
            put(b + P_CW + j * 8, inp["conv_w"][l][j])
        put(b + P_CB, inp["conv_b"][l]); put(b + P_BA, inp["rg_ba"][l]); put(b + P_BX, inp["rg_bx"][l])
        put(b + P_LAM, inp["rg_lam"][l])
    put(P_FIN, inp["final_norm"])
    return prm


def kernel(**inputs):
    inp = {k: np.asarray(v) for k, v in inputs.items()}
    x = inp["x"].astype(np.float32, copy=False)
    B = x.shape[0]
    nc = build_program()
    prm = pack_prm(inp)
    wmap = {n: np.ascontiguousarray(inp[n], dtype=np.float32) for n in WNAMES}
    in_maps = []
    for b in range(B):
        m = {"xT": np.ascontiguousarray(x[b].T), "prm": prm}
        m.update(wmap)
        in_maps.append(m)
    res = run_bass_kernel_spmd(nc, in_maps, core_ids=list(range(B)))
    out = np.stack([np.ascontiguousarray(res.results[b]["yT"].T) for b in range(B)], axis=0)
    return out.astype(np.float32, copy=False)
```

```python
import numpy as np
from contextlib import ExitStack
import concourse.bass as bass
import concourse.mybir as mybir
from concourse.bass_utils import run_bass_kernel_spmd

F32 = mybir.dt.float32
BF16 = mybir.dt.bfloat16
AF = mybir.ActivationFunctionType
ALU = mybir.AluOpType
AX = mybir.AxisListType

S = 2048
D = 1024
DFF = 2816
NT = 4
TT = 512
EPS = 1e-6
DEPTH = 2
N_IN = 5032
IT = 12
TOPK = 256
NEG = -30000.0
PL = 88
P_F1, P_MX, P_F2, P_CW, P_CB, P_BA, P_BX, P_LAM = 0, 8, 16, 24, 56, 64, 72, 80
P_FIN = DEPTH * PL
NP = P_FIN + 8
C_Q, C_K, C_V, C_QI, C_KI, C_WI, C_XR, C_GR, C_GA, C_GL = 0, 512, 576, 640, 896, 928, 936, 1960, 2984, 4008

SEM_ROLL = 30000
N_DMA_SEMS = 24


class Buf:
    __slots__ = ("name", "last_w", "readers")

    def __init__(self, name=""):
        self.name = name
        self.last_w = None
        self.readers = {}


class Ctx:
    def __init__(self, nc, stack):
        self.nc = nc
        self.stack = stack
        self.eng = {"pe": nc.tensor, "act": nc.scalar, "dve": nc.vector,
                    "pool": nc.gpsimd, "sp": nc.sync}
        self.sems = {}
        self.cur = {}
        self.cnt = {}
        self.gen = {}
        self.known = {e: {} for e in self.eng}
        for e in self.eng:
            self.gen[e] = 0
            self._new_sem(e)
        self.dma_sems = []
        for i in range(N_DMA_SEMS):
            key = f"dma{i}"
            self.sems[key] = stack.enter_context(nc.semaphore(key))
            self.dma_sems.append([key, 0])
        self.dma_rr = 0
        self.dma_rr2 = 0
        self.n_wait = 0
        self.n_ins = 0

    def _new_sem(self, e):
        key = f"{e}_{self.gen[e]}"
        self.gen[e] += 1
        self.sems[key] = self.stack.enter_context(self.nc.semaphore(key))
        self.cur[e] = key
        self.cnt[e] = 0

    def _wait(self, e, ev):
        key, val, src = ev
        if self.known[e].get(key, 0) >= val:
            return
        if not key.startswith("dma") and key == self.cur[src]:
            assert self.cnt[src] >= val, f"wait on future signal {key} {val} > {self.cnt[src]}"
        self.eng[e].wait_ge(self.sems[key], val)
        self.known[e][key] = val
        self.n_wait += 1

    def _deps(self, e, reads, writes):
        evs = []
        for b in reads:
            if b.last_w is not None:
                evs.append(b.last_w)
        for b in writes:
            if b.last_w is not None and b.last_w[2] != e:
                evs.append(b.last_w)
            for r, ev in b.readers.items():
                if r != e:
                    evs.append(ev)
        for ev in evs:
            self._wait(e, ev)

    def op(self, e, fn, reads=(), writes=(), sig=True):
        self._deps(e, reads, writes)
        ins = fn()
        self.n_ins += 1
        if sig:
            if self.cnt[e] >= SEM_ROLL:
                self._new_sem(e)
            self.cnt[e] += 1
            ins.then_inc(self.sems[self.cur[e]], 1)
            ev = (self.cur[e], self.cnt[e], e)
        else:
            ev = (self.cur[e], self.cnt[e] + 1, e)
        for b in reads:
            b.readers[e] = ev
        for b in writes:
            b.last_w = ev
            b.readers = {}
        return ins

    def dma(self, q, out, in_, reads=(), writes=(), **kw):
        self._deps(q, reads, writes)
        half = len(self.dma_sems) // 2
        if q == "pool":
            slot = self.dma_sems[self.dma_rr % half]
            self.dma_rr += 1
        else:
            slot = self.dma_sems[half + self.dma_rr2 % half]
            self.dma_rr2 += 1
        key, val = slot
        if val > 0:
            self._wait(q, (key, val, "dma"))
        ins = self.eng[q].dma_start(out=out, in_=in_, **kw)
        slot[1] = val + 16
        ins.then_inc(self.sems[key], 16)
        ev = (key, val + 16, "dma:" + q)
        for b in reads:
            b.readers["dma:" + q + key] = ev
        for b in writes:
            b.last_w = ev
            b.readers = {}
        self.n_ins += 1
        return ev

    def barrier_all(self):
        evs = [(self.cur[e], self.cnt[e], e) for e in self.eng if self.cnt[e] > 0]
        evs += [(key, val, "dma") for key, val in self.dma_sems if val > 0]
        for e in self.eng:
            for ev in evs:
                if ev[2] != e:
                    self._wait(e, ev)


class G:
    pass


def _grid(a, b):
    return [[Buf() for _ in range(b)] for _ in range(a)]


def emit_norm(g, st, x_sb, xb, gcol, out_fn, tag):
    cx, nc = g.cx, g.nc
    sq = st.enter_context(nc.sbuf_tensor(f"sq{tag}", [128, 2, 8, TT], BF16))
    sqb = [Buf(), Buf()]
    rs = st.enter_context(nc.sbuf_tensor(f"rs{tag}", [128, 2, TT], F32))
    rsb = [Buf(), Buf()]
    ps = st.enter_context(nc.psum_tensor(f"psn{tag}", [128, 2, TT], F32))
    psb = [Buf(), Buf()]
    for tt in range(NT):
        s = tt % 2
        tsl = slice(tt * TT, (tt + 1) * TT)
        for c in range(8):
            cx.op("act", lambda c=c: nc.scalar.activation(out=sq[:, s, c, :], in_=x_sb[:, c, tsl], func=AF.Square),
                  reads=[xb[c][tt]], writes=[sqb[s]], sig=(c == 7))
        for c in range(8):
            cx.op("pe", lambda c=c: nc.tensor.matmul(ps[:, s, :], lhsT=g.ones_bf[:, :], rhs=sq[:, s, c, :],
                                                      start=(c == 0), stop=(c == 7)),
                  reads=[sqb[s], g.constb], writes=[psb[s]], sig=(c == 7))
        cx.op("act", lambda: nc.scalar.activation(out=rs[:, s, :], in_=ps[:, s, :], func=AF.Sqrt,
                                                   scale=1.0 / D, bias=g.eps_c[:, 0:1]),
              reads=[psb[s], g.constb], writes=[rsb[s]])
        cx.op("dve", lambda: nc.vector.reciprocal(out=rs[:, s, :], in_=rs[:, s, :]),
              reads=[rsb[s]], writes=[rsb[s]])
        for c in range(8):
            out_fn(tt, c, rs[:, s, :], rsb[s])


def norm_to_bf16(g, x_sb, xb, xn_sb, xnb, gcol):
    cx, nc = g.cx, g.nc

    def fn(tt, c, rs_ap, rs_buf):
        tsl = slice(tt * TT, (tt + 1) * TT)
        cx.op("dve", lambda: nc.vector.scalar_tensor_tensor(
            out=xn_sb[:, c, tsl], in0=x_sb[:, c, tsl], scalar=g.prm_sb[:, gcol + c:gcol + c + 1], in1=rs_ap,
            op0=ALU.mult, op1=ALU.mult),
            reads=[xb[c][tt], rs_buf, g.prmb], writes=[xnb[c][tt]])
    return fn


GROUPS = [(0, 6), (6, 12), (12, 17), (17, 22)]


def emit_ffn(g, st, x_sb, xb, xn_sb, xnb, wg, wu, wd, tag):
    cx, nc = g.cx, g.nc
    GM = 6
    wg_s = st.enter_context(nc.sbuf_tensor(f"wg_s{tag}", [128, 2, 8, GM * 128], BF16))
    wu_s = st.enter_context(nc.sbuf_tensor(f"wu_s{tag}", [128, 2, 8, GM * 128], BF16))
    wd_s = st.enter_context(nc.sbuf_tensor(f"wd_s{tag}", [128, 2, GM, D], BF16))
    wgb = [Buf(), Buf()]; wub = [Buf(), Buf()]; wdb = [Buf(), Buf()]
    a_s = st.enter_context(nc.sbuf_tensor(f"a_s{tag}", [128, 2, GM, TT], BF16))
    ab = [Buf(), Buf()]
    sg_s = st.enter_context(nc.sbuf_tensor(f"sg_s{tag}", [128, 2, TT], F32))
    sgb = [Buf(), Buf()]
    psg = st.enter_context(nc.psum_tensor(f"psg{tag}", [128, 2, TT], F32))
    psu = st.enter_context(nc.psum_tensor(f"psu{tag}", [128, 2, TT], F32))
    psy = st.enter_context(nc.psum_tensor(f"psy{tag}", [128, 2, TT], F32))
    psgb = [Buf(), Buf()]; psub = [Buf(), Buf()]; psyb = [Buf(), Buf()]
    wgv = wg.rearrange("(kc p) n -> p kc n", p=128)
    wuv = wu.rearrange("(kc p) n -> p kc n", p=128)
    wdv = wd.rearrange("(j p) d -> p j d", p=128)

    def load(gi):
        c0, c1 = GROUPS[gi]
        G_ = c1 - c0
        s = gi % 2
        cx.dma("pool", wg_s[:, s, :, 0:G_ * 128], wgv[:, :, c0 * 128:c1 * 128], writes=[wgb[s]])
        cx.dma("pool", wu_s[:, s, :, 0:G_ * 128], wuv[:, :, c0 * 128:c1 * 128], writes=[wub[s]])
        cx.dma("pool", wd_s[:, s, 0:G_, :], wdv[:, c0:c1, :], writes=[wdb[s]])

    load(0)
    jj = 0
    yy = 0
    for gi, (c0, c1) in enumerate(GROUPS):
        G_ = c1 - c0
        s = gi % 2
        if gi + 1 < len(GROUPS):
            load(gi + 1)
        for tt in range(NT):
            tsl = slice(tt * TT, (tt + 1) * TT)
            sa = (gi * NT + tt) % 2
            for j in range(G_):
                p = jj % 2
                jj += 1
                for kc in range(8):
                    cx.op("pe", lambda kc=kc: nc.tensor.matmul(
                        psg[:, p, :], lhsT=wg_s[:, s, kc, j * 128:(j + 1) * 128], rhs=xn_sb[:, kc, tsl],
                        start=(kc == 0), stop=(kc == 7)),
                        reads=[wgb[s], xnb[kc][tt]], writes=[psgb[p]], sig=(kc == 7))
                for kc in range(8):
                    cx.op("pe", lambda kc=kc: nc.tensor.matmul(
                        psu[:, p, :], lhsT=wu_s[:, s, kc, j * 128:(j + 1) * 128], rhs=xn_sb[:, kc, tsl],
                        start=(kc == 0), stop=(kc == 7)),
                        reads=[wub[s], xnb[kc][tt]], writes=[psub[p]], sig=(kc == 7))
                cx.op("act", lambda: nc.scalar.activation(out=sg_s[:, p, :], in_=psg[:, p, :], func=AF.Silu),
                      reads=[psgb[p]], writes=[sgb[p]])
                cx.op("dve", lambda: nc.vector.tensor_tensor(out=a_s[:, sa, j, :], in0=psu[:, p, :],
                                                             in1=sg_s[:, p, :], op=ALU.mult),
                      reads=[psub[p], sgb[p]], writes=[ab[sa]])
            for dc in range(8):
                p = yy % 2
                yy += 1
                for j in range(G_):
                    cx.op("pe", lambda j=j: nc.tensor.matmul(
                        psy[:, p, :], lhsT=wd_s[:, s, j, dc * 128:(dc + 1) * 128], rhs=a_s[:, sa, j, :],
                        start=(j == 0), stop=(j == G_ - 1)),
                        reads=[wdb[s], ab[sa]], writes=[psyb[p]], sig=(j == G_ - 1))
                cx.op("dve", lambda: nc.vector.scalar_tensor_tensor(
                    out=x_sb[:, dc, tsl], in0=psy[:, p, :], scalar=0.5, in1=x_sb[:, dc, tsl],
                    op0=ALU.mult, op1=ALU.add),
                    reads=[psyb[p], xb[dc][tt]], writes=[xb[dc][tt]])


def emit_mixer(g, l, W, have_xn=False):
    cx, nc = g.cx, g.nc
    pb = l * PL
    w_in_v = W["w_in"][l].rearrange("(kc p) n -> p kc n", p=128)
    xres_v = g.xres.rearrange("(c p) t -> p c t", p=128)
    with ExitStack() as ms:
        xn_sb, xnb = g.xn_sb, g.xnb
        x_sb, xb = g.x_sb, g.xb
        if not have_xn:
            with ExitStack() as st:
                for tt in range(NT):
                    cx.dma("sp", x_sb[:, :, tt * TT:(tt + 1) * TT], xres_v[:, :, tt * TT:(tt + 1) * TT],
                           reads=[g.xresb[c][tt] for c in range(8)], writes=[xb[c][tt] for c in range(8)])
                emit_norm(g, st, x_sb, xb, pb + P_MX, norm_to_bf16(g, x_sb, xb, xn_sb, xnb, pb + P_MX), f"m{l}")
        cx.barrier_all()
        if getattr(g, "stop", 9) == 0:
            return
        attT = ms.enter_context(nc.sbuf_tensor(f"attT{l}", [128, 4, S], BF16))
        attTb = [Buf() for _ in range(16)]
        with ExitStack() as st:
            qT = st.enter_context(nc.sbuf_tensor(f"qT{l}", [128, 4, S], BF16)); qTb = Buf()
            kT2 = st.enter_context(nc.sbuf_tensor(f"kT2{l}", [128, 2, S], BF16)); kT2b = Buf()
            qiT3 = st.enter_context(nc.sbuf_tensor(f"qiT3{l}", [128, 3, S], BF16)); qiT3b = Buf()
            kiT3 = st.enter_context(nc.sbuf_tensor(f"kiT3{l}", [128, 3, S], BF16)); kiT3b = Buf()
            vaug2 = st.enter_context(nc.sbuf_tensor(f"vaug{l}", [128, 16, 2, 128], BF16)); vb = Buf()
            wi_sb = st.enter_context(nc.sbuf_tensor(f"wi{l}", [128, 16, 8], F32)); wib = Buf()
            wabs = st.enter_context(nc.sbuf_tensor(f"wabs{l}", [128, 16, 8], F32)); wabsb = Buf()
            wsgn = st.enter_context(nc.sbuf_tensor(f"wsgn{l}", [128, 16, 8], F32)); wsgnb = Buf()
            with ExitStack() as s1:
                wt = s1.enter_context(nc.sbuf_tensor(f"m1w{l}", [128, 3, 8, 128], BF16))
                wtb = [Buf(), Buf(), Buf()]
                ps1 = s1.enter_context(nc.psum_tensor(f"m1ps{l}", [128, 2, TT], F32))
                ps1b = [Buf(), Buf()]
                psv = s1.enter_context(nc.psum_tensor(f"m1pv{l}", [128, 2, TT], F32))
                psvb = [Buf(), Buf()]
                chunks = []
                for c in range(4):
                    chunks.append(([(C_Q + c * 128, 128, 0)], 128, qT[:, c, :], qTb))
                chunks.append(([(C_K, 64, 0), (C_K, 64, 64)], 128, "K", kT2b))
                cx.op("dve", lambda: nc.vector.memset(kT2[:, :, :], 0.0), writes=[kT2b])
                chunks.append(([(C_QI, 96, 0)], 96, qiT3[0:96, 0, :], qiT3b))
                chunks.append(([(C_QI + 96, 96, 0)], 96, qiT3[0:96, 1, :], qiT3b))
                chunks.append(([(C_QI + 192, 64, 0)], 64, qiT3[0:64, 2, :], qiT3b))
                chunks.append(([(C_KI, 32, 0), (C_KI, 32, 32), (C_KI, 32, 64)], 96, "KI", kiT3b))
                cx.op("dve", lambda: nc.vector.memset(kiT3[:, :, :], 0.0), writes=[kiT3b])
                cx.op("dve", lambda: nc.vector.memset(qiT3[:, :, :], 0.0), writes=[qiT3b])
                chunks.append(([(C_V, 64, 0), (C_KI, 40, 64)], 104, None, None))

                def loadw(i):
                    cols, M, _, _ = chunks[i]
                    s = i % 3
                    for (c0, n, off) in cols:
                        cx.dma("pool", wt[:, s, :, off:off + n], w_in_v[:, :, c0:c0 + n], writes=[wtb[s]])
                loadw(0)
                loadw(1)
                pp = 0
                for i, (cols, M, dst, dstb) in enumerate(chunks):
                    s = i % 3
                    if i + 2 < len(chunks):
                        loadw(i + 2)
                    if dst is not None:
                        for tt in range(NT):
                            tsl = slice(tt * TT, (tt + 1) * TT)
                            p = pp % 2
                            pp += 1
                            for kc in range(8):
                                cx.op("pe", lambda kc=kc: nc.tensor.matmul(
                                    ps1[0:M, p, :], lhsT=wt[:, s, kc, 0:M], rhs=xn_sb[:, kc, tsl],
                                    start=(kc == 0), stop=(kc == 7)),
                                    reads=[wtb[s], xnb[kc][tt]], writes=[ps1b[p]], sig=(kc == 7))
                            if dst == "KI":
                                for v in range(3):
                                    cx.op("dve", lambda v=v: nc.vector.tensor_copy(out=kiT3[32 * v:32 * v + 32, v, tsl],
                                                                                   in_=ps1[32 * v:32 * v + 32, p, :]),
                                          reads=[ps1b[p]], writes=[dstb])
                            elif isinstance(dst, str):
                                cx.op("dve", lambda: nc.vector.tensor_copy(out=kT2[0:64, 0, tsl], in_=ps1[0:64, p, :]),
                                      reads=[ps1b[p]], writes=[dstb])
                                cx.op("dve", lambda: nc.vector.tensor_copy(out=kT2[64:128, 1, tsl], in_=ps1[64:128, p, :]),
                                      reads=[ps1b[p]], writes=[dstb])
                            elif pp % 2 == 0:
                                cx.op("act", lambda: nc.scalar.copy(out=dst[:, tsl], in_=ps1[0:M, p, :]),
                                      reads=[ps1b[p]], writes=[dstb])
                            else:
                                cx.op("dve", lambda: nc.vector.tensor_copy(out=dst[:, tsl], in_=ps1[0:M, p, :]),
                                      reads=[ps1b[p]], writes=[dstb])
                    else:
                        cx.op("dve", lambda: nc.vector.memset(vaug2[:, :, :, :], 1.0), writes=[vb])
                        for tb in range(16):
                            p = tb % 2
                            for kc in range(8):
                                cx.op("pe", lambda kc=kc: nc.tensor.matmul(
                                    psv[:, p, 0:104], lhsT=xn_sb[:, kc, tb * 128:(tb + 1) * 128], rhs=wt[:, s, kc, 0:104],
                                    start=(kc == 0), stop=(kc == 7)),
                                    reads=[wtb[s], xnb[kc][tb // 4]], writes=[psvb[p]], sig=(kc == 7))
                            cx.op("dve", lambda: nc.vector.tensor_copy(out=vaug2[:, tb, 0, 0:64], in_=psv[:, p, 0:64]),
                                  reads=[psvb[p]], writes=[vb])
                            cx.op("dve", lambda: nc.vector.tensor_copy(out=vaug2[:, tb, 1, 64:128], in_=psv[:, p, 0:64]),
                                  reads=[psvb[p]], writes=[vb])
                            cx.op("dve", lambda: nc.vector.tensor_copy(out=wi_sb[:, tb, :], in_=psv[:, p, 96:104]),
                                  reads=[psvb[p]], writes=[wib])
                        cx.op("dve", lambda: nc.vector.scalar_tensor_tensor(
                            out=wabs[:, :, :], in0=wi_sb[:, :, :], scalar=-1.0, in1=wi_sb[:, :, :],
                            op0=ALU.mult, op1=ALU.max), reads=[wib], writes=[wabsb])
                        cx.op("act", lambda: nc.scalar.activation(out=wsgn[:, :, :], in_=wi_sb[:, :, :], func=AF.Sign),
                              reads=[wib], writes=[wsgnb])
                cx.barrier_all()
            if getattr(g, "stop", 9) == 1:
                return
            with ExitStack() as s2:
                score = g.x_sb; scb = [Buf() for _ in range(4)]
                Rt = s2.enter_context(nc.sbuf_tensor(f"Rt{l}", [128, 3, TT], F32)); Rb = [Buf(), Buf(), Buf()]
                bias = s2.enter_context(nc.sbuf_tensor(f"bias{l}", [128, 4, S], BF16)); biasb = [Buf() for _ in range(4)]
                _bv = g.x_sb.bitcast(BF16)[:, 4:8, :].rearrange("p a (k q) -> p a k q", q=TT)

                def bT(par, kb, lo, hi):
                    return _bv[:, par * 2 + kb // 8, kb % 8, lo:hi]
                biasTb = [[Buf() for _ in range(16)] for _ in range(2)]
                PT = s2.enter_context(nc.sbuf_tensor(f"PT{l}", [128, 3, TT], BF16)); PTb = [Buf(), Buf(), Buf()]
                sm = s2.enter_context(nc.sbuf_tensor(f"sm{l}", [128, 2, 64, 4], F32)); smb = [Buf(), Buf()]
                rdn = s2.enter_context(nc.sbuf_tensor(f"rdn{l}", [128, 2, TT], F32)); rdnb = [Buf(), Buf()]
                smm = s2.enter_context(nc.sbuf_tensor(f"smm{l}", [128, 2, 4], F32)); smmb = [Buf(), Buf()]
                sma = s2.enter_context(nc.sbuf_tensor(f"sma{l}", [128, 2, 4], F32)); smab = [Buf(), Buf()]
                ACTJ = (0, 3)
                junk, junkb = bias[:, 0, :], biasb[0]
                junkA, junkAb = bias[:, 1, :], biasb[1]
                psL = s2.enter_context(nc.psum_tensor(f"psL{l}", [128, 2, TT], F32)); psLb = [Buf(), Buf()]
                accP = s2.enter_context(nc.psum_tensor(f"accP{l}", [128, TT], F32)); accPb = Buf()
                psS = s2.enter_context(nc.psum_tensor(f"psS{l}", [128, 2, TT], F32)); psSb = [Buf(), Buf()]
                psO = s2.enter_context(nc.psum_tensor(f"psO{l}", [128, 2, TT], F32)); psOb = [Buf(), Buf()]
                psB = s2.enter_context(nc.psum_tensor(f"psB{l}", [128, TT], F32)); psBb = Buf()
                st_ = {"L": 0, "R": 0, "S": 0, "PT": 0, "O": 0}
                V = nc.vector
                cx.op("dve", lambda: V.memset(sm[:, :, :, :], 0.0), writes=smb)

                def indexer(qb, j):
                    n = (qb + 1) * 128
                    qsl = slice(qb * 128, (qb + 1) * 128)
                    for kt in range((n + TT - 1) // TT):
                        w = min(TT, n - kt * TT)
                        for h in range(8):
                            jj, po = h // 3, (h % 3) * 32
                            pL = st_["L"] % 2; st_["L"] += 1
                            r = st_["R"] % 3; st_["R"] += 1
                            cx.op("pe", lambda: nc.tensor.matmul(
                                psL[:, pL, 0:w], lhsT=qiT3[:, jj, qsl], rhs=kiT3[:, h % 3, kt * TT:kt * TT + w],
                                start=True, stop=True), reads=[qiT3b, kiT3b], writes=[psLb[pL]])
                            cx.op("act", lambda: nc.scalar.activation(
                                out=Rt[:, r, 0:w], in_=psL[:, pL, 0:w], func=AF.Relu, scale=wabs[:, qb, h:h + 1]),
                                reads=[psLb[pL], wabsb], writes=[Rb[r]])
                            if h == 0:
                                cx.op("dve", lambda: V.tensor_scalar(
                                    out=accP[:, 0:w], in0=Rt[:, r, 0:w], scalar1=wsgn[:, qb, 0:1], scalar2=None,
                                    op0=ALU.mult), reads=[Rb[r], wsgnb], writes=[accPb])
                            elif h < 7:
                                cx.op("dve", lambda: V.scalar_tensor_tensor(
                                    out=accP[:, 0:w], in0=Rt[:, r, 0:w], scalar=wsgn[:, qb, h:h + 1], in1=accP[:, 0:w],
                                    op0=ALU.mult, op1=ALU.add), reads=[Rb[r], wsgnb, accPb], writes=[accPb])
                            else:
                                cx.op("dve", lambda: V.scalar_tensor_tensor(
                                    out=score[:, j, kt * TT:kt * TT + w], in0=Rt[:, r, 0:w], scalar=wsgn[:, qb, h:h + 1],
                                    in1=accP[:, 0:w], op0=ALU.mult, op1=ALU.add),
                                    reads=[Rb[r], wsgnb, accPb], writes=[scb[j]])
                            yield
                    cx.op("dve", lambda: V.tensor_tensor(
                        out=score[:, j, qsl], in0=score[:, j, qsl], in1=g.negm[:, :], op=ALU.add),
                        reads=[scb[j], g.constb], writes=[scb[j]])

                def bisect(sb):
                    s = sb % 2
                    B = smb[s]
                    Bm, Ba = smmb[s], smab[s]
                    js = [j for j in range(4) if 4 * sb + j >= 2]
                    actj = [j for j in js if j in ACTJ]

                    def sv(fn, extra_r=(), extra_w=()):
                        cx.op("dve", fn, reads=[B] + list(extra_r), writes=[B] + list(extra_w))
                    for j in range(4):
                        n = (4 * sb + j + 1) * 128
                        val = (2.0 * TOPK - 1.0 - n) if j in actj else (TOPK - 0.5)
                        sv(lambda: V.memset(sm[:, s, 6, j:j + 1], val))
                    for j in js:
                        qb = 4 * sb + j
                        n = (qb + 1) * 128
                        sv(lambda: V.tensor_reduce(out=sm[:, s, 0, j:j + 1], in_=score[:, j, 0:n], axis=AX.X, op=ALU.max), extra_r=[scb[j]])
                        sv(lambda: V.tensor_reduce(out=sm[:, s, 1, j:j + 1], in_=score[:, j, 0:qb * 128], axis=AX.X, op=ALU.min), extra_r=[scb[j]])
                    sv(lambda: V.tensor_tensor(out=sm[:, s, 2, :], in0=sm[:, s, 0, :], in1=sm[:, s, 1, :], op=ALU.subtract))
                    sv(lambda: V.tensor_tensor(out=sm[:, s, 8:8 + IT, :], in0=g.pw3[:, 0:IT, :],
                                               in1=sm[:, s, 2:3, :].to_broadcast([128, IT, 4]), op=ALU.mult), extra_r=[g.constb])
                    for i in range(IT):
                        sv(lambda: V.tensor_tensor(out=sm[:, s, 5, :], in0=sm[:, s, 1, :], in1=sm[:, s, 8 + i, :], op=ALU.add))
                        if actj:
                            sv(lambda: V.scalar_tensor_tensor(out=smm[:, s, :], in0=sm[:, s, 1, :], scalar=-1.0, in1=sm[:, s, 8 + i, :],
                                                              op0=ALU.mult, op1=ALU.subtract), extra_w=[Bm])
                        for j in actj:
                            n = (4 * sb + j + 1) * 128
                            cx.op("act", lambda: nc.scalar.activation(
                                out=junkA[:, 0:n], in_=score[:, j, 0:n], func=AF.Sign, bias=smm[:, s, j:j + 1], scale=1.0,
                                accum_out=sma[:, s, j:j + 1]), reads=[Bm, scb[j]], writes=[Ba, junkAb])
                        for j in js:
                            if j in actj:
                                continue
                            n = (4 * sb + j + 1) * 128
                            sv(lambda: V.tensor_scalar(out=junk[:, 0:n], in0=score[:, j, 0:n], scalar1=sm[:, s, 5, j:j + 1],
                                                       scalar2=None, op0=ALU.is_ge, op1=ALU.add, accum_out=sm[:, s, 3, j:j + 1]),
                               extra_r=[scb[j]], extra_w=[junkb])
                        for j in actj:
                            sv(lambda: V.tensor_copy(out=sm[:, s, 3, j:j + 1], in_=sma[:, s, j:j + 1]), extra_r=[Ba])
                        sv(lambda: V.tensor_tensor(out=sm[:, s, 4, :], in0=sm[:, s, 3, :], in1=sm[:, s, 6, :], op=ALU.is_ge))
                        sv(lambda: V.tensor_tensor(out=sm[:, s, 4, :], in0=sm[:, s, 4, :], in1=sm[:, s, 8 + i, :], op=ALU.mult))
                        sv(lambda: V.tensor_tensor(out=sm[:, s, 1, :], in0=sm[:, s, 1, :], in1=sm[:, s, 4, :], op=ALU.add))
                        yield
                    for j in js:
                        n = (4 * sb + j + 1) * 128
                        cx.op("dve", lambda: V.tensor_scalar(out=bias[:, j, 0:n], in0=score[:, j, 0:n], scalar1=sm[:, s, 1, j:j + 1],
                                                             scalar2=NEG, op0=ALU.is_lt, op1=ALU.mult),
                              reads=[B, scb[j]], writes=[biasb[j]])
                    for j in range(4):
                        qb = 4 * sb + j
                        if qb < 2:
                            if qb > 0:
                                cx.op("dve", lambda: V.memset(bias[:, j, 0:qb * 128], 0.0), writes=[biasb[j]])
                            cx.op("dve", lambda: V.tensor_copy(out=bias[:, j, qb * 128:(qb + 1) * 128], in_=g.negb[:, :]),
                                  reads=[g.constb], writes=[biasb[j]])

                def make_biasT(sb):
                    par = sb % 2
                    for kb in range(4 * sb + 4):
                        j0 = max(0, kb - 4 * sb)
                        for j in range(j0, 4):
                            cx.op("pe", lambda: nc.tensor.matmul(
                                psB[:, j * 128:(j + 1) * 128], lhsT=bias[:, j, kb * 128:(kb + 1) * 128], rhs=g.ident[:, :],
                                start=(j == j0), stop=(j == 3), skip_group_check=True),
                                reads=[biasb[j], g.constb], writes=[psBb], sig=(j == 3))
                        cx.op("act", lambda: nc.scalar.copy(out=bT(par, kb, j0 * 128, TT), in_=psB[:, j0 * 128:TT]),
                              reads=[psBb], writes=[biasTb[par][kb]])

                def attention(sb):
                    par = sb % 2
                    nk = 4 * sb + 4
                    items = [(h, kb) for h in range(8) for kb in range(nk)]
                    slots = {}

                    def emit_sb(i):
                        h, kb = items[i]
                        c, hp = h // 2, (h % 2) * 64
                        qo = max(0, kb - 4 * sb) * 128
                        qs = slice(sb * TT + qo, (sb + 1) * TT)
                        ksl = slice(kb * 128, (kb + 1) * 128)
                        pS = st_["S"] % 2; st_["S"] += 1
                        slots[i] = pS
                        cx.op("pe", lambda: nc.tensor.matmul(
                            psS[:, pS, qo:TT], lhsT=kT2[:, h % 2, ksl], rhs=qT[:, c, qs],
                            start=True, stop=False), reads=[kT2b, qTb], writes=[psSb[pS]], sig=False)
                        cx.op("pe", lambda: nc.tensor.matmul(
                            psS[:, pS, qo:TT], lhsT=g.ident[:, :], rhs=bT(par, kb, qo, TT),
                            start=False, stop=True), reads=[biasTb[par][kb], g.constb], writes=[psSb[pS]])
                    emit_sb(0)
                    po = None
                    for i, (h, kb) in enumerate(items):
                        c = h // 2
                        if kb == 0:
                            po = st_["O"] % 2; st_["O"] += 1
                        qo = max(0, kb - 4 * sb) * 128
                        pS = slots.pop(i)
                        pt = st_["PT"] % 3; st_["PT"] += 1
                        cx.op("act", lambda: nc.scalar.activation(
                            out=PT[:, pt, qo:TT], in_=psS[:, pS, qo:TT], func=AF.Exp, scale=0.125),
                            reads=[psSb[pS]], writes=[PTb[pt]])
                        if i + 1 < len(items):
                            emit_sb(i + 1)
                        cx.op("pe", lambda: nc.tensor.matmul(
                            psO[:, po, qo:TT], lhsT=vaug2[:, kb, h % 2, :], rhs=PT[:, pt, qo:TT],
                            start=(kb == 0), stop=(kb == nk - 1), skip_group_check=True),
                            reads=[PTb[pt], vb], writes=[psOb[po]], sig=(kb == nk - 1))
                        if kb == nk - 1:
                            osl = slice(sb * TT, (sb + 1) * TT)
                            if h % 2 == 0:
                                num, den, dst_p = slice(0, 64), slice(64, 128), slice(0, 64)
                            else:
                                num, den, dst_p = slice(64, 128), slice(0, 64), slice(64, 128)
                            cx.op("act", lambda: nc.scalar.activation(out=rdn[dst_p, po, :], in_=psO[den, po, :], func=AF.Ln),
                                  reads=[psOb[po]], writes=[rdnb[po]])
                            cx.op("act", lambda: nc.scalar.activation(out=rdn[dst_p, po, :], in_=rdn[dst_p, po, :], func=AF.Exp, scale=-1.0),
                                  reads=[rdnb[po]], writes=[rdnb[po]])
                            cx.op("dve", lambda: V.tensor_tensor(out=attT[dst_p, c, osl], in0=psO[num, po, :], in1=rdn[dst_p, po, :],
                                                                 op=ALU.mult),
                                  reads=[psOb[po], rdnb[po]], writes=attTb[4 * sb:4 * sb + 4])
                        yield

                def run_all(gen):
                    for _ in gen:
                        pass

                def chain(gens):
                    for gg in gens:
                        for _ in gg:
                            yield

                def interleave(main, n_main, side, n_side):
                    done = 0
                    k = 0
                    for _ in main:
                        k += 1
                        tgt = (n_side * k) // max(n_main, 1)
                        while done < tgt:
                            next(side, None)
                            done += 1

                for sb in range(4):
                    qbs = [4 * sb + j for j in range(4) if 4 * sb + j >= 2]
                    n_units = sum(((qb + 1) * 128 + TT - 1) // TT * 8 for qb in qbs)
                    ig = chain([indexer(qb, qb - 4 * sb) for qb in qbs])
                    bg = bisect(sb)
                    if sb == 0:
                        run_all(ig)
                        run_all(bg)
                    else:
                        ag = attention(sb - 1)
                        n_items = 8 * 4 * sb
                        interleave(ig, n_units, ag, n_items // 2)
                        interleave(bg, IT, ag, n_items - n_items // 2)
                        run_all(ag)
                    make_biasT(sb)
                run_all(attention(3))
                cx.barrier_all()

        if getattr(g, "stop", 9) == 2:
            return
        rnnT = ms.enter_context(nc.sbuf_tensor(f"rnnT{l}", [128, 8, S], BF16))
        rnnb = [Buf() for _ in range(8)]
        with ExitStack() as st:
            T0, T1, T2, T3 = [g.x_sb[:, 2 * i_:2 * i_ + 2, :] for i_ in range(4)]
            T0b, T1b, T2b, T3b = [[Buf(), Buf()] for _ in range(4)]
            xrb = st.enter_context(nc.sbuf_tensor(f"xrb{l}", [128, 2, S + 8], BF16)); xrbb = [Buf(), Buf()]
            xrb2 = st.enter_context(nc.sbuf_tensor(f"xrc{l}", [128, 2, S + 8], BF16)); xrb2b = [Buf(), Buf()]
            xcb = st.enter_context(nc.sbuf_tensor(f"xcb{l}", [128, 2, S], BF16)); xcbb = [Buf(), Buf()]
            wxr = st.enter_context(nc.sbuf_tensor(f"wxr{l}", [128, 2, 8, 128], BF16)); wxrb = [Buf(), Buf()]
            wgr = st.enter_context(nc.sbuf_tensor(f"wgr{l}", [128, 2, 8, 128], BF16)); wgrb = [Buf(), Buf()]
            dg = st.enter_context(nc.sbuf_tensor(f"dg{l}", [128, 2, 4, 128], BF16)); dgb = [Buf(), Buf()]
            bdA = st.enter_context(nc.sbuf_tensor(f"bdA{l}", [128, 8, 128], BF16)); bdAb = Buf()
            bdX = st.enter_context(nc.sbuf_tensor(f"bdX{l}", [128, 8, 128], BF16)); bdXb = Buf()
            nsp = st.enter_context(nc.sbuf_tensor(f"nsp{l}", [128, 3, 8], F32)); nspb = Buf()
            ps3 = st.enter_context(nc.psum_tensor(f"ps3{l}", [128, 2, TT], F32)); ps3b = [Buf(), Buf()]
            psc = st.enter_context(nc.psum_tensor(f"psc{l}", [128, 2, TT], F32)); pscb = [Buf(), Buf()]
            psa = st.enter_context(nc.psum_tensor(f"psa{l}", [128, 2, TT], F32)); psab = [Buf(), Buf()]
            psx = st.enter_context(nc.psum_tensor(f"psx{l}", [128, 2, TT], F32)); psxb = [Buf(), Buf()]
            V = nc.vector
            cx.op("dve", lambda: V.memset(bdA[:, :, :], 0.0), writes=[bdAb])
            cx.op("dve", lambda: V.memset(bdX[:, :, :], 0.0), writes=[bdXb])
            cx.op("dve", lambda: V.memset(xrb[:, :, 0:8], 0.0), writes=xrbb)
            cx.op("dve", lambda: V.memset(xrb2[:, :, 0:8], 0.0), writes=xrb2b)
            for half in range(2):
                hs = slice(half * 64, (half + 1) * 64)
                srcA = W["rg_wa"][l].rearrange("(c two) i j -> two i c j", two=2)[half]
                srcX = W["rg_wx"][l].rearrange("(c two) i j -> two i c j", two=2)[half]
                cx.dma("pool", bdA[hs, :, hs], srcA, writes=[bdAb])
                cx.dma("pool", bdX[hs, :, hs], srcX, writes=[bdXb])
            lam = g.prm_sb[:, pb + P_LAM:pb + P_LAM + 8]
            cx.op("act", lambda: nc.scalar.activation(out=nsp[:, 0, :], in_=lam, func=AF.Exp, scale=-1.0),
                  reads=[g.prmb], writes=[nspb])
            cx.op("act", lambda: nc.scalar.activation(out=nsp[:, 0, :], in_=nsp[:, 0, :], func=AF.Ln, scale=1.0, bias=g.one_c[:, 0:1]),
                  reads=[nspb, g.constb], writes=[nspb])
            cx.op("dve", lambda: V.tensor_scalar(out=nsp[:, 1, :], in0=nsp[:, 0, :], scalar1=-8.0, scalar2=None, op0=ALU.mult),
                  reads=[nspb], writes=[nspb])
            cx.op("dve", lambda: V.tensor_scalar(out=nsp[:, 2, :], in0=nsp[:, 0, :], scalar1=-16.0, scalar2=None, op0=ALU.mult),
                  reads=[nspb], writes=[nspb])

            def load3(c):
                s = c % 2
                cx.dma("pool", wxr[:, s, :, :], w_in_v[:, :, C_XR + c * 128:C_XR + (c + 1) * 128], writes=[wxrb[s]])
                cx.dma("pool", wgr[:, s, :, :], w_in_v[:, :, C_GR + c * 128:C_GR + (c + 1) * 128], writes=[wgrb[s]])
            k3 = [0]

            def stageA(c):
                s = c % 2
                pcol = lambda base: g.prm_sb[:, pb + base + c:pb + base + c + 1]
                for j in range(4):
                    cx.op("act", lambda j=j: nc.scalar.activation(
                        out=dg[:, s, j, :], in_=g.ident[:, :], func=AF.Identity,
                        scale=g.prm_sb[:, pb + P_CW + j * 8 + c:pb + P_CW + j * 8 + c + 1]),
                        reads=[g.constb, g.prmb], writes=[dgb[s]])
                for tt in range(NT):
                    tsl = slice(tt * TT, (tt + 1) * TT)
                    p = k3[0] % 2; k3[0] += 1
                    for kc in range(8):
                        cx.op("pe", lambda kc=kc: nc.tensor.matmul(
                            ps3[:, p, :], lhsT=wxr[:, s, kc, :], rhs=xn_sb[:, kc, tsl], start=(kc == 0), stop=(kc == 7)),
                            reads=[wxrb[s], xnb[kc][tt]], writes=[ps3b[p]], sig=(kc == 7))
                    cx.op("dve", lambda: V.tensor_copy(out=xrb[:, s, 4 + tt * TT:4 + (tt + 1) * TT], in_=ps3[:, p, :]),
                          reads=[ps3b[p]], writes=[xrbb[s]])
                    cx.op("dve", lambda: V.tensor_copy(out=xrb2[:, s, 5 + tt * TT:5 + (tt + 1) * TT], in_=ps3[:, p, :]),
                          reads=[ps3b[p]], writes=[xrb2b[s]])
                for tt in range(NT):
                    tsl = slice(tt * TT, (tt + 1) * TT)
                    p = tt % 2
                    for j in range(4):
                        if j % 2 == 1:
                            rhs_ap = xrb[:, s, tt * TT + 1 + j:tt * TT + 1 + j + TT]
                        else:
                            rhs_ap = xrb2[:, s, tt * TT + 2 + j:tt * TT + 2 + j + TT]
                        cx.op("pe", lambda j=j, rhs_ap=rhs_ap: nc.tensor.matmul(
                            psc[:, p, :], lhsT=dg[:, s, j, :], rhs=rhs_ap,
                            start=(j == 0), stop=(j == 3)), reads=[dgb[s], xrbb[s], xrb2b[s]], writes=[pscb[p]], sig=(j == 3))
                    cx.op("dve", lambda: V.tensor_scalar(out=T0[:, s, tsl], in0=psc[:, p, :], scalar1=pcol(P_CB),
                                                         scalar2=None, op0=ALU.add),
                          reads=[pscb[p], g.prmb], writes=[T0b[s]])
                    cx.op("dve", lambda: V.tensor_scalar(out=xcb[:, s, tsl], in0=psc[:, p, :], scalar1=pcol(P_CB),
                                                         scalar2=None, op0=ALU.add),
                          reads=[pscb[p], g.prmb], writes=[xcbb[s]])
                for tt in range(NT):
                    tsl = slice(tt * TT, (tt + 1) * TT)
                    p = tt % 2
                    cx.op("pe", lambda: nc.tensor.matmul(psa[:, p, :], lhsT=bdA[:, c, :], rhs=xcb[:, s, tsl], start=True, stop=True),
                          reads=[bdAb, xcbb[s]], writes=[psab[p]])
                    cx.op("act", lambda: nc.scalar.activation(out=T1[:, s, tsl], in_=psa[:, p, :], func=AF.Sigmoid,
                                                               bias=pcol(P_BA), scale=1.0),
                          reads=[psab[p], g.prmb], writes=[T1b[s]])
                    cx.op("pe", lambda: nc.tensor.matmul(psx[:, p, :], lhsT=bdX[:, c, :], rhs=xcb[:, s, tsl], start=True, stop=True),
                          reads=[bdXb, xcbb[s]], writes=[psxb[p]])
                    cx.op("act", lambda: nc.scalar.activation(out=T2[:, s, tsl], in_=psx[:, p, :], func=AF.Sigmoid,
                                                               bias=pcol(P_BX), scale=1.0),
                          reads=[psxb[p], g.prmb], writes=[T2b[s]])

            def stageB(c):
                s = c % 2
                A = nc.scalar.activation
                cx.op("act", lambda: A(out=T3[:, s, :], in_=T1[:, s, :], func=AF.Exp, scale=nsp[:, 1, c:c + 1]),
                      reads=[T1b[s], nspb], writes=[T3b[s]])
                cx.op("act", lambda: A(out=T1[:, s, :], in_=T1[:, s, :], func=AF.Exp, scale=nsp[:, 2, c:c + 1]),
                      reads=[T1b[s], nspb], writes=[T1b[s]])
                cx.op("act", lambda: A(out=T1[:, s, :], in_=T1[:, s, :], func=AF.Sqrt, scale=-1.0, bias=g.one_c[:, 0:1]),
                      reads=[T1b[s], g.constb], writes=[T1b[s]])
                cx.op("dve", lambda: V.tensor_tensor(out=T2[:, s, :], in0=T2[:, s, :], in1=T0[:, s, :], op=ALU.mult),
                      reads=[T2b[s], T0b[s]], writes=[T2b[s]])
                cx.op("dve", lambda: V.tensor_tensor(out=T2[:, s, :], in0=T2[:, s, :], in1=T1[:, s, :], op=ALU.mult),
                      reads=[T2b[s], T1b[s]], writes=[T2b[s]])
                cx.op("dve", lambda: V.tensor_tensor_scan(out=T0[:, s, :], data0=T3[:, s, :], data1=T2[:, s, :], initial=0.0,
                                                          op0=ALU.mult, op1=ALU.add),
                      reads=[T3b[s], T2b[s]], writes=[T0b[s]])
                for tt in range(NT):
                    tsl = slice(tt * TT, (tt + 1) * TT)
                    p = k3[0] % 2; k3[0] += 1
                    for kc in range(8):
                        cx.op("pe", lambda kc=kc: nc.tensor.matmul(
                            ps3[:, p, :], lhsT=wgr[:, s, kc, :], rhs=xn_sb[:, kc, tsl], start=(kc == 0), stop=(kc == 7)),
                            reads=[wgrb[s], xnb[kc][tt]], writes=[ps3b[p]], sig=(kc == 7))
                    cx.op("act", lambda: A(out=T1[:, s, tsl], in_=ps3[:, p, :], func=AF.Gelu_apprx_tanh),
                          reads=[ps3b[p]], writes=[T1b[s]])
                cx.op("dve", lambda: V.tensor_tensor(out=rnnT[:, c, :], in0=T0[:, s, :], in1=T1[:, s, :], op=ALU.mult),
                      reads=[T0b[s], T1b[s]], writes=[rnnb[c]])

            load3(0)
            load3(1)
            stageA(0)
            for c in range(8):
                if c + 1 < 8:
                    stageA(c + 1)
                stageB(c)
                if c + 2 < 8:
                    load3(c + 2)
            cx.barrier_all()

        if getattr(g, "stop", 9) == 3:
            return
        with ExitStack() as st:
            merged = st.enter_context(nc.sbuf_tensor(f"merged{l}", [128, 8, S], BF16))
            mgb = _grid(8, NT)
            for c in range(8):
                cx.dma("sp" if c % 2 == 0 else "act", x_sb[:, c, :], xres_v[:, c, :],
                       reads=[g.xresb[c][tt] for tt in range(NT)], writes=[xb[c][tt] for tt in range(NT)])
            wga = st.enter_context(nc.sbuf_tensor(f"wga{l}", [128, 2, 8, 128], BF16)); wgab = [Buf(), Buf()]
            wgl = st.enter_context(nc.sbuf_tensor(f"wgl{l}", [128, 2, 8, 128], BF16)); wglb = [Buf(), Buf()]
            wpp = st.enter_context(nc.sbuf_tensor(f"wpp{l}", [128, 2, 4, 128], BF16)); wppb = [Buf(), Buf()]
            wrr = st.enter_context(nc.sbuf_tensor(f"wrr{l}", [128, 2, 8, 128], BF16)); wrrb = [Buf(), Buf()]
            sga = st.enter_context(nc.sbuf_tensor(f"sga{l}", [128, 2, TT], F32)); sgab = [Buf(), Buf()]
            sgl = st.enter_context(nc.sbuf_tensor(f"sgl{l}", [128, 2, TT], F32)); sglb = [Buf(), Buf()]
            wp_v = W["w_att_proj"][l].rearrange("(kc p) n -> p kc n", p=128)
            wr_v = W["w_rnn_proj"][l].rearrange("(kc p) n -> p kc n", p=128)
            wo_v = W["w_out"][l].rearrange("(kc p) n -> p kc n", p=128)
            with ExitStack() as s4:
                pga = s4.enter_context(nc.psum_tensor(f"pga{l}", [128, 2, TT], F32)); pgab = [Buf(), Buf()]
                pgl = s4.enter_context(nc.psum_tensor(f"pgl{l}", [128, 2, TT], F32)); pglb = [Buf(), Buf()]
                ppp = s4.enter_context(nc.psum_tensor(f"ppp{l}", [128, 2, TT], F32)); pppb = [Buf(), Buf()]
                prr = s4.enter_context(nc.psum_tensor(f"prr{l}", [128, 2, TT], F32)); prrb = [Buf(), Buf()]

                def load4(c):
                    s = c % 2
                    csl = slice(c * 128, (c + 1) * 128)
                    cx.dma("pool", wga[:, s, :, :], w_in_v[:, :, C_GA + c * 128:C_GA + (c + 1) * 128], writes=[wgab[s]])
                    cx.dma("pool", wgl[:, s, :, :], w_in_v[:, :, C_GL + c * 128:C_GL + (c + 1) * 128], writes=[wglb[s]])
                    cx.dma("pool", wpp[:, s, :, :], wp_v[:, :, csl], writes=[wppb[s]])
                    cx.dma("pool", wrr[:, s, :, :], wr_v[:, :, csl], writes=[wrrb[s]])
                load4(0)
                k4 = 0
                for c in range(8):
                    s = c % 2
                    if c + 1 < 8:
                        load4(c + 1)
                    for tt in range(NT):
                        tsl = slice(tt * TT, (tt + 1) * TT)
                        p = k4 % 2; k4 += 1
                        for kc in range(8):
                            cx.op("pe", lambda kc=kc: nc.tensor.matmul(
                                pga[:, p, :], lhsT=wga[:, s, kc, :], rhs=xn_sb[:, kc, tsl], start=(kc == 0), stop=(kc == 7)),
                                reads=[wgab[s], xnb[kc][tt]], writes=[pgab[p]], sig=(kc == 7))
                        cx.op("act", lambda: nc.scalar.activation(out=sga[:, p, :], in_=pga[:, p, :], func=AF.Sigmoid),
                              reads=[pgab[p]], writes=[sgab[p]])
                        for kc in range(4):
                            cx.op("pe", lambda kc=kc: nc.tensor.matmul(
                                ppp[:, p, :], lhsT=wpp[:, s, kc, :], rhs=attT[:, kc, tsl], start=(kc == 0), stop=(kc == 3)),
                                reads=[wppb[s]] + attTb[tt * 4:(tt + 1) * 4], writes=[pppb[p]], sig=(kc == 3))
                        cx.op("dve", lambda: nc.vector.tensor_tensor(out=sga[:, p, :], in0=ppp[:, p, :], in1=sga[:, p, :], op=ALU.mult),
                              reads=[pppb[p], sgab[p]], writes=[sgab[p]])
                        for kc in range(8):
                            cx.op("pe", lambda kc=kc: nc.tensor.matmul(
                                pgl[:, p, :], lhsT=wgl[:, s, kc, :], rhs=xn_sb[:, kc, tsl], start=(kc == 0), stop=(kc == 7)),
                                reads=[wglb[s], xnb[kc][tt]], writes=[pglb[p]], sig=(kc == 7))
                        cx.op("act", lambda: nc.scalar.activation(out=sgl[:, p, :], in_=pgl[:, p, :], func=AF.Sigmoid),
                              reads=[pglb[p]], writes=[sglb[p]])
                        for kc in range(8):
                            cx.op("pe", lambda kc=kc: nc.tensor.matmul(
                                prr[:, p, :], lhsT=wrr[:, s, kc, :], rhs=rnnT[:, kc, tsl], start=(kc == 0), stop=(kc == 7)),
                                reads=[wrrb[s]] + rnnb, writes=[prrb[p]], sig=(kc == 7))
                        cx.op("dve", lambda: nc.vector.tensor_tensor(out=sgl[:, p, :], in0=prr[:, p, :], in1=sgl[:, p, :], op=ALU.mult),
                              reads=[prrb[p], sglb[p]], writes=[sglb[p]])
                        cx.op("dve", lambda: nc.vector.tensor_tensor(out=merged[:, c, tsl], in0=sga[:, p, :], in1=sgl[:, p, :], op=ALU.add),
                              reads=[sgab[p], sglb[p]], writes=[mgb[c][tt]])
                cx.barrier_all()
            with ExitStack() as s5:
                wo = s5.enter_context(nc.sbuf_tensor(f"wo{l}", [128, 2, 8, 128], BF16)); wob = [Buf(), Buf()]
                pso = s5.enter_context(nc.psum_tensor(f"pso{l}", [128, 2, TT], F32)); psob = [Buf(), Buf()]

                def load5(c):
                    s = c % 2
                    cx.dma("pool", wo[:, s, :, :], wo_v[:, :, c * 128:(c + 1) * 128], writes=[wob[s]])
                load5(0)
                tiles5 = [(c, tt) for c in range(8) for tt in range(NT)]

                for k, (c, tt) in enumerate(tiles5):
                    s = c % 2
                    if tt == 0 and c + 1 < 8:
                        load5(c + 1)
                    tsl = slice(tt * TT, (tt + 1) * TT)
                    p = k % 2
                    for kc in range(8):
                        cx.op("pe", lambda kc=kc: nc.tensor.matmul(
                            pso[:, p, :], lhsT=wo[:, s, kc, :], rhs=merged[:, kc, tsl], start=(kc == 0), stop=(kc == 7)),
                            reads=[wob[s], mgb[kc][tt]], writes=[psob[p]], sig=(kc == 7))
                    cx.op("dve", lambda: nc.vector.tensor_tensor(out=x_sb[:, c, tsl], in0=pso[:, p, :], in1=x_sb[:, c, tsl], op=ALU.add),
                          reads=[psob[p], xb[c][tt]], writes=[xb[c][tt]])
                cx.barrier_all()


WNAMES = ["ffn1_wg", "ffn1_wu", "ffn1_wd", "w_in", "rg_wa", "rg_wx", "w_att_proj", "w_rnn_proj", "w_out",
          "ffn2_wg", "ffn2_wu", "ffn2_wd"]
WSHAPES = {"ffn1_wg": [DEPTH, D, DFF], "ffn1_wu": [DEPTH, D, DFF], "ffn1_wd": [DEPTH, DFF, D],
           "w_in": [DEPTH, D, N_IN], "rg_wa": [DEPTH, 16, 64, 64], "rg_wx": [DEPTH, 16, 64, 64],
           "w_att_proj": [DEPTH, 512, D], "w_rnn_proj": [DEPTH, D, D], "w_out": [DEPTH, D, D],
           "ffn2_wg": [DEPTH, D, DFF], "ffn2_wu": [DEPTH, D, DFF], "ffn2_wd": [DEPTH, DFF, D]}


def build_program(plan=None, stop=9):
    if plan is None:
        plan = ["ffn1_0", "mix_0", "ffn2_0+ffn1_1", "mix_1", "ffn2_1+final"]
    nc = bass.Bass("TRN2", target_bir_lowering=False)
    xT = nc.dram_tensor("xT", [D, S], F32, kind="ExternalInput").ap()
    prm = nc.dram_tensor("prm", [128, NP], F32, kind="ExternalInput").ap()
    W = {n: nc.dram_tensor(n, WSHAPES[n], F32, kind="ExternalInput").ap() for n in WNAMES}
    yT = nc.dram_tensor("yT", [D, S], F32, kind="ExternalOutput").ap()
    xres = nc.dram_tensor("xres", [D, S], F32, kind="Internal").ap()
    g = G()
    g.nc = nc
    g.stop = stop
    g.xres = xres
    g.xresb = _grid(8, NT)
    xT_v = xT.rearrange("(c p) t -> p c t", p=128)
    yT_v = yT.rearrange("(c p) t -> p c t", p=128)
    xres_v = xres.rearrange("(c p) t -> p c t", p=128)
    with ExitStack() as top:
        cx = Ctx(nc, top)
        g.cx = cx
        g.prm_sb = top.enter_context(nc.sbuf_tensor("prm_sb", [128, NP], F32)); g.prmb = Buf()
        g.ones_bf = top.enter_context(nc.sbuf_tensor("ones_bf", [128, 128], BF16))
        g.ident = top.enter_context(nc.sbuf_tensor("ident", [128, 128], BF16))
        g.negm = top.enter_context(nc.sbuf_tensor("negm", [128, 128], F32))
        g.negb = top.enter_context(nc.sbuf_tensor("negb", [128, 128], BF16))
        g.zer = top.enter_context(nc.sbuf_tensor("zer", [128, 128], F32))
        g.pw3 = top.enter_context(nc.sbuf_tensor("pw3", [128, 32, 4], F32))
        g.eps_c = top.enter_context(nc.sbuf_tensor("eps_c", [128, 1], F32))
        g.one_c = top.enter_context(nc.sbuf_tensor("one_c", [128, 1], F32))
        g.constb = Buf()
        P = nc.gpsimd
        cx.dma("sp", g.prm_sb[:, :], prm[:, :], writes=[g.prmb])
        cx.op("pool", lambda: P.memset(g.ones_bf[:, :], 1.0), writes=[g.constb])
        cx.op("pool", lambda: P.memset(g.zer[:, :], 0.0), writes=[g.constb])
        cx.op("pool", lambda: P.memset(g.eps_c[:, :], EPS), writes=[g.constb])
        cx.op("pool", lambda: P.memset(g.one_c[:, :], 1.0), writes=[g.constb])
        for i in range(IT):
            cx.op("pool", lambda i=i: P.memset(g.pw3[:, i, :], 2.0 ** -(i + 1)), writes=[g.constb])
        cx.op("pool", lambda: P.affine_select(out=g.ident[:, :], in_=g.ones_bf[:, :], pattern=[[1, 128]],
                                              compare_op=ALU.is_equal, fill=0.0, base=0, channel_multiplier=-1),
              reads=[g.constb], writes=[g.constb])
        cx.op("pool", lambda: P.affine_select(out=g.negm[:, :], in_=g.zer[:, :], pattern=[[-1, 128]],
                                              compare_op=ALU.is_ge, fill=-1e30, base=0, channel_multiplier=1),
              reads=[g.constb], writes=[g.constb])
        cx.op("pool", lambda: P.affine_select(out=g.negb[:, :], in_=g.zer[:, :], pattern=[[-1, 128]],
                                              compare_op=ALU.is_ge, fill=NEG, base=0, channel_multiplier=1),
              reads=[g.constb], writes=[g.constb])
        cx.barrier_all()

        def ffn_phase(src_v, steps, final, next_mix_l=None):
            with ExitStack() as st:
                g.uid = getattr(g, "uid", 0) + 1
                x_sb, xb, xn_sb, xnb = g.x_sb, g.xb, g.xn_sb, g.xnb
                if src_v is not None:
                    for tt in range(NT):
                        tsl = slice(tt * TT, (tt + 1) * TT)
                        cx.dma("sp" if tt % 2 == 0 else "act", x_sb[:, :, tsl], src_v[:, :, tsl],
                               writes=[xb[c][tt] for c in range(8)])
                for si, (l, which) in enumerate(steps):
                    with ExitStack() as s2:
                        gcol = l * PL + (P_F1 if which == 1 else P_F2)
                        emit_norm(g, s2, x_sb, xb, gcol, norm_to_bf16(g, x_sb, xb, xn_sb, xnb, gcol), f"f{l}{which}")
                        emit_ffn(g, s2, x_sb, xb, xn_sb, xnb, W[f"ffn{which}_wg"][l], W[f"ffn{which}_wu"][l],
                                 W[f"ffn{which}_wd"][l], f"f{l}{which}")
                        cx.barrier_all()
                if not final:
                    for tt in range(NT):
                        tsl = slice(tt * TT, (tt + 1) * TT)
                        cx.dma("sp" if tt % 2 == 0 else "act", xres_v[:, :, tsl], x_sb[:, :, tsl],
                               reads=[xb[c][tt] for c in range(8)], writes=[g.xresb[c][tt] for c in range(8)])
                    if next_mix_l is not None:
                        with ExitStack() as s2:
                            gcol = next_mix_l * PL + P_MX
                            emit_norm(g, s2, x_sb, xb, gcol, norm_to_bf16(g, x_sb, xb, xn_sb, xnb, gcol), f"mx{next_mix_l}")
                            cx.barrier_all()
                else:
                    with ExitStack() as s2:
                        yo = s2.enter_context(nc.sbuf_tensor(f"yo{g.uid}", [128, 2, 8, TT], F32))
                        yob = [Buf(), Buf()]

                        def fin(tt, c, rs_ap, rs_buf):
                            tsl = slice(tt * TT, (tt + 1) * TT)
                            cx.op("dve", lambda: nc.vector.scalar_tensor_tensor(
                                out=yo[:, tt % 2, c, :], in0=x_sb[:, c, tsl], scalar=g.prm_sb[:, P_FIN + c:P_FIN + c + 1],
                                in1=rs_ap, op0=ALU.mult, op1=ALU.mult),
                                reads=[xb[c][tt], rs_buf, g.prmb], writes=[yob[tt % 2]])
                            if c == 7:
                                cx.dma("sp", yT_v[:, :, tsl], yo[:, tt % 2, :, :], reads=[yob[tt % 2]])
                        emit_norm(g, s2, x_sb, xb, P_FIN, fin, "fin")
                        cx.barrier_all()
                cx.barrier_all()

        g.xn_sb = top.enter_context(nc.sbuf_tensor("xn_sb", [128, 8, S], BF16))
        g.xnb = _grid(8, NT)
        g.x_sb = top.enter_context(nc.sbuf_tensor("x_sb", [128, 8, S], F32))
        g.xb = _grid(8, NT)
        first = True
        have_xn = False
        for pi, ph in enumerate(plan):
            nxt = plan[pi + 1] if pi + 1 < len(plan) else None
            if ph.startswith("mix"):
                if first:
                    allx = [b for r in g.xb for b in r]
                    cx.dma("sp", g.x_sb[:, :, :], xT_v, writes=allx)
                    cx.dma("sp", xres_v, g.x_sb[:, :, :], reads=allx, writes=[b for r in g.xresb for b in r])
                    cx.barrier_all()
                emit_mixer(g, int(ph.split("_")[1]), W, have_xn=have_xn)
                have_xn = False
            else:
                steps = []
                final = False
                for part in ph.split("+"):
                    if part == "final":
                        final = True
                    else:
                        steps.append((int(part.split("_")[1]), 1 if part.startswith("ffn1") else 2))
                nml = int(nxt.split("_")[1]) if (nxt is not None and nxt.startswith("mix")) else None
                ffn_phase(xT_v if first else None, steps, final, next_mix_l=nml)
                have_xn = nml is not None
            first = False
        if not plan[-1].endswith("final"):
            cx.dma("sp", yT_v, g.x_sb[:, :, :], reads=[b for r in g.xb for b in r])
        cx.barrier_all()
        g.stats = (cx.n_ins, cx.n_wait)
    return nc


def pack_prm(inp):
    prm = np.zeros((128, NP), np.float32)

    def put(col, v):
        prm[:, col:col + 8] = np.asarray(v, np.float32).reshape(8, 128).T
    for l in range(DEPTH):
        b = l * PL
        put(b + P_F1, inp["ffn1_norm"][l]); put(b + P_MX, inp["mix_norm"][l]); put(b + P_F2, inp["ffn2_norm"][l])
        for j in range(4):
            put(b + P_CW + j * 8, inp["conv_w"][l][j])
        put(b + P_CB, inp["conv_b"][l]); put(b + P_BA, inp["rg_ba"][l]); put(b + P_BX, inp["rg_bx"][l])
        put(b + P_LAM, inp["rg_lam"][l])
    put(P_FIN, inp["final_norm"])
    return prm


def kernel(**inputs):
    inp = {k: np.asarray(v) for k, v in inputs.items()}
    x = inp["x"].astype(np.float32, copy=False)
    B = x.shape[0]
    nc = build_program()
    prm = pack_prm(inp)
    wmap = {n: np.ascontiguousarray(inp[n], dtype=np.float32) for n in WNAMES}
    in_maps = []
    for b in range(B):
        m = {"xT": np.ascontiguousarray(x[b].T), "prm": prm}
        m.update(wmap)
        in_maps.append(m)
    res = run_bass_kernel_spmd(nc, in_maps, core_ids=list(range(B)))
    out = np.stack([np.ascontiguousarray(res.results[b]["yT"].T) for b in range(B)], axis=0)
    return out.astype(np.float32, copy=False)
```

```python
import numpy as np
from contextlib import ExitStack
import concourse.bass as bass
import concourse.mybir as mybir
from concourse.bass_utils import run_bass_kernel_spmd

F32 = mybir.dt.float32
BF16 = mybir.dt.bfloat16
AF = mybir.ActivationFunctionType
ALU = mybir.AluOpType
AX = mybir.AxisListType

S = 2048
D = 1024
DFF = 2816
NT = 4
TT = 512
EPS = 1e-6
DEPTH = 2
N_IN = 5032
IT = 12
TOPK = 256
NEG = -30000.0
PL = 88
P_F1, P_MX, P_F2, P_CW, P_CB, P_BA, P_BX, P_LAM = 0, 8, 16, 24, 56, 64, 72, 80
P_FIN = DEPTH * PL
NP = P_FIN + 8
C_Q, C_K, C_V, C_QI, C_KI, C_WI, C_XR, C_GR, C_GA, C_GL = 0, 512, 576, 640, 896, 928, 936, 1960, 2984, 4008

SEM_ROLL = 30000
N_DMA_SEMS = 24


class Buf:
    __slots__ = ("name", "last_w", "readers")

    def __init__(self, name=""):
        self.name = name
        self.last_w = None
        self.readers = {}


class Ctx:
    def __init__(self, nc, stack):
        self.nc = nc
        self.stack = stack
        self.eng = {"pe": nc.tensor, "act": nc.scalar, "dve": nc.vector,
                    "pool": nc.gpsimd, "sp": nc.sync}
        self.sems = {}
        self.cur = {}
        self.cnt = {}
        self.gen = {}
        self.known = {e: {} for e in self.eng}
        for e in self.eng:
            self.gen[e] = 0
            self._new_sem(e)
        self.dma_sems = []
        for i in range(N_DMA_SEMS):
            key = f"dma{i}"
            self.sems[key] = stack.enter_context(nc.semaphore(key))
            self.dma_sems.append([key, 0])
        self.dma_rr = 0
        self.dma_rr2 = 0
        self.n_wait = 0
        self.n_ins = 0

    def _new_sem(self, e):
        key = f"{e}_{self.gen[e]}"
        self.gen[e] += 1
        self.sems[key] = self.stack.enter_context(self.nc.semaphore(key))
        self.cur[e] = key
        self.cnt[e] = 0

    def _wait(self, e, ev):
        key, val, src = ev
        if self.known[e].get(key, 0) >= val:
            return
        if not key.startswith("dma") and key == self.cur[src]:
            assert self.cnt[src] >= val, f"wait on future signal {key} {val} > {self.cnt[src]}"
        self.eng[e].wait_ge(self.sems[key], val)
        self.known[e][key] = val
        self.n_wait += 1

    def _deps(self, e, reads, writes):
        evs = []
        for b in reads:
            if b.last_w is not None:
                evs.append(b.last_w)
        for b in writes:
            if b.last_w is not None and b.last_w[2] != e:
                evs.append(b.last_w)
            for r, ev in b.readers.items():
                if r != e:
                    evs.append(ev)
        for ev in evs:
            self._wait(e, ev)

    def op(self, e, fn, reads=(), writes=(), sig=True):
        self._deps(e, reads, writes)
        ins = fn()
        self.n_ins += 1
        if sig:
            if self.cnt[e] >= SEM_ROLL:
                self._new_sem(e)
            self.cnt[e] += 1
            ins.then_inc(self.sems[self.cur[e]], 1)
            ev = (self.cur[e], self.cnt[e], e)
        else:
            ev = (self.cur[e], self.cnt[e] + 1, e)
        for b in reads:
            b.readers[e] = ev
        for b in writes:
            b.last_w = ev
            b.readers = {}
        return ins

    def dma(self, q, out, in_, reads=(), writes=(), **kw):
        self._deps(q, reads, writes)
        half = len(self.dma_sems) // 2
        if q == "pool":
            slot = self.dma_sems[self.dma_rr % half]
            self.dma_rr += 1
        else:
            slot = self.dma_sems[half + self.dma_rr2 % half]
            self.dma_rr2 += 1
        key, val = slot
        if val > 0:
            self._wait(q, (key, val, "dma"))
        ins = self.eng[q].dma_start(out=out, in_=in_, **kw)
        slot[1] = val + 16
        ins.then_inc(self.sems[key], 16)
        ev = (key, val + 16, "dma:" + q)
        for b in reads:
            b.readers["dma:" + q + key] = ev
        for b in writes:
            b.last_w = ev
            b.readers = {}
        self.n_ins += 1
        return ev

    def barrier_all(self):
        evs = [(self.cur[e], self.cnt[e], e) for e in self.eng if self.cnt[e] > 0]
        evs += [(key, val, "dma") for key, val in self.dma_sems if val > 0]
        for e in self.eng:
            for ev in evs:
                if ev[2] != e:
                    self._wait(e, ev)


class G:
    pass


def _grid(a, b):
    return [[Buf() for _ in range(b)] for _ in range(a)]


class Normer:
    def __init__(self, g, st, tag):
        nc = g.nc
        self.g = g
        self.sq = st.enter_context(nc.sbuf_tensor(f"sq{tag}", [128, 2, 8, TT], BF16))
        self.sqb = [Buf(), Buf()]
        self.rs = st.enter_context(nc.sbuf_tensor(f"rs{tag}", [128, 2, TT], F32))
        self.rsb = [Buf(), Buf()]
        self.ps = st.enter_context(nc.psum_tensor(f"psn{tag}", [128, 2, TT], F32))
        self.psb = [Buf(), Buf()]
        self.k = 0

    def tile(self, x_sb, xb, tt, out_fn):
        g = self.g
        cx, nc = g.cx, g.nc
        sq, rs, ps = self.sq, self.rs, self.ps
        s = self.k % 2
        self.k += 1
        tsl = slice(tt * TT, (tt + 1) * TT)
        for c in range(8):
            cx.op("act", lambda c=c: nc.scalar.activation(out=sq[:, s, c, :], in_=x_sb[:, c, tsl], func=AF.Square),
                  reads=[xb[c][tt]], writes=[self.sqb[s]], sig=(c == 7))
        for c in range(8):
            cx.op("pe", lambda c=c: nc.tensor.matmul(ps[:, s, :], lhsT=g.ones_bf[:, :], rhs=sq[:, s, c, :],
                                                      start=(c == 0), stop=(c == 7)),
                  reads=[self.sqb[s], g.constb], writes=[self.psb[s]], sig=(c == 7))
        cx.op("act", lambda: nc.scalar.activation(out=rs[:, s, :], in_=ps[:, s, :], func=AF.Sqrt,
                                                   scale=1.0 / D, bias=g.eps_c[:, 0:1]),
              reads=[self.psb[s], g.constb], writes=[self.rsb[s]])
        cx.op("dve", lambda: nc.vector.reciprocal(out=rs[:, s, :], in_=rs[:, s, :]),
              reads=[self.rsb[s]], writes=[self.rsb[s]])
        for c in range(8):
            out_fn(tt, c, rs[:, s, :], self.rsb[s])


def emit_norm(g, st, x_sb, xb, gcol, out_fn, tag):
    nm = Normer(g, st, tag)
    for tt in range(NT):
        nm.tile(x_sb, xb, tt, out_fn)


def norm_to_bf16(g, x_sb, xb, xn_sb, xnb, gcol):
    cx, nc = g.cx, g.nc

    def fn(tt, c, rs_ap, rs_buf):
        tsl = slice(tt * TT, (tt + 1) * TT)
        cx.op("dve", lambda: nc.vector.scalar_tensor_tensor(
            out=xn_sb[:, c, tsl], in0=x_sb[:, c, tsl], scalar=g.prm_sb[:, gcol + c:gcol + c + 1], in1=rs_ap,
            op0=ALU.mult, op1=ALU.mult),
            reads=[xb[c][tt], rs_buf, g.prmb], writes=[xnb[c][tt]])
    return fn


GROUPS = [(0, 6), (6, 12), (12, 17), (17, 22)]


class FfnBufs:
    def __init__(self, g, st, tag):
        nc = g.nc
        GM = 6
        self.wg_s = st.enter_context(nc.sbuf_tensor(f"wg_s{tag}", [128, 2, 8, GM * 128], BF16))
        self.wu_s = st.enter_context(nc.sbuf_tensor(f"wu_s{tag}", [128, 2, 8, GM * 128], BF16))
        self.wd_s = st.enter_context(nc.sbuf_tensor(f"wd_s{tag}", [128, 2, GM, D], BF16))
        self.wgb = [Buf(), Buf()]; self.wub = [Buf(), Buf()]; self.wdb = [Buf(), Buf()]
        self.a_s = st.enter_context(nc.sbuf_tensor(f"a_s{tag}", [128, 2, GM, TT], BF16))
        self.ab = [Buf(), Buf()]
        self.sg_s = st.enter_context(nc.sbuf_tensor(f"sg_s{tag}", [128, 2, TT], F32))
        self.sgb = [Buf(), Buf()]
        self.psg = st.enter_context(nc.psum_tensor(f"psg{tag}", [128, 2, TT], F32))
        self.psu = st.enter_context(nc.psum_tensor(f"psu{tag}", [128, 2, TT], F32))
        self.psy = st.enter_context(nc.psum_tensor(f"psy{tag}", [128, 2, TT], F32))
        self.psgb = [Buf(), Buf()]; self.psub = [Buf(), Buf()]; self.psyb = [Buf(), Buf()]
        self.gi = 0
        self.jj = 0
        self.yy = 0
        self.aa = 0


def emit_ffn(g, fb, x_sb, xb, xn_sb, xnb, wg, wu, wd, after_tile=None):
    cx, nc = g.cx, g.nc
    wg_s, wu_s, wd_s, a_s, sg_s, psg, psu, psy = fb.wg_s, fb.wu_s, fb.wd_s, fb.a_s, fb.sg_s, fb.psg, fb.psu, fb.psy
    wgb, wub, wdb, ab, sgb, psgb, psub, psyb = fb.wgb, fb.wub, fb.wdb, fb.ab, fb.sgb, fb.psgb, fb.psub, fb.psyb
    wgv = wg.rearrange("(kc p) n -> p kc n", p=128)
    wuv = wu.rearrange("(kc p) n -> p kc n", p=128)
    wdv = wd.rearrange("(j p) d -> p j d", p=128)
    slots = {}

    def load(i):
        c0, c1 = GROUPS[i]
        G_ = c1 - c0
        s = fb.gi % 2
        fb.gi += 1
        slots[i] = s
        cx.dma("pool", wg_s[:, s, :, 0:G_ * 128], wgv[:, :, c0 * 128:c1 * 128], writes=[wgb[s]])
        cx.dma("pool", wu_s[:, s, :, 0:G_ * 128], wuv[:, :, c0 * 128:c1 * 128], writes=[wub[s]])
        cx.dma("pool", wd_s[:, s, 0:G_, :], wdv[:, c0:c1, :], writes=[wdb[s]])

    load(0)
    for gi, (c0, c1) in enumerate(GROUPS):
        G_ = c1 - c0
        if gi + 1 < len(GROUPS):
            load(gi + 1)
        s = slots[gi]
        for tt in range(NT):
            tsl = slice(tt * TT, (tt + 1) * TT)
            sa = fb.aa % 2
            fb.aa += 1
            for j in range(G_):
                p = fb.jj % 2
                fb.jj += 1
                for kc in range(8):
                    cx.op("pe", lambda kc=kc: nc.tensor.matmul(
                        psg[:, p, :], lhsT=wg_s[:, s, kc, j * 128:(j + 1) * 128], rhs=xn_sb[:, kc, tsl],
                        start=(kc == 0), stop=(kc == 7)),
                        reads=[wgb[s], xnb[kc][tt]], writes=[psgb[p]], sig=(kc == 7))
                for kc in range(8):
                    cx.op("pe", lambda kc=kc: nc.tensor.matmul(
                        psu[:, p, :], lhsT=wu_s[:, s, kc, j * 128:(j + 1) * 128], rhs=xn_sb[:, kc, tsl],
                        start=(kc == 0), stop=(kc == 7)),
                        reads=[wub[s], xnb[kc][tt]], writes=[psub[p]], sig=(kc == 7))
                cx.op("act", lambda: nc.scalar.activation(out=sg_s[:, p, :], in_=psg[:, p, :], func=AF.Silu),
                      reads=[psgb[p]], writes=[sgb[p]])
                cx.op("dve", lambda: nc.vector.tensor_tensor(out=a_s[:, sa, j, :], in0=psu[:, p, :],
                                                             in1=sg_s[:, p, :], op=ALU.mult),
                      reads=[psub[p], sgb[p]], writes=[ab[sa]])
            for dc in range(8):
                p = fb.yy % 2
                fb.yy += 1
                for j in range(G_):
                    cx.op("pe", lambda j=j: nc.tensor.matmul(
                        psy[:, p, :], lhsT=wd_s[:, s, j, dc * 128:(dc + 1) * 128], rhs=a_s[:, sa, j, :],
                        start=(j == 0), stop=(j == G_ - 1)),
                        reads=[wdb[s], ab[sa]], writes=[psyb[p]], sig=(j == G_ - 1))
                cx.op("dve", lambda: nc.vector.scalar_tensor_tensor(
                    out=x_sb[:, dc, tsl], in0=psy[:, p, :], scalar=0.5, in1=x_sb[:, dc, tsl],
                    op0=ALU.mult, op1=ALU.add),
                    reads=[psyb[p], xb[dc][tt]], writes=[xb[dc][tt]])
            if after_tile is not None and gi == len(GROUPS) - 1:
                after_tile(tt)


def emit_mixer(g, l, W, have_xn=False):
    cx, nc = g.cx, g.nc
    pb = l * PL
    w_in_v = W["w_in"][l].rearrange("(kc p) n -> p kc n", p=128)
    xres_v = g.xres.rearrange("(c p) t -> p c t", p=128)
    with ExitStack() as ms:
        xn_sb, xnb = g.xn_sb, g.xnb
        x_sb, xb = g.x_sb, g.xb
        if not have_xn:
            with ExitStack() as st:
                for tt in range(NT):
                    cx.dma("sp", x_sb[:, :, tt * TT:(tt + 1) * TT], xres_v[:, :, tt * TT:(tt + 1) * TT],
                           reads=[g.xresb[c][tt] for c in range(8)], writes=[xb[c][tt] for c in range(8)])
                emit_norm(g, st, x_sb, xb, pb + P_MX, norm_to_bf16(g, x_sb, xb, xn_sb, xnb, pb + P_MX), f"m{l}")
        cx.barrier_all()
        if getattr(g, "stop", 9) == 0:
            return
        attT = ms.enter_context(nc.sbuf_tensor(f"attT{l}", [128, 4, S], BF16))
        attTb = [Buf() for _ in range(16)]
        with ExitStack() as st:
            qT = st.enter_context(nc.sbuf_tensor(f"qT{l}", [128, 4, S], BF16)); qTb = Buf()
            kT2 = st.enter_context(nc.sbuf_tensor(f"kT2{l}", [128, 2, S], BF16)); kT2b = Buf()
            qiT3 = st.enter_context(nc.sbuf_tensor(f"qiT3{l}", [128, 3, S], BF16)); qiT3b = Buf()
            kiT3 = st.enter_context(nc.sbuf_tensor(f"kiT3{l}", [128, 3, S], BF16)); kiT3b = Buf()
            vaug2 = st.enter_context(nc.sbuf_tensor(f"vaug{l}", [128, 16, 2, 128], BF16)); vb = Buf()
            wi_sb = st.enter_context(nc.sbuf_tensor(f"wi{l}", [128, 16, 8], F32)); wib = Buf()
            wabs = st.enter_context(nc.sbuf_tensor(f"wabs{l}", [128, 16, 8], F32)); wabsb = Buf()
            wsgn = st.enter_context(nc.sbuf_tensor(f"wsgn{l}", [128, 16, 8], F32)); wsgnb = Buf()
            with ExitStack() as s1:
                wt = s1.enter_context(nc.sbuf_tensor(f"m1w{l}", [128, 3, 8, 128], BF16))
                wtb = [Buf(), Buf(), Buf()]
                ps1 = s1.enter_context(nc.psum_tensor(f"m1ps{l}", [128, 2, TT], F32))
                ps1b = [Buf(), Buf()]
                psv = s1.enter_context(nc.psum_tensor(f"m1pv{l}", [128, 2, TT], F32))
                psvb = [Buf(), Buf()]
                chunks = []
                for c in range(4):
                    chunks.append(([(C_Q + c * 128, 128, 0)], 128, qT[:, c, :], qTb))
                chunks.append(([(C_K, 64, 0), (C_K, 64, 64)], 128, "K", kT2b))
                cx.op("dve", lambda: nc.vector.memset(kT2[:, :, :], 0.0), writes=[kT2b])
                chunks.append(([(C_QI, 96, 0)], 96, qiT3[0:96, 0, :], qiT3b))
                chunks.append(([(C_QI + 96, 96, 0)], 96, qiT3[0:96, 1, :], qiT3b))
                chunks.append(([(C_QI + 192, 64, 0)], 64, qiT3[0:64, 2, :], qiT3b))
                chunks.append(([(C_KI, 32, 0), (C_KI, 32, 32), (C_KI, 32, 64)], 96, "KI", kiT3b))
                cx.op("dve", lambda: nc.vector.memset(kiT3[:, :, :], 0.0), writes=[kiT3b])
                cx.op("dve", lambda: nc.vector.memset(qiT3[:, :, :], 0.0), writes=[qiT3b])
                chunks.append(([(C_V, 64, 0), (C_KI, 40, 64)], 104, None, None))

                def loadw(i):
                    cols, M, _, _ = chunks[i]
                    s = i % 3
                    for (c0, n, off) in cols:
                        cx.dma("pool", wt[:, s, :, off:off + n], w_in_v[:, :, c0:c0 + n], writes=[wtb[s]])
                loadw(0)
                loadw(1)
                pp = 0
                for i, (cols, M, dst, dstb) in enumerate(chunks):
                    s = i % 3
                    if i + 2 < len(chunks):
                        loadw(i + 2)
                    if dst is not None:
                        for tt in range(NT):
                            tsl = slice(tt * TT, (tt + 1) * TT)
                            p = pp % 2
                            pp += 1
                            for kc in range(8):
                                cx.op("pe", lambda kc=kc: nc.tensor.matmul(
                                    ps1[0:M, p, :], lhsT=wt[:, s, kc, 0:M], rhs=xn_sb[:, kc, tsl],
                                    start=(kc == 0), stop=(kc == 7)),
                                    reads=[wtb[s], xnb[kc][tt]], writes=[ps1b[p]], sig=(kc == 7))
                            if dst == "KI":
                                for v in range(3):
                                    cx.op("dve", lambda v=v: nc.vector.tensor_copy(out=kiT3[32 * v:32 * v + 32, v, tsl],
                                                                                   in_=ps1[32 * v:32 * v + 32, p, :]),
                                          reads=[ps1b[p]], writes=[dstb])
                            elif isinstance(dst, str):
                                cx.op("dve", lambda: nc.vector.tensor_copy(out=kT2[0:64, 0, tsl], in_=ps1[0:64, p, :]),
                                      reads=[ps1b[p]], writes=[dstb])
                                cx.op("dve", lambda: nc.vector.tensor_copy(out=kT2[64:128, 1, tsl], in_=ps1[64:128, p, :]),
                                      reads=[ps1b[p]], writes=[dstb])
                            elif pp % 2 == 0:
                                cx.op("act", lambda: nc.scalar.copy(out=dst[:, tsl], in_=ps1[0:M, p, :]),
                                      reads=[ps1b[p]], writes=[dstb])
                            else:
                                cx.op("dve", lambda: nc.vector.tensor_copy(out=dst[:, tsl], in_=ps1[0:M, p, :]),
                                      reads=[ps1b[p]], writes=[dstb])
                    else:
                        cx.op("dve", lambda: nc.vector.memset(vaug2[:, :, :, :], 1.0), writes=[vb])
                        for tb in range(16):
                            p = tb % 2
                            for kc in range(8):
                                cx.op("pe", lambda kc=kc: nc.tensor.matmul(
                                    psv[:, p, 0:104], lhsT=xn_sb[:, kc, tb * 128:(tb + 1) * 128], rhs=wt[:, s, kc, 0:104],
                                    start=(kc == 0), stop=(kc == 7)),
                                    reads=[wtb[s], xnb[kc][tb // 4]], writes=[psvb[p]], sig=(kc == 7))
                            cx.op("dve", lambda: nc.vector.tensor_copy(out=vaug2[:, tb, 0, 0:64], in_=psv[:, p, 0:64]),
                                  reads=[psvb[p]], writes=[vb])
                            cx.op("dve", lambda: nc.vector.tensor_copy(out=vaug2[:, tb, 1, 64:128], in_=psv[:, p, 0:64]),
                                  reads=[psvb[p]], writes=[vb])
                            cx.op("dve", lambda: nc.vector.tensor_copy(out=wi_sb[:, tb, :], in_=psv[:, p, 96:104]),
                                  reads=[psvb[p]], writes=[wib])
                        cx.op("dve", lambda: nc.vector.scalar_tensor_tensor(
                            out=wabs[:, :, :], in0=wi_sb[:, :, :], scalar=-1.0, in1=wi_sb[:, :, :],
                            op0=ALU.mult, op1=ALU.max), reads=[wib], writes=[wabsb])
                        cx.op("act", lambda: nc.scalar.activation(out=wsgn[:, :, :], in_=wi_sb[:, :, :], func=AF.Sign),
                              reads=[wib], writes=[wsgnb])
                cx.barrier_all()
            if getattr(g, "stop", 9) == 1:
                return
            with ExitStack() as s2:
                score = g.x_sb; scb = [Buf() for _ in range(4)]
                Rt = s2.enter_context(nc.sbuf_tensor(f"Rt{l}", [128, 3, TT], F32)); Rb = [Buf(), Buf(), Buf()]
                bias = s2.enter_context(nc.sbuf_tensor(f"bias{l}", [128, 4, S], BF16)); biasb = [Buf() for _ in range(4)]
                _bv = g.x_sb.bitcast(BF16)[:, 4:8, :].rearrange("p a (k q) -> p a k q", q=TT)

                def bT(par, kb, lo, hi):
                    return _bv[:, par * 2 + kb // 8, kb % 8, lo:hi]
                biasTb = [[Buf() for _ in range(16)] for _ in range(2)]
                PT = s2.enter_context(nc.sbuf_tensor(f"PT{l}", [128, 3, TT], BF16)); PTb = [Buf(), Buf(), Buf()]
                sm = s2.enter_context(nc.sbuf_tensor(f"sm{l}", [128, 2, 64, 4], F32)); smb = [Buf(), Buf()]
                rdn = s2.enter_context(nc.sbuf_tensor(f"rdn{l}", [128, 2, TT], F32)); rdnb = [Buf(), Buf()]
                smm = s2.enter_context(nc.sbuf_tensor(f"smm{l}", [128, 2, 4], F32)); smmb = [Buf(), Buf()]
                sma = s2.enter_context(nc.sbuf_tensor(f"sma{l}", [128, 2, 4], F32)); smab = [Buf(), Buf()]
                ACTJ = (0, 3)
                junk, junkb = bias[:, 0, :], biasb[0]
                junkA, junkAb = bias[:, 1, :], biasb[1]
                psL = s2.enter_context(nc.psum_tensor(f"psL{l}", [128, 2, TT], F32)); psLb = [Buf(), Buf()]
                accP = s2.enter_context(nc.psum_tensor(f"accP{l}", [128, TT], F32)); accPb = Buf()
                psS = s2.enter_context(nc.psum_tensor(f"psS{l}", [128, 2, TT], F32)); psSb = [Buf(), Buf()]
                psO = s2.enter_context(nc.psum_tensor(f"psO{l}", [128, 2, TT], F32)); psOb = [Buf(), Buf()]
                psB = s2.enter_context(nc.psum_tensor(f"psB{l}", [128, TT], F32)); psBb = Buf()
                st_ = {"L": 0, "R": 0, "S": 0, "PT": 0, "O": 0}
                V = nc.vector
                cx.op("dve", lambda: V.memset(sm[:, :, :, :], 0.0), writes=smb)

                def indexer(qb, j):
                    n = (qb + 1) * 128
                    qsl = slice(qb * 128, (qb + 1) * 128)
                    for kt in range((n + TT - 1) // TT):
                        w = min(TT, n - kt * TT)
                        for h in range(8):
                            jj, po = h // 3, (h % 3) * 32
                            pL = st_["L"] % 2; st_["L"] += 1
                            r = st_["R"] % 3; st_["R"] += 1
                            cx.op("pe", lambda: nc.tensor.matmul(
                                psL[:, pL, 0:w], lhsT=qiT3[:, jj, qsl], rhs=kiT3[:, h % 3, kt * TT:kt * TT + w],
                                start=True, stop=True), reads=[qiT3b, kiT3b], writes=[psLb[pL]])
                            cx.op("act", lambda: nc.scalar.activation(
                                out=Rt[:, r, 0:w], in_=psL[:, pL, 0:w], func=AF.Relu, scale=wabs[:, qb, h:h + 1]),
                                reads=[psLb[pL], wabsb], writes=[Rb[r]])
                            if h == 0:
                                cx.op("dve", lambda: V.tensor_scalar(
                                    out=accP[:, 0:w], in0=Rt[:, r, 0:w], scalar1=wsgn[:, qb, 0:1], scalar2=None,
                                    op0=ALU.mult), reads=[Rb[r], wsgnb], writes=[accPb])
                            elif h < 7:
                                cx.op("dve", lambda: V.scalar_tensor_tensor(
                                    out=accP[:, 0:w], in0=Rt[:, r, 0:w], scalar=wsgn[:, qb, h:h + 1], in1=accP[:, 0:w],
                                    op0=ALU.mult, op1=ALU.add), reads=[Rb[r], wsgnb, accPb], writes=[accPb])
                            else:
                                cx.op("dve", lambda: V.scalar_tensor_tensor(
                                    out=score[:, j, kt * TT:kt * TT + w], in0=Rt[:, r, 0:w], scalar=wsgn[:, qb, h:h + 1],
                                    in1=accP[:, 0:w], op0=ALU.mult, op1=ALU.add),
                                    reads=[Rb[r], wsgnb, accPb], writes=[scb[j]])
                            yield
                    cx.op("dve", lambda: V.tensor_tensor(
                        out=score[:, j, qsl], in0=score[:, j, qsl], in1=g.negm[:, :], op=ALU.add),
                        reads=[scb[j], g.constb], writes=[scb[j]])

                def bisect(sb):
                    s = sb % 2
                    B = smb[s]
                    Bm, Ba = smmb[s], smab[s]
                    js = [j for j in range(4) if 4 * sb + j >= 2]
                    actj = [j for j in js if j in ACTJ]

                    def sv(fn, extra_r=(), extra_w=()):
                        cx.op("dve", fn, reads=[B] + list(extra_r), writes=[B] + list(extra_w))
                    for j in range(4):
                        n = (4 * sb + j + 1) * 128
                        val = (2.0 * TOPK - 1.0 - n) if j in actj else (TOPK - 0.5)
                        sv(lambda: V.memset(sm[:, s, 6, j:j + 1], val))
                    for j in js:
                        qb = 4 * sb + j
                        n = (qb + 1) * 128
                        sv(lambda: V.tensor_reduce(out=sm[:, s, 0, j:j + 1], in_=score[:, j, 0:n], axis=AX.X, op=ALU.max), extra_r=[scb[j]])
                        sv(lambda: V.tensor_reduce(out=sm[:, s, 1, j:j + 1], in_=score[:, j, 0:qb * 128], axis=AX.X, op=ALU.min), extra_r=[scb[j]])
                    sv(lambda: V.tensor_tensor(out=sm[:, s, 2, :], in0=sm[:, s, 0, :], in1=sm[:, s, 1, :], op=ALU.subtract))
                    sv(lambda: V.tensor_tensor(out=sm[:, s, 8:8 + IT, :], in0=g.pw3[:, 0:IT, :],
                                               in1=sm[:, s, 2:3, :].to_broadcast([128, IT, 4]), op=ALU.mult), extra_r=[g.constb])
                    for i in range(IT):
                        sv(lambda: V.tensor_tensor(out=sm[:, s, 5, :], in0=sm[:, s, 1, :], in1=sm[:, s, 8 + i, :], op=ALU.add))
                        if actj:
                            sv(lambda: V.scalar_tensor_tensor(out=smm[:, s, :], in0=sm[:, s, 1, :], scalar=-1.0, in1=sm[:, s, 8 + i, :],
                                                              op0=ALU.mult, op1=ALU.subtract), extra_w=[Bm])
                        for j in actj:
                            n = (4 * sb + j + 1) * 128
                            cx.op("act", lambda: nc.scalar.activation(
                                out=junkA[:, 0:n], in_=score[:, j, 0:n], func=AF.Sign, bias=smm[:, s, j:j + 1], scale=1.0,
                                accum_out=sma[:, s, j:j + 1]), reads=[Bm, scb[j]], writes=[Ba, junkAb])
                        for j in js:
                            if j in actj:
                                continue
                            n = (4 * sb + j + 1) * 128
                            sv(lambda: V.tensor_scalar(out=junk[:, 0:n], in0=score[:, j, 0:n], scalar1=sm[:, s, 5, j:j + 1],
                                                       scalar2=None, op0=ALU.is_ge, op1=ALU.add, accum_out=sm[:, s, 3, j:j + 1]),
                               extra_r=[scb[j]], extra_w=[junkb])
                        for j in actj:
                            sv(lambda: V.tensor_copy(out=sm[:, s, 3, j:j + 1], in_=sma[:, s, j:j + 1]), extra_r=[Ba])
                        sv(lambda: V.tensor_tensor(out=sm[:, s, 4, :], in0=sm[:, s, 3, :], in1=sm[:, s, 6, :], op=ALU.is_ge))
                        sv(lambda: V.tensor_tensor(out=sm[:, s, 4, :], in0=sm[:, s, 4, :], in1=sm[:, s, 8 + i, :], op=ALU.mult))
                        sv(lambda: V.tensor_tensor(out=sm[:, s, 1, :], in0=sm[:, s, 1, :], in1=sm[:, s, 4, :], op=ALU.add))
                        yield
                    for j in js:
                        n = (4 * sb + j + 1) * 128
                        cx.op("dve", lambda: V.tensor_scalar(out=bias[:, j, 0:n], in0=score[:, j, 0:n], scalar1=sm[:, s, 1, j:j + 1],
                                                             scalar2=NEG, op0=ALU.is_lt, op1=ALU.mult),
                              reads=[B, scb[j]], writes=[biasb[j]])
                    for j in range(4):
                        qb = 4 * sb + j
                        if qb < 2:
                            if qb > 0:
                                cx.op("dve", lambda: V.memset(bias[:, j, 0:qb * 128], 0.0), writes=[biasb[j]])
                            cx.op("dve", lambda: V.tensor_copy(out=bias[:, j, qb * 128:(qb + 1) * 128], in_=g.negb[:, :]),
                                  reads=[g.constb], writes=[biasb[j]])

                def make_biasT(sb):
                    par = sb % 2
                    for kb in range(4 * sb + 4):
                        j0 = max(0, kb - 4 * sb)
                        for j in range(j0, 4):
                            cx.op("pe", lambda: nc.tensor.matmul(
                                psB[:, j * 128:(j + 1) * 128], lhsT=bias[:, j, kb * 128:(kb + 1) * 128], rhs=g.ident[:, :],
                                start=(j == j0), stop=(j == 3), skip_group_check=True),
                                reads=[biasb[j], g.constb], writes=[psBb], sig=(j == 3))
                        cx.op("act", lambda: nc.scalar.copy(out=bT(par, kb, j0 * 128, TT), in_=psB[:, j0 * 128:TT]),
                              reads=[psBb], writes=[biasTb[par][kb]])

                def attention(sb):
                    par = sb % 2
                    nk = 4 * sb + 4
                    items = [(h, kb) for h in range(8) for kb in range(nk)]
                    slots = {}

                    def emit_sb(i):
                        h, kb = items[i]
                        c, hp = h // 2, (h % 2) * 64
                        qo = max(0, kb - 4 * sb) * 128
                        qs = slice(sb * TT + qo, (sb + 1) * TT)
                        ksl = slice(kb * 128, (kb + 1) * 128)
                        pS = st_["S"] % 2; st_["S"] += 1
                        slots[i] = pS
                        cx.op("pe", lambda: nc.tensor.matmul(
                            psS[:, pS, qo:TT], lhsT=kT2[:, h % 2, ksl], rhs=qT[:, c, qs],
                            start=True, stop=False), reads=[kT2b, qTb], writes=[psSb[pS]], sig=False)
                        cx.op("pe", lambda: nc.tensor.matmul(
                            psS[:, pS, qo:TT], lhsT=g.ident[:, :], rhs=bT(par, kb, qo, TT),
                            start=False, stop=True), reads=[biasTb[par][kb], g.constb], writes=[psSb[pS]])
                    emit_sb(0)
                    po = None
                    for i, (h, kb) in enumerate(items):
                        c = h // 2
                        if kb == 0:
                            po = st_["O"] % 2; st_["O"] += 1
                        qo = max(0, kb - 4 * sb) * 128
                        pS = slots.pop(i)
                        pt = st_["PT"] % 3; st_["PT"] += 1
                        cx.op("act", lambda: nc.scalar.activation(
                            out=PT[:, pt, qo:TT], in_=psS[:, pS, qo:TT], func=AF.Exp, scale=0.125),
                            reads=[psSb[pS]], writes=[PTb[pt]])
                        if i + 1 < len(items):
                            emit_sb(i + 1)
                        cx.op("pe", lambda: nc.tensor.matmul(
                            psO[:, po, qo:TT], lhsT=vaug2[:, kb, h % 2, :], rhs=PT[:, pt, qo:TT],
                            start=(kb == 0), stop=(kb == nk - 1), skip_group_check=True),
                            reads=[PTb[pt], vb], writes=[psOb[po]], sig=(kb == nk - 1))
                        if kb == nk - 1:
                            osl = slice(sb * TT, (sb + 1) * TT)
                            if h % 2 == 0:
                                num, den, dst_p = slice(0, 64), slice(64, 128), slice(0, 64)
                            else:
                                num, den, dst_p = slice(64, 128), slice(0, 64), slice(64, 128)
                            cx.op("act", lambda: nc.scalar.activation(out=rdn[dst_p, po, :], in_=psO[den, po, :], func=AF.Ln),
                                  reads=[psOb[po]], writes=[rdnb[po]])
                            cx.op("act", lambda: nc.scalar.activation(out=rdn[dst_p, po, :], in_=rdn[dst_p, po, :], func=AF.Exp, scale=-1.0),
                                  reads=[rdnb[po]], writes=[rdnb[po]])
                            cx.op("dve", lambda: V.tensor_tensor(out=attT[dst_p, c, osl], in0=psO[num, po, :], in1=rdn[dst_p, po, :],
                                                                 op=ALU.mult),
                                  reads=[psOb[po], rdnb[po]], writes=attTb[4 * sb:4 * sb + 4])
                        yield

                def run_all(gen):
                    for _ in gen:
                        pass

                def chain(gens):
                    for gg in gens:
                        for _ in gg:
                            yield

                def interleave(main, n_main, side, n_side):
                    done = 0
                    k = 0
                    for _ in main:
                        k += 1
                        tgt = (n_side * k) // max(n_main, 1)
                        while done < tgt:
                            next(side, None)
                            done += 1

                for sb in range(4):
                    qbs = [4 * sb + j for j in range(4) if 4 * sb + j >= 2]
                    n_units = sum(((qb + 1) * 128 + TT - 1) // TT * 8 for qb in qbs)
                    ig = chain([indexer(qb, qb - 4 * sb) for qb in qbs])
                    bg = bisect(sb)
                    if sb == 0:
                        run_all(ig)
                        run_all(bg)
                    else:
                        ag = attention(sb - 1)
                        n_items = 8 * 4 * sb
                        interleave(ig, n_units, ag, n_items // 2)
                        interleave(bg, IT, ag, n_items - n_items // 2)
                        run_all(ag)
                    make_biasT(sb)
                run_all(attention(3))
                cx.barrier_all()

        if getattr(g, "stop", 9) == 2:
            return
        rnnT = ms.enter_context(nc.sbuf_tensor(f"rnnT{l}", [128, 8, S], BF16))
        rnnb = [Buf() for _ in range(8)]
        with ExitStack() as st:
            T0, T1, T2, T3 = [g.x_sb[:, 2 * i_:2 * i_ + 2, :] for i_ in range(4)]
            T0b, T1b, T2b, T3b = [[Buf(), Buf()] for _ in range(4)]
            xrb = st.enter_context(nc.sbuf_tensor(f"xrb{l}", [128, 2, S + 8], BF16)); xrbb = [Buf(), Buf()]
            xrb2 = st.enter_context(nc.sbuf_tensor(f"xrc{l}", [128, 2, S + 8], BF16)); xrb2b = [Buf(), Buf()]
            xcb = st.enter_context(nc.sbuf_tensor(f"xcb{l}", [128, 2, S], BF16)); xcbb = [Buf(), Buf()]
            wxr = st.enter_context(nc.sbuf_tensor(f"wxr{l}", [128, 2, 8, 128], BF16)); wxrb = [Buf(), Buf()]
            wgr = st.enter_context(nc.sbuf_tensor(f"wgr{l}", [128, 2, 8, 128], BF16)); wgrb = [Buf(), Buf()]
            dg = st.enter_context(nc.sbuf_tensor(f"dg{l}", [128, 2, 4, 128], BF16)); dgb = [Buf(), Buf()]
            bdA = st.enter_context(nc.sbuf_tensor(f"bdA{l}", [128, 8, 128], BF16)); bdAb = Buf()
            bdX = st.enter_context(nc.sbuf_tensor(f"bdX{l}", [128, 8, 128], BF16)); bdXb = Buf()
            nsp = st.enter_context(nc.sbuf_tensor(f"nsp{l}", [128, 3, 8], F32)); nspb = Buf()
            ps3 = st.enter_context(nc.psum_tensor(f"ps3{l}", [128, 2, TT], F32)); ps3b = [Buf(), Buf()]
            psc = st.enter_context(nc.psum_tensor(f"psc{l}", [128, 2, TT], F32)); pscb = [Buf(), Buf()]
            psa = st.enter_context(nc.psum_tensor(f"psa{l}", [128, 2, TT], F32)); psab = [Buf(), Buf()]
            psx = st.enter_context(nc.psum_tensor(f"psx{l}", [128, 2, TT], F32)); psxb = [Buf(), Buf()]
            V = nc.vector
            cx.op("dve", lambda: V.memset(bdA[:, :, :], 0.0), writes=[bdAb])
            cx.op("dve", lambda: V.memset(bdX[:, :, :], 0.0), writes=[bdXb])
            cx.op("dve", lambda: V.memset(xrb[:, :, 0:8], 0.0), writes=xrbb)
            cx.op("dve", lambda: V.memset(xrb2[:, :, 0:8], 0.0), writes=xrb2b)
            for half in range(2):
                hs = slice(half * 64, (half + 1) * 64)
                srcA = W["rg_wa"][l].rearrange("(c two) i j -> two i c j", two=2)[half]
                srcX = W["rg_wx"][l].rearrange("(c two) i j -> two i c j", two=2)[half]
                cx.dma("pool", bdA[hs, :, hs], srcA, writes=[bdAb])
                cx.dma("pool", bdX[hs, :, hs], srcX, writes=[bdXb])
            lam = g.prm_sb[:, pb + P_LAM:pb + P_LAM + 8]
            cx.op("act", lambda: nc.scalar.activation(out=nsp[:, 0, :], in_=lam, func=AF.Exp, scale=-1.0),
                  reads=[g.prmb], writes=[nspb])
            cx.op("act", lambda: nc.scalar.activation(out=nsp[:, 0, :], in_=nsp[:, 0, :], func=AF.Ln, scale=1.0, bias=g.one_c[:, 0:1]),
                  reads=[nspb, g.constb], writes=[nspb])
            cx.op("dve", lambda: V.tensor_scalar(out=nsp[:, 1, :], in0=nsp[:, 0, :], scalar1=-8.0, scalar2=None, op0=ALU.mult),
                  reads=[nspb], writes=[nspb])
            cx.op("dve", lambda: V.tensor_scalar(out=nsp[:, 2, :], in0=nsp[:, 0, :], scalar1=-16.0, scalar2=None, op0=ALU.mult),
                  reads=[nspb], writes=[nspb])

            def load3(c):
                s = c % 2
                cx.dma("pool", wxr[:, s, :, :], w_in_v[:, :, C_XR + c * 128:C_XR + (c + 1) * 128], writes=[wxrb[s]])
                cx.dma("pool", wgr[:, s, :, :], w_in_v[:, :, C_GR + c * 128:C_GR + (c + 1) * 128], writes=[wgrb[s]])
            k3 = [0]

            def stageA(c):
                s = c % 2
                pcol = lambda base: g.prm_sb[:, pb + base + c:pb + base + c + 1]
                for j in range(4):
                    cx.op("act", lambda j=j: nc.scalar.activation(
                        out=dg[:, s, j, :], in_=g.ident[:, :], func=AF.Identity,
                        scale=g.prm_sb[:, pb + P_CW + j * 8 + c:pb + P_CW + j * 8 + c + 1]),
                        reads=[g.constb, g.prmb], writes=[dgb[s]])
                for tt in range(NT):
                    tsl = slice(tt * TT, (tt + 1) * TT)
                    p = k3[0] % 2; k3[0] += 1
                    for kc in range(8):
                        cx.op("pe", lambda kc=kc: nc.tensor.matmul(
                            ps3[:, p, :], lhsT=wxr[:, s, kc, :], rhs=xn_sb[:, kc, tsl], start=(kc == 0), stop=(kc == 7)),
                            reads=[wxrb[s], xnb[kc][tt]], writes=[ps3b[p]], sig=(kc == 7))
                    cx.op("dve", lambda: V.tensor_copy(out=xrb[:, s, 4 + tt * TT:4 + (tt + 1) * TT], in_=ps3[:, p, :]),
                          reads=[ps3b[p]], writes=[xrbb[s]])
                    cx.op("dve", lambda: V.tensor_copy(out=xrb2[:, s, 5 + tt * TT:5 + (tt + 1) * TT], in_=ps3[:, p, :]),
                          reads=[ps3b[p]], writes=[xrb2b[s]])
                for tt in range(NT):
                    tsl = slice(tt * TT, (tt + 1) * TT)
                    p = tt % 2
                    for j in range(4):
                        if j % 2 == 1:
                            rhs_ap = xrb[:, s, tt * TT + 1 + j:tt * TT + 1 + j + TT]
                        else:
                            rhs_ap = xrb2[:, s, tt * TT + 2 + j:tt * TT + 2 + j + TT]
                        cx.op("pe", lambda j=j, rhs_ap=rhs_ap: nc.tensor.matmul(
                            psc[:, p, :], lhsT=dg[:, s, j, :], rhs=rhs_ap,
                            start=(j == 0), stop=(j == 3)), reads=[dgb[s], xrbb[s], xrb2b[s]], writes=[pscb[p]], sig=(j == 3))
                    cx.op("dve", lambda: V.tensor_scalar(out=T0[:, s, tsl], in0=psc[:, p, :], scalar1=pcol(P_CB),
                                                         scalar2=None, op0=ALU.add),
                          reads=[pscb[p], g.prmb], writes=[T0b[s]])
                    cx.op("dve", lambda: V.tensor_scalar(out=xcb[:, s, tsl], in0=psc[:, p, :], scalar1=pcol(P_CB),
                                                         scalar2=None, op0=ALU.add),
                          reads=[pscb[p], g.prmb], writes=[xcbb[s]])
                for tt in range(NT):
                    tsl = slice(tt * TT, (tt + 1) * TT)
                    p = tt % 2
                    cx.op("pe", lambda: nc.tensor.matmul(psa[:, p, :], lhsT=bdA[:, c, :], rhs=xcb[:, s, tsl], start=True, stop=True),
                          reads=[bdAb, xcbb[s]], writes=[psab[p]])
                    cx.op("act", lambda: nc.scalar.activation(out=T1[:, s, tsl], in_=psa[:, p, :], func=AF.Sigmoid,
                                                               bias=pcol(P_BA), scale=1.0),
                          reads=[psab[p], g.prmb], writes=[T1b[s]])
                    cx.op("pe", lambda: nc.tensor.matmul(psx[:, p, :], lhsT=bdX[:, c, :], rhs=xcb[:, s, tsl], start=True, stop=True),
                          reads=[bdXb, xcbb[s]], writes=[psxb[p]])
                    cx.op("act", lambda: nc.scalar.activation(out=T2[:, s, tsl], in_=psx[:, p, :], func=AF.Sigmoid,
                                                               bias=pcol(P_BX), scale=1.0),
                          reads=[psxb[p], g.prmb], writes=[T2b[s]])

            def stageB(c):
                s = c % 2
                A = nc.scalar.activation
                cx.op("act", lambda: A(out=T3[:, s, :], in_=T1[:, s, :], func=AF.Exp, scale=nsp[:, 1, c:c + 1]),
                      reads=[T1b[s], nspb], writes=[T3b[s]])
                cx.op("act", lambda: A(out=T1[:, s, :], in_=T1[:, s, :], func=AF.Exp, scale=nsp[:, 2, c:c + 1]),
                      reads=[T1b[s], nspb], writes=[T1b[s]])
                cx.op("act", lambda: A(out=T1[:, s, :], in_=T1[:, s, :], func=AF.Sqrt, scale=-1.0, bias=g.one_c[:, 0:1]),
                      reads=[T1b[s], g.constb], writes=[T1b[s]])
                cx.op("dve", lambda: V.tensor_tensor(out=T2[:, s, :], in0=T2[:, s, :], in1=T0[:, s, :], op=ALU.mult),
                      reads=[T2b[s], T0b[s]], writes=[T2b[s]])
                cx.op("dve", lambda: V.tensor_tensor(out=T2[:, s, :], in0=T2[:, s, :], in1=T1[:, s, :], op=ALU.mult),
                      reads=[T2b[s], T1b[s]], writes=[T2b[s]])
                cx.op("dve", lambda: V.tensor_tensor_scan(out=T0[:, s, :], data0=T3[:, s, :], data1=T2[:, s, :], initial=0.0,
                                                          op0=ALU.mult, op1=ALU.add),
                      reads=[T3b[s], T2b[s]], writes=[T0b[s]])
                for tt in range(NT):
                    tsl = slice(tt * TT, (tt + 1) * TT)
                    p = k3[0] % 2; k3[0] += 1
                    for kc in range(8):
                        cx.op("pe", lambda kc=kc: nc.tensor.matmul(
                            ps3[:, p, :], lhsT=wgr[:, s, kc, :], rhs=xn_sb[:, kc, tsl], start=(kc == 0), stop=(kc == 7)),
                            reads=[wgrb[s], xnb[kc][tt]], writes=[ps3b[p]], sig=(kc == 7))
                    cx.op("act", lambda: A(out=T1[:, s, tsl], in_=ps3[:, p, :], func=AF.Gelu_apprx_tanh),
                          reads=[ps3b[p]], writes=[T1b[s]])
                cx.op("dve", lambda: V.tensor_tensor(out=rnnT[:, c, :], in0=T0[:, s, :], in1=T1[:, s, :], op=ALU.mult),
                      reads=[T0b[s], T1b[s]], writes=[rnnb[c]])

            load3(0)
            load3(1)
            stageA(0)
            for c in range(8):
                if c + 1 < 8:
                    stageA(c + 1)
                stageB(c)
                if c + 2 < 8:
                    load3(c + 2)
            cx.barrier_all()

        if getattr(g, "stop", 9) == 3:
            return
        with ExitStack() as st:
            merged = st.enter_context(nc.sbuf_tensor(f"merged{l}", [128, 8, S], BF16))
            mgb = _grid(8, NT)
            for c in range(8):
                cx.dma("sp" if c % 2 == 0 else "act", x_sb[:, c, :], xres_v[:, c, :],
                       reads=[g.xresb[c][tt] for tt in range(NT)], writes=[xb[c][tt] for tt in range(NT)])
            wga = st.enter_context(nc.sbuf_tensor(f"wga{l}", [128, 2, 8, 128], BF16)); wgab = [Buf(), Buf()]
            wgl = st.enter_context(nc.sbuf_tensor(f"wgl{l}", [128, 2, 8, 128], BF16)); wglb = [Buf(), Buf()]
            wpp = st.enter_context(nc.sbuf_tensor(f"wpp{l}", [128, 2, 4, 128], BF16)); wppb = [Buf(), Buf()]
            wrr = st.enter_context(nc.sbuf_tensor(f"wrr{l}", [128, 2, 8, 128], BF16)); wrrb = [Buf(), Buf()]
            sga = st.enter_context(nc.sbuf_tensor(f"sga{l}", [128, 2, TT], F32)); sgab = [Buf(), Buf()]
            sgl = st.enter_context(nc.sbuf_tensor(f"sgl{l}", [128, 2, TT], F32)); sglb = [Buf(), Buf()]
            wp_v = W["w_att_proj"][l].rearrange("(kc p) n -> p kc n", p=128)
            wr_v = W["w_rnn_proj"][l].rearrange("(kc p) n -> p kc n", p=128)
            wo_v = W["w_out"][l].rearrange("(kc p) n -> p kc n", p=128)
            with ExitStack() as s4:
                pga = s4.enter_context(nc.psum_tensor(f"pga{l}", [128, 2, TT], F32)); pgab = [Buf(), Buf()]
                pgl = s4.enter_context(nc.psum_tensor(f"pgl{l}", [128, 2, TT], F32)); pglb = [Buf(), Buf()]
                ppp = s4.enter_context(nc.psum_tensor(f"ppp{l}", [128, 2, TT], F32)); pppb = [Buf(), Buf()]
                prr = s4.enter_context(nc.psum_tensor(f"prr{l}", [128, 2, TT], F32)); prrb = [Buf(), Buf()]

                def load4(c):
                    s = c % 2
                    csl = slice(c * 128, (c + 1) * 128)
                    cx.dma("pool", wga[:, s, :, :], w_in_v[:, :, C_GA + c * 128:C_GA + (c + 1) * 128], writes=[wgab[s]])
                    cx.dma("pool", wgl[:, s, :, :], w_in_v[:, :, C_GL + c * 128:C_GL + (c + 1) * 128], writes=[wglb[s]])
                    cx.dma("pool", wpp[:, s, :, :], wp_v[:, :, csl], writes=[wppb[s]])
                    cx.dma("pool", wrr[:, s, :, :], wr_v[:, :, csl], writes=[wrrb[s]])
                load4(0)
                k4 = 0
                for c in range(8):
                    s = c % 2
                    if c + 1 < 8:
                        load4(c + 1)
                    for tt in range(NT):
                        tsl = slice(tt * TT, (tt + 1) * TT)
                        p = k4 % 2; k4 += 1
                        for kc in range(8):
                            cx.op("pe", lambda kc=kc: nc.tensor.matmul(
                                pga[:, p, :], lhsT=wga[:, s, kc, :], rhs=xn_sb[:, kc, tsl], start=(kc == 0), stop=(kc == 7)),
                                reads=[wgab[s], xnb[kc][tt]], writes=[pgab[p]], sig=(kc == 7))
                        cx.op("act", lambda: nc.scalar.activation(out=sga[:, p, :], in_=pga[:, p, :], func=AF.Sigmoid),
                              reads=[pgab[p]], writes=[sgab[p]])
                        for kc in range(4):
                            cx.op("pe", lambda kc=kc: nc.tensor.matmul(
                                ppp[:, p, :], lhsT=wpp[:, s, kc, :], rhs=attT[:, kc, tsl], start=(kc == 0), stop=(kc == 3)),
                                reads=[wppb[s]] + attTb[tt * 4:(tt + 1) * 4], writes=[pppb[p]], sig=(kc == 3))
                        cx.op("dve", lambda: nc.vector.tensor_tensor(out=sga[:, p, :], in0=ppp[:, p, :], in1=sga[:, p, :], op=ALU.mult),
                              reads=[pppb[p], sgab[p]], writes=[sgab[p]])
                        for kc in range(8):
                            cx.op("pe", lambda kc=kc: nc.tensor.matmul(
                                pgl[:, p, :], lhsT=wgl[:, s, kc, :], rhs=xn_sb[:, kc, tsl], start=(kc == 0), stop=(kc == 7)),
                                reads=[wglb[s], xnb[kc][tt]], writes=[pglb[p]], sig=(kc == 7))
                        cx.op("act", lambda: nc.scalar.activation(out=sgl[:, p, :], in_=pgl[:, p, :], func=AF.Sigmoid),
                              reads=[pglb[p]], writes=[sglb[p]])
                        for kc in range(8):
                            cx.op("pe", lambda kc=kc: nc.tensor.matmul(
                                prr[:, p, :], lhsT=wrr[:, s, kc, :], rhs=rnnT[:, kc, tsl], start=(kc == 0), stop=(kc == 7)),
                                reads=[wrrb[s]] + rnnb, writes=[prrb[p]], sig=(kc == 7))
                        cx.op("dve", lambda: nc.vector.tensor_tensor(out=sgl[:, p, :], in0=prr[:, p, :], in1=sgl[:, p, :], op=ALU.mult),
                              reads=[prrb[p], sglb[p]], writes=[sglb[p]])
                        cx.op("dve", lambda: nc.vector.tensor_tensor(out=merged[:, c, tsl], in0=sga[:, p, :], in1=sgl[:, p, :], op=ALU.add),
                              reads=[sgab[p], sglb[p]], writes=[mgb[c][tt]])
                cx.barrier_all()
            with ExitStack() as s5:
                wo = s5.enter_context(nc.sbuf_tensor(f"wo{l}", [128, 2, 8, 128], BF16)); wob = [Buf(), Buf()]
                pso = s5.enter_context(nc.psum_tensor(f"pso{l}", [128, 2, TT], F32)); psob = [Buf(), Buf()]

                def load5(c):
                    s = c % 2
                    cx.dma("pool", wo[:, s, :, :], wo_v[:, :, c * 128:(c + 1) * 128], writes=[wob[s]])
                load5(0)
                tiles5 = [(c, tt) for c in range(8) for tt in range(NT)]

                for k, (c, tt) in enumerate(tiles5):
                    s = c % 2
                    if tt == 0 and c + 1 < 8:
                        load5(c + 1)
                    tsl = slice(tt * TT, (tt + 1) * TT)
                    p = k % 2
                    for kc in range(8):
                        cx.op("pe", lambda kc=kc: nc.tensor.matmul(
                            pso[:, p, :], lhsT=wo[:, s, kc, :], rhs=merged[:, kc, tsl], start=(kc == 0), stop=(kc == 7)),
                            reads=[wob[s], mgb[kc][tt]], writes=[psob[p]], sig=(kc == 7))
                    cx.op("dve", lambda: nc.vector.tensor_tensor(out=x_sb[:, c, tsl], in0=pso[:, p, :], in1=x_sb[:, c, tsl], op=ALU.add),
                          reads=[psob[p], xb[c][tt]], writes=[xb[c][tt]])
                cx.barrier_all()


WNAMES = ["ffn1_wg", "ffn1_wu", "ffn1_wd", "w_in", "rg_wa", "rg_wx", "w_att_proj", "w_rnn_proj", "w_out",
          "ffn2_wg", "ffn2_wu", "ffn2_wd"]
WSHAPES = {"ffn1_wg": [DEPTH, D, DFF], "ffn1_wu": [DEPTH, D, DFF], "ffn1_wd": [DEPTH, DFF, D],
           "w_in": [DEPTH, D, N_IN], "rg_wa": [DEPTH, 16, 64, 64], "rg_wx": [DEPTH, 16, 64, 64],
           "w_att_proj": [DEPTH, 512, D], "w_rnn_proj": [DEPTH, D, D], "w_out": [DEPTH, D, D],
           "ffn2_wg": [DEPTH, D, DFF], "ffn2_wu": [DEPTH, D, DFF], "ffn2_wd": [DEPTH, DFF, D]}


def build_program(plan=None, stop=9):
    if plan is None:
        plan = ["ffn1_0", "mix_0", "ffn2_0+ffn1_1", "mix_1", "ffn2_1+final"]
    nc = bass.Bass("TRN2", target_bir_lowering=False)
    xT = nc.dram_tensor("xT", [D, S], F32, kind="ExternalInput").ap()
    prm = nc.dram_tensor("prm", [128, NP], F32, kind="ExternalInput").ap()
    W = {n: nc.dram_tensor(n, WSHAPES[n], F32, kind="ExternalInput").ap() for n in WNAMES}
    yT = nc.dram_tensor("yT", [D, S], F32, kind="ExternalOutput").ap()
    xres = nc.dram_tensor("xres", [D, S], F32, kind="Internal").ap()
    g = G()
    g.nc = nc
    g.stop = stop
    g.xres = xres
    g.xresb = _grid(8, NT)
    xT_v = xT.rearrange("(c p) t -> p c t", p=128)
    yT_v = yT.rearrange("(c p) t -> p c t", p=128)
    xres_v = xres.rearrange("(c p) t -> p c t", p=128)
    with ExitStack() as top:
        cx = Ctx(nc, top)
        g.cx = cx
        g.prm_sb = top.enter_context(nc.sbuf_tensor("prm_sb", [128, NP], F32)); g.prmb = Buf()
        g.ones_bf = top.enter_context(nc.sbuf_tensor("ones_bf", [128, 128], BF16))
        g.ident = top.enter_context(nc.sbuf_tensor("ident", [128, 128], BF16))
        g.negm = top.enter_context(nc.sbuf_tensor("negm", [128, 128], F32))
        g.negb = top.enter_context(nc.sbuf_tensor("negb", [128, 128], BF16))
        g.zer = top.enter_context(nc.sbuf_tensor("zer", [128, 128], F32))
        g.pw3 = top.enter_context(nc.sbuf_tensor("pw3", [128, 32, 4], F32))
        g.eps_c = top.enter_context(nc.sbuf_tensor("eps_c", [128, 1], F32))
        g.one_c = top.enter_context(nc.sbuf_tensor("one_c", [128, 1], F32))
        g.constb = Buf()
        P = nc.gpsimd
        cx.dma("sp", g.prm_sb[:, :], prm[:, :], writes=[g.prmb])
        cx.op("pool", lambda: P.memset(g.ones_bf[:, :], 1.0), writes=[g.constb])
        cx.op("pool", lambda: P.memset(g.zer[:, :], 0.0), writes=[g.constb])
        cx.op("pool", lambda: P.memset(g.eps_c[:, :], EPS), writes=[g.constb])
        cx.op("pool", lambda: P.memset(g.one_c[:, :], 1.0), writes=[g.constb])
        for i in range(IT):
            cx.op("pool", lambda i=i: P.memset(g.pw3[:, i, :], 2.0 ** -(i + 1)), writes=[g.constb])
        cx.op("pool", lambda: P.affine_select(out=g.ident[:, :], in_=g.ones_bf[:, :], pattern=[[1, 128]],
                                              compare_op=ALU.is_equal, fill=0.0, base=0, channel_multiplier=-1),
              reads=[g.constb], writes=[g.constb])
        cx.op("pool", lambda: P.affine_select(out=g.negm[:, :], in_=g.zer[:, :], pattern=[[-1, 128]],
                                              compare_op=ALU.is_ge, fill=-1e30, base=0, channel_multiplier=1),
              reads=[g.constb], writes=[g.constb])
        cx.op("pool", lambda: P.affine_select(out=g.negb[:, :], in_=g.zer[:, :], pattern=[[-1, 128]],
                                              compare_op=ALU.is_ge, fill=NEG, base=0, channel_multiplier=1),
              reads=[g.constb], writes=[g.constb])
        cx.barrier_all()

        def ffn_phase(src_v, steps, final, next_mix_l=None):
            with ExitStack() as st:
                g.uid = getattr(g, "uid", 0) + 1
                x_sb, xb, xn_sb, xnb = g.x_sb, g.xb, g.xn_sb, g.xnb
                fb = FfnBufs(g, st, f"p{g.uid}")
                nm = Normer(g, st, f"p{g.uid}")
                if src_v is not None:
                    for tt in range(NT):
                        tsl = slice(tt * TT, (tt + 1) * TT)
                        cx.dma("sp" if tt % 2 == 0 else "act", x_sb[:, :, tsl], src_v[:, :, tsl],
                               writes=[xb[c][tt] for c in range(8)])

                def gcol_of(l, which):
                    return l * PL + (P_F1 if which == 1 else P_F2)
                g0 = gcol_of(*steps[0])
                for tt in range(NT):
                    nm.tile(x_sb, xb, tt, norm_to_bf16(g, x_sb, xb, xn_sb, xnb, g0))

                def fin(tt, c, rs_ap, rs_buf):
                    tsl = slice(tt * TT, (tt + 1) * TT)
                    cx.op("dve", lambda: nc.vector.scalar_tensor_tensor(
                        out=x_sb[:, c, tsl], in0=x_sb[:, c, tsl], scalar=g.prm_sb[:, P_FIN + c:P_FIN + c + 1],
                        in1=rs_ap, op0=ALU.mult, op1=ALU.mult),
                        reads=[xb[c][tt], rs_buf, g.prmb], writes=[xb[c][tt]])
                    if c == 7:
                        cx.dma("sp" if tt % 2 == 0 else "act", yT_v[:, :, tsl], x_sb[:, :, tsl],
                               reads=[xb[cc][tt] for cc in range(8)])

                for si, (l, which) in enumerate(steps):
                    last = (si == len(steps) - 1)
                    if not last:
                        gn = gcol_of(*steps[si + 1])
                        hook = lambda tt, gn=gn: nm.tile(x_sb, xb, tt, norm_to_bf16(g, x_sb, xb, xn_sb, xnb, gn))
                    elif final:
                        hook = lambda tt: nm.tile(x_sb, xb, tt, fin)
                    else:
                        def hook(tt):
                            tsl = slice(tt * TT, (tt + 1) * TT)
                            cx.dma("sp" if tt % 2 == 0 else "act", xres_v[:, :, tsl], x_sb[:, :, tsl],
                                   reads=[xb[c][tt] for c in range(8)], writes=[g.xresb[c][tt] for c in range(8)])
                            if next_mix_l is not None:
                                gm = next_mix_l * PL + P_MX
                                nm.tile(x_sb, xb, tt, norm_to_bf16(g, x_sb, xb, xn_sb, xnb, gm))
                    emit_ffn(g, fb, x_sb, xb, xn_sb, xnb, W[f"ffn{which}_wg"][l], W[f"ffn{which}_wu"][l],
                             W[f"ffn{which}_wd"][l], after_tile=hook)
                cx.barrier_all()

        g.xn_sb = top.enter_context(nc.sbuf_tensor("xn_sb", [128, 8, S], BF16))
        g.xnb = _grid(8, NT)
        g.x_sb = top.enter_context(nc.sbuf_tensor("x_sb", [128, 8, S], F32))
        g.xb = _grid(8, NT)
        first = True
        have_xn = False
        for pi, ph in enumerate(plan):
            nxt = plan[pi + 1] if pi + 1 < len(plan) else None
            if ph.startswith("mix"):
                if first:
                    allx = [b for r in g.xb for b in r]
                    cx.dma("sp", g.x_sb[:, :, :], xT_v, writes=allx)
                    cx.dma("sp", xres_v, g.x_sb[:, :, :], reads=allx, writes=[b for r in g.xresb for b in r])
                    cx.barrier_all()
                emit_mixer(g, int(ph.split("_")[1]), W, have_xn=have_xn)
                have_xn = False
            else:
                steps = []
                final = False
                for part in ph.split("+"):
                    if part == "final":
                        final = True
                    else:
                        steps.append((int(part.split("_")[1]), 1 if part.startswith("ffn1") else 2))
                nml = int(nxt.split("_")[1]) if (nxt is not None and nxt.startswith("mix")) else None
                ffn_phase(xT_v if first else None, steps, final, next_mix_l=nml)
                have_xn = nml is not None
            first = False
        if not plan[-1].endswith("final"):
            cx.dma("sp", yT_v, g.x_sb[:, :, :], reads=[b for r in g.xb for b in r])
        cx.barrier_all()
        g.stats = (cx.n_ins, cx.n_wait)
    return nc


def pack_prm(inp):
    prm = np.zeros((128, NP), np.float32)

    def put(col, v):
        prm[:, col:col + 8] = np.asarray(v, np.float32).reshape(8, 128).T
    for l in range(DEPTH):
        b = l * PL
        put(b + P_F1, inp["ffn1_norm"][l]); put(b + P_MX, inp["mix_norm"][l]); put(b + P_F2, inp["ffn2_norm"][l])
        for j in range(4):
            put(b + P_CW + j * 8, inp["conv_w"][l][j])
        put(b + P_CB, inp["conv_b"][l]); put(b + P_BA, inp["rg_ba"][l]); put(b + P_BX, inp["rg_bx"][l])
        put(b + P_LAM, inp["rg_lam"][l])
    put(P_FIN, inp["final_norm"])
    return prm


def kernel(**inputs):
    inp = {k: np.asarray(v) for k, v in inputs.items()}
    x = inp["x"].astype(np.float32, copy=False)
    B = x.shape[0]
    nc = build_program()
    prm = pack_prm(inp)
    wmap = {n: np.ascontiguousarray(inp[n], dtype=np.float32) for n in WNAMES}
    in_maps = []
    for b in range(B):
        m = {"xT": np.ascontiguousarray(x[b].T), "prm": prm}
        m.update(wmap)
        in_maps.append(m)
    res = run_bass_kernel_spmd(nc, in_maps, core_ids=list(range(B)))
    out = np.stack([np.ascontiguousarray(res.results[b]["yT"].T) for b in range(B)], axis=0)
    return out.astype(np.float32, copy=False)
```

```python
import numpy as np
from contextlib import ExitStack
import concourse.bass as bass
import concourse.mybir as mybir
from concourse.bass_utils import run_bass_kernel_spmd

F32 = mybir.dt.float32
BF16 = mybir.dt.bfloat16
AF = mybir.ActivationFunctionType
ALU = mybir.AluOpType
AX = mybir.AxisListType

S = 2048
D = 1024
DFF = 2816
NT = 4
TT = 512
EPS = 1e-6
DEPTH = 2
N_IN = 5032
IT = 12
TOPK = 256
NEG = -30000.0
PL = 88
P_F1, P_MX, P_F2, P_CW, P_CB, P_BA, P_BX, P_LAM = 0, 8, 16, 24, 56, 64, 72, 80
P_FIN = DEPTH * PL
NP = P_FIN + 8
C_Q, C_K, C_V, C_QI, C_KI, C_WI, C_XR, C_GR, C_GA, C_GL = 0, 512, 576, 640, 896, 928, 936, 1960, 2984, 4008

SEM_ROLL = 30000
N_DMA_SEMS = 24


class Buf:
    __slots__ = ("name", "last_w", "readers")

    def __init__(self, name=""):
        self.name = name
        self.last_w = None
        self.readers = {}


class Ctx:
    def __init__(self, nc, stack):
        self.nc = nc
        self.stack = stack
        self.eng = {"pe": nc.tensor, "act": nc.scalar, "dve": nc.vector,
                    "pool": nc.gpsimd, "sp": nc.sync}
        self.sems = {}
        self.cur = {}
        self.cnt = {}
        self.gen = {}
        self.known = {e: {} for e in self.eng}
        for e in self.eng:
            self.gen[e] = 0
            self._new_sem(e)
        self.dma_sems = []
        for i in range(N_DMA_SEMS):
            key = f"dma{i}"
            self.sems[key] = stack.enter_context(nc.semaphore(key))
            self.dma_sems.append([key, 0])
        self.dma_rr = 0
        self.dma_rr2 = 0
        self.n_wait = 0
        self.n_ins = 0

    def _new_sem(self, e):
        key = f"{e}_{self.gen[e]}"
        self.gen[e] += 1
        self.sems[key] = self.stack.enter_context(self.nc.semaphore(key))
        self.cur[e] = key
        self.cnt[e] = 0

    def _wait(self, e, ev):
        key, val, src = ev
        if self.known[e].get(key, 0) >= val:
            return
        if not key.startswith("dma") and key == self.cur[src]:
            assert self.cnt[src] >= val, f"wait on future signal {key} {val} > {self.cnt[src]}"
        self.eng[e].wait_ge(self.sems[key], val)
        self.known[e][key] = val
        self.n_wait += 1

    def _deps(self, e, reads, writes):
        evs = []
        for b in reads:
            if b.last_w is not None:
                evs.append(b.last_w)
        for b in writes:
            if b.last_w is not None and b.last_w[2] != e:
                evs.append(b.last_w)
            for r, ev in b.readers.items():
                if r != e:
                    evs.append(ev)
        for ev in evs:
            self._wait(e, ev)

    def op(self, e, fn, reads=(), writes=(), sig=True):
        self._deps(e, reads, writes)
        ins = fn()
        self.n_ins += 1
        if sig:
            if self.cnt[e] >= SEM_ROLL:
                self._new_sem(e)
            self.cnt[e] += 1
            ins.then_inc(self.sems[self.cur[e]], 1)
            ev = (self.cur[e], self.cnt[e], e)
        else:
            ev = (self.cur[e], self.cnt[e] + 1, e)
        for b in reads:
            b.readers[e] = ev
        for b in writes:
            b.last_w = ev
            b.readers = {}
        return ins

    def dma(self, q, out, in_, reads=(), writes=(), **kw):
        self._deps(q, reads, writes)
        half = len(self.dma_sems) // 2
        if q == "pool":
            slot = self.dma_sems[self.dma_rr % half]
            self.dma_rr += 1
        else:
            slot = self.dma_sems[half + self.dma_rr2 % half]
            self.dma_rr2 += 1
        key, val = slot
        if val > 0:
            self._wait(q, (key, val, "dma"))
        ins = self.eng[q].dma_start(out=out, in_=in_, **kw)
        slot[1] = val + 16
        ins.then_inc(self.sems[key], 16)
        ev = (key, val + 16, "dma:" + q)
        for b in reads:
            b.readers["dma:" + q + key] = ev
        for b in writes:
            b.last_w = ev
            b.readers = {}
        self.n_ins += 1
        return ev

    def barrier_all(self):
        evs = [(self.cur[e], self.cnt[e], e) for e in self.eng if self.cnt[e] > 0]
        evs += [(key, val, "dma") for key, val in self.dma_sems if val > 0]
        for e in self.eng:
            for ev in evs:
                if ev[2] != e:
                    self._wait(e, ev)


class G:
    pass


def _grid(a, b):
    return [[Buf() for _ in range(b)] for _ in range(a)]


class Normer:
    def __init__(self, g, st, tag):
        nc = g.nc
        self.g = g
        self.sq = st.enter_context(nc.sbuf_tensor(f"sq{tag}", [128, 2, 8, TT], BF16))
        self.sqb = [Buf(), Buf()]
        self.rs = st.enter_context(nc.sbuf_tensor(f"rs{tag}", [128, 2, TT], F32))
        self.rsb = [Buf(), Buf()]
        self.ps = st.enter_context(nc.psum_tensor(f"psn{tag}", [128, 2, TT], F32))
        self.psb = [Buf(), Buf()]
        self.k = 0

    def tile(self, x_sb, xb, tt, out_fn):
        g = self.g
        cx, nc = g.cx, g.nc
        sq, rs, ps = self.sq, self.rs, self.ps
        s = self.k % 2
        self.k += 1
        tsl = slice(tt * TT, (tt + 1) * TT)
        for c in range(8):
            cx.op("act", lambda c=c: nc.scalar.activation(out=sq[:, s, c, :], in_=x_sb[:, c, tsl], func=AF.Square),
                  reads=[xb[c][tt]], writes=[self.sqb[s]], sig=(c == 7))
        for c in range(8):
            cx.op("pe", lambda c=c: nc.tensor.matmul(ps[:, s, :], lhsT=g.ones_bf[:, :], rhs=sq[:, s, c, :],
                                                      start=(c == 0), stop=(c == 7)),
                  reads=[self.sqb[s], g.constb], writes=[self.psb[s]], sig=(c == 7))
        cx.op("act", lambda: nc.scalar.activation(out=rs[:, s, :], in_=ps[:, s, :], func=AF.Sqrt,
                                                   scale=1.0 / D, bias=g.eps_c[:, 0:1]),
              reads=[self.psb[s], g.constb], writes=[self.rsb[s]])
        cx.op("dve", lambda: nc.vector.reciprocal(out=rs[:, s, :], in_=rs[:, s, :]),
              reads=[self.rsb[s]], writes=[self.rsb[s]])
        for c in range(8):
            out_fn(tt, c, rs[:, s, :], self.rsb[s])


def emit_norm(g, st, x_sb, xb, gcol, out_fn, tag):
    nm = Normer(g, st, tag)
    for tt in range(NT):
        nm.tile(x_sb, xb, tt, out_fn)


def norm_to_bf16(g, x_sb, xb, xn_sb, xnb, gcol):
    cx, nc = g.cx, g.nc

    def fn(tt, c, rs_ap, rs_buf):
        tsl = slice(tt * TT, (tt + 1) * TT)
        cx.op("dve", lambda: nc.vector.scalar_tensor_tensor(
            out=xn_sb[:, c, tsl], in0=x_sb[:, c, tsl], scalar=g.prm_sb[:, gcol + c:gcol + c + 1], in1=rs_ap,
            op0=ALU.mult, op1=ALU.mult),
            reads=[xb[c][tt], rs_buf, g.prmb], writes=[xnb[c][tt]])
    return fn


GROUPS = [(0, 6), (6, 12), (12, 17), (17, 22)]


class FfnBufs:
    def __init__(self, g, st, tag):
        nc = g.nc
        GM = 6
        self.wg_s = st.enter_context(nc.sbuf_tensor(f"wg_s{tag}", [128, 2, 8, GM * 128], BF16))
        self.wu_s = st.enter_context(nc.sbuf_tensor(f"wu_s{tag}", [128, 2, 8, GM * 128], BF16))
        self.wd_s = st.enter_context(nc.sbuf_tensor(f"wd_s{tag}", [128, 2, GM, D], BF16))
        self.wgb = [Buf(), Buf()]; self.wub = [Buf(), Buf()]; self.wdb = [Buf(), Buf()]
        self.a_s = st.enter_context(nc.sbuf_tensor(f"a_s{tag}", [128, 2, GM, TT], BF16))
        self.ab = [Buf(), Buf()]
        self.sg_s = st.enter_context(nc.sbuf_tensor(f"sg_s{tag}", [128, 2, TT], F32))
        self.sgb = [Buf(), Buf()]
        self.psg = st.enter_context(nc.psum_tensor(f"psg{tag}", [128, 2, TT], F32))
        self.psu = st.enter_context(nc.psum_tensor(f"psu{tag}", [128, 2, TT], F32))
        self.psy = st.enter_context(nc.psum_tensor(f"psy{tag}", [128, 2, TT], F32))
        self.psgb = [Buf(), Buf()]; self.psub = [Buf(), Buf()]; self.psyb = [Buf(), Buf()]
        self.gi = 0
        self.jj = 0
        self.yy = 0
        self.aa = 0


def emit_ffn(g, fb, x_sb, xb, xn_sb, xnb, wg, wu, wd, after_tile=None):
    cx, nc = g.cx, g.nc
    wg_s, wu_s, wd_s, a_s, sg_s, psg, psu, psy = fb.wg_s, fb.wu_s, fb.wd_s, fb.a_s, fb.sg_s, fb.psg, fb.psu, fb.psy
    wgb, wub, wdb, ab, sgb, psgb, psub, psyb = fb.wgb, fb.wub, fb.wdb, fb.ab, fb.sgb, fb.psgb, fb.psub, fb.psyb
    wgv = wg.rearrange("(kc p) n -> p kc n", p=128)
    wuv = wu.rearrange("(kc p) n -> p kc n", p=128)
    wdv = wd.rearrange("(j p) d -> p j d", p=128)
    slots = {}

    def load(i):
        c0, c1 = GROUPS[i]
        G_ = c1 - c0
        s = fb.gi % 2
        fb.gi += 1
        slots[i] = s
        cx.dma("pool", wg_s[:, s, :, 0:G_ * 128], wgv[:, :, c0 * 128:c1 * 128], writes=[wgb[s]])
        cx.dma("pool", wu_s[:, s, :, 0:G_ * 128], wuv[:, :, c0 * 128:c1 * 128], writes=[wub[s]])
        cx.dma("pool", wd_s[:, s, 0:G_, :], wdv[:, c0:c1, :], writes=[wdb[s]])

    load(0)
    for gi, (c0, c1) in enumerate(GROUPS):
        G_ = c1 - c0
        if gi + 1 < len(GROUPS):
            load(gi + 1)
        s = slots[gi]
        for tt in range(NT):
            tsl = slice(tt * TT, (tt + 1) * TT)
            sa = fb.aa % 2
            fb.aa += 1
            for j in range(G_):
                p = fb.jj % 2
                fb.jj += 1
                for kc in range(8):
                    cx.op("pe", lambda kc=kc: nc.tensor.matmul(
                        psg[:, p, :], lhsT=wg_s[:, s, kc, j * 128:(j + 1) * 128], rhs=xn_sb[:, kc, tsl],
                        start=(kc == 0), stop=(kc == 7)),
                        reads=[wgb[s], xnb[kc][tt]], writes=[psgb[p]], sig=(kc == 7))
                for kc in range(8):
                    cx.op("pe", lambda kc=kc: nc.tensor.matmul(
                        psu[:, p, :], lhsT=wu_s[:, s, kc, j * 128:(j + 1) * 128], rhs=xn_sb[:, kc, tsl],
                        start=(kc == 0), stop=(kc == 7)),
                        reads=[wub[s], xnb[kc][tt]], writes=[psub[p]], sig=(kc == 7))
                cx.op("act", lambda: nc.scalar.activation(out=sg_s[:, p, :], in_=psg[:, p, :], func=AF.Silu),
                      reads=[psgb[p]], writes=[sgb[p]])
                cx.op("dve", lambda: nc.vector.tensor_tensor(out=a_s[:, sa, j, :], in0=psu[:, p, :],
                                                             in1=sg_s[:, p, :], op=ALU.mult),
                      reads=[psub[p], sgb[p]], writes=[ab[sa]])
            for dc in range(8):
                p = fb.yy % 2
                fb.yy += 1
                for j in range(G_):
                    cx.op("pe", lambda j=j: nc.tensor.matmul(
                        psy[:, p, :], lhsT=wd_s[:, s, j, dc * 128:(dc + 1) * 128], rhs=a_s[:, sa, j, :],
                        start=(j == 0), stop=(j == G_ - 1)),
                        reads=[wdb[s], ab[sa]], writes=[psyb[p]], sig=(j == G_ - 1))
                cx.op("dve", lambda: nc.vector.scalar_tensor_tensor(
                    out=x_sb[:, dc, tsl], in0=psy[:, p, :], scalar=0.5, in1=x_sb[:, dc, tsl],
                    op0=ALU.mult, op1=ALU.add),
                    reads=[psyb[p], xb[dc][tt]], writes=[xb[dc][tt]])
            if after_tile is not None and gi == len(GROUPS) - 1:
                after_tile(tt)


def emit_mixer(g, l, W, have_xn=False):
    cx, nc = g.cx, g.nc
    pb = l * PL
    w_in_v = W["w_in"][l].rearrange("(kc p) n -> p kc n", p=128)
    xres_v = g.xres.rearrange("(c p) t -> p c t", p=128)
    with ExitStack() as ms:
        xn_sb, xnb = g.xn_sb, g.xnb
        x_sb, xb = g.x_sb, g.xb
        if not have_xn:
            with ExitStack() as st:
                for tt in range(NT):
                    cx.dma("sp", x_sb[:, :, tt * TT:(tt + 1) * TT], xres_v[:, :, tt * TT:(tt + 1) * TT],
                           reads=[g.xresb[c][tt] for c in range(8)], writes=[xb[c][tt] for c in range(8)])
                emit_norm(g, st, x_sb, xb, pb + P_MX, norm_to_bf16(g, x_sb, xb, xn_sb, xnb, pb + P_MX), f"m{l}")
        cx.barrier_all()
        if getattr(g, "stop", 9) == 0:
            return
        attT = ms.enter_context(nc.sbuf_tensor(f"attT{l}", [128, 4, S], BF16))
        attTb = [Buf() for _ in range(16)]
        with ExitStack() as st:
            qT = st.enter_context(nc.sbuf_tensor(f"qT{l}", [128, 4, S], BF16)); qTb = Buf()
            kT2 = st.enter_context(nc.sbuf_tensor(f"kT2{l}", [128, 2, S], BF16)); kT2b = Buf()
            qiT3 = st.enter_context(nc.sbuf_tensor(f"qiT3{l}", [128, 3, S], BF16)); qiT3b = Buf()
            kiT3 = st.enter_context(nc.sbuf_tensor(f"kiT3{l}", [128, 3, S], BF16)); kiT3b = Buf()
            vaug2 = st.enter_context(nc.sbuf_tensor(f"vaug{l}", [128, 16, 2, 128], BF16)); vb = Buf()
            wi_sb = st.enter_context(nc.sbuf_tensor(f"wi{l}", [128, 16, 8], F32)); wib = Buf()
            wabs = st.enter_context(nc.sbuf_tensor(f"wabs{l}", [128, 16, 8], F32)); wabsb = Buf()
            wsgn = st.enter_context(nc.sbuf_tensor(f"wsgn{l}", [128, 16, 8], F32)); wsgnb = Buf()
            with ExitStack() as s1:
                wt = s1.enter_context(nc.sbuf_tensor(f"m1w{l}", [128, 3, 8, 128], BF16))
                wtb = [Buf(), Buf(), Buf()]
                ps1 = s1.enter_context(nc.psum_tensor(f"m1ps{l}", [128, 2, TT], F32))
                ps1b = [Buf(), Buf()]
                psv = s1.enter_context(nc.psum_tensor(f"m1pv{l}", [128, 2, TT], F32))
                psvb = [Buf(), Buf()]
                chunks = []
                for c in range(4):
                    chunks.append(([(C_Q + c * 128, 128, 0)], 128, qT[:, c, :], qTb))
                chunks.append(([(C_K, 64, 0), (C_K, 64, 64)], 128, "K", kT2b))
                cx.op("dve", lambda: nc.vector.memset(kT2[:, :, :], 0.0), writes=[kT2b])
                chunks.append(([(C_QI, 96, 0)], 96, qiT3[0:96, 0, :], qiT3b))
                chunks.append(([(C_QI + 96, 96, 0)], 96, qiT3[0:96, 1, :], qiT3b))
                chunks.append(([(C_QI + 192, 64, 0)], 64, qiT3[0:64, 2, :], qiT3b))
                chunks.append(([(C_KI, 32, 0), (C_KI, 32, 32), (C_KI, 32, 64)], 96, "KI", kiT3b))
                cx.op("dve", lambda: nc.vector.memset(kiT3[:, :, :], 0.0), writes=[kiT3b])
                cx.op("dve", lambda: nc.vector.memset(qiT3[:, :, :], 0.0), writes=[qiT3b])
                chunks.append(([(C_V, 64, 0), (C_KI, 40, 64)], 104, None, None))

                def loadw(i):
                    cols, M, _, _ = chunks[i]
                    s = i % 3
                    for (c0, n, off) in cols:
                        cx.dma("pool", wt[:, s, :, off:off + n], w_in_v[:, :, c0:c0 + n], writes=[wtb[s]])
                loadw(0)
                loadw(1)
                pp = 0
                for i, (cols, M, dst, dstb) in enumerate(chunks):
                    s = i % 3
                    if i + 2 < len(chunks):
                        loadw(i + 2)
                    if dst is not None:
                        for tt in range(NT):
                            tsl = slice(tt * TT, (tt + 1) * TT)
                            p = pp % 2
                            pp += 1
                            for kc in range(8):
                                cx.op("pe", lambda kc=kc: nc.tensor.matmul(
                                    ps1[0:M, p, :], lhsT=wt[:, s, kc, 0:M], rhs=xn_sb[:, kc, tsl],
                                    start=(kc == 0), stop=(kc == 7)),
                                    reads=[wtb[s], xnb[kc][tt]], writes=[ps1b[p]], sig=(kc == 7))
                            if dst == "KI":
                                for v in range(3):
                                    cx.op("dve", lambda v=v: nc.vector.tensor_copy(out=kiT3[32 * v:32 * v + 32, v, tsl],
                                                                                   in_=ps1[32 * v:32 * v + 32, p, :]),
                                          reads=[ps1b[p]], writes=[dstb])
                            elif isinstance(dst, str):
                                cx.op("dve", lambda: nc.vector.tensor_copy(out=kT2[0:64, 0, tsl], in_=ps1[0:64, p, :]),
                                      reads=[ps1b[p]], writes=[dstb])
                                cx.op("dve", lambda: nc.vector.tensor_copy(out=kT2[64:128, 1, tsl], in_=ps1[64:128, p, :]),
                                      reads=[ps1b[p]], writes=[dstb])
                            elif pp % 2 == 0:
                                cx.op("act", lambda: nc.scalar.copy(out=dst[:, tsl], in_=ps1[0:M, p, :]),
                                      reads=[ps1b[p]], writes=[dstb])
                            else:
                                cx.op("dve", lambda: nc.vector.tensor_copy(out=dst[:, tsl], in_=ps1[0:M, p, :]),
                                      reads=[ps1b[p]], writes=[dstb])
                    else:
                        cx.op("dve", lambda: nc.vector.memset(vaug2[:, :, :, :], 1.0), writes=[vb])
                        for tb in range(16):
                            p = tb % 2
                            for kc in range(8):
                                cx.op("pe", lambda kc=kc: nc.tensor.matmul(
                                    psv[:, p, 0:104], lhsT=xn_sb[:, kc, tb * 128:(tb + 1) * 128], rhs=wt[:, s, kc, 0:104],
                                    start=(kc == 0), stop=(kc == 7)),
                                    reads=[wtb[s], xnb[kc][tb // 4]], writes=[psvb[p]], sig=(kc == 7))
                            cx.op("dve", lambda: nc.vector.tensor_copy(out=vaug2[:, tb, 0, 0:64], in_=psv[:, p, 0:64]),
                                  reads=[psvb[p]], writes=[vb])
                            cx.op("dve", lambda: nc.vector.tensor_copy(out=vaug2[:, tb, 1, 64:128], in_=psv[:, p, 0:64]),
                                  reads=[psvb[p]], writes=[vb])
                            cx.op("dve", lambda: nc.vector.tensor_copy(out=wi_sb[:, tb, :], in_=psv[:, p, 96:104]),
                                  reads=[psvb[p]], writes=[wib])
                        cx.op("dve", lambda: nc.vector.scalar_tensor_tensor(
                            out=wabs[:, :, :], in0=wi_sb[:, :, :], scalar=-1.0, in1=wi_sb[:, :, :],
                            op0=ALU.mult, op1=ALU.max), reads=[wib], writes=[wabsb])
                        cx.op("act", lambda: nc.scalar.activation(out=wsgn[:, :, :], in_=wi_sb[:, :, :], func=AF.Sign),
                              reads=[wib], writes=[wsgnb])
                cx.barrier_all()
            if getattr(g, "stop", 9) == 1:
                return
            with ExitStack() as s2:
                score = g.x_sb; scb = [Buf() for _ in range(4)]
                Rt = s2.enter_context(nc.sbuf_tensor(f"Rt{l}", [128, 3, TT], F32)); Rb = [Buf(), Buf(), Buf()]
                bias = s2.enter_context(nc.sbuf_tensor(f"bias{l}", [128, 4, S], BF16)); biasb = [Buf() for _ in range(4)]
                _bv = g.x_sb.bitcast(BF16)[:, 4:8, :].rearrange("p a (k q) -> p a k q", q=TT)

                def bT(par, kb, lo, hi):
                    return _bv[:, par * 2 + kb // 8, kb % 8, lo:hi]
                biasTb = [[Buf() for _ in range(16)] for _ in range(2)]
                PT = s2.enter_context(nc.sbuf_tensor(f"PT{l}", [128, 3, TT], BF16)); PTb = [Buf(), Buf(), Buf()]
                sm = s2.enter_context(nc.sbuf_tensor(f"sm{l}", [128, 2, 64, 4], F32)); smb = [Buf(), Buf()]
                rdn = s2.enter_context(nc.sbuf_tensor(f"rdn{l}", [128, 2, TT], F32)); rdnb = [Buf(), Buf()]
                smm = s2.enter_context(nc.sbuf_tensor(f"smm{l}", [128, 2, 4], F32)); smmb = [Buf(), Buf()]
                sma = s2.enter_context(nc.sbuf_tensor(f"sma{l}", [128, 2, 4], F32)); smab = [Buf(), Buf()]
                ACTJ = (0, 3)
                junk, junkb = bias[:, 0, :], biasb[0]
                junkA, junkAb = bias[:, 1, :], biasb[1]
                psL = s2.enter_context(nc.psum_tensor(f"psL{l}", [128, 2, TT], F32)); psLb = [Buf(), Buf()]
                accP = s2.enter_context(nc.psum_tensor(f"accP{l}", [128, TT], F32)); accPb = Buf()
                psS = s2.enter_context(nc.psum_tensor(f"psS{l}", [128, 2, TT], F32)); psSb = [Buf(), Buf()]
                psO = s2.enter_context(nc.psum_tensor(f"psO{l}", [128, 2, TT], F32)); psOb = [Buf(), Buf()]
                psB = s2.enter_context(nc.psum_tensor(f"psB{l}", [128, TT], F32)); psBb = Buf()
                st_ = {"L": 0, "R": 0, "S": 0, "PT": 0, "O": 0}
                V = nc.vector
                cx.op("dve", lambda: V.memset(sm[:, :, :, :], 0.0), writes=smb)

                def indexer(qb, j):
                    n = (qb + 1) * 128
                    qsl = slice(qb * 128, (qb + 1) * 128)
                    for kt in range((n + TT - 1) // TT):
                        w = min(TT, n - kt * TT)
                        for h in range(8):
                            jj, po = h // 3, (h % 3) * 32
                            pL = st_["L"] % 2; st_["L"] += 1
                            r = st_["R"] % 3; st_["R"] += 1
                            cx.op("pe", lambda: nc.tensor.matmul(
                                psL[:, pL, 0:w], lhsT=qiT3[:, jj, qsl], rhs=kiT3[:, h % 3, kt * TT:kt * TT + w],
                                start=True, stop=True), reads=[qiT3b, kiT3b], writes=[psLb[pL]])
                            cx.op("act", lambda: nc.scalar.activation(
                                out=Rt[:, r, 0:w], in_=psL[:, pL, 0:w], func=AF.Relu, scale=wabs[:, qb, h:h + 1]),
                                reads=[psLb[pL], wabsb], writes=[Rb[r]])
                            if h == 0:
                                cx.op("dve", lambda: V.tensor_scalar(
                                    out=accP[:, 0:w], in0=Rt[:, r, 0:w], scalar1=wsgn[:, qb, 0:1], scalar2=None,
                                    op0=ALU.mult), reads=[Rb[r], wsgnb], writes=[accPb])
                            elif h < 7:
                                cx.op("dve", lambda: V.scalar_tensor_tensor(
                                    out=accP[:, 0:w], in0=Rt[:, r, 0:w], scalar=wsgn[:, qb, h:h + 1], in1=accP[:, 0:w],
                                    op0=ALU.mult, op1=ALU.add), reads=[Rb[r], wsgnb, accPb], writes=[accPb])
                            else:
                                cx.op("dve", lambda: V.scalar_tensor_tensor(
                                    out=score[:, j, kt * TT:kt * TT + w], in0=Rt[:, r, 0:w], scalar=wsgn[:, qb, h:h + 1],
                                    in1=accP[:, 0:w], op0=ALU.mult, op1=ALU.add),
                                    reads=[Rb[r], wsgnb, accPb], writes=[scb[j]])
                            yield
                    cx.op("dve", lambda: V.tensor_tensor(
                        out=score[:, j, qsl], in0=score[:, j, qsl], in1=g.negm[:, :], op=ALU.add),
                        reads=[scb[j], g.constb], writes=[scb[j]])

                def bisect(sb):
                    s = sb % 2
                    B = smb[s]
                    Bm, Ba = smmb[s], smab[s]
                    js = [j for j in range(4) if 4 * sb + j >= 2]
                    actj = [j for j in js if j in ACTJ]

                    def sv(fn, extra_r=(), extra_w=()):
                        cx.op("dve", fn, reads=[B] + list(extra_r), writes=[B] + list(extra_w))
                    for j in range(4):
                        n = (4 * sb + j + 1) * 128
                        val = (2.0 * TOPK - 1.0 - n) if j in actj else (TOPK - 0.5)
                        sv(lambda: V.memset(sm[:, s, 6, j:j + 1], val))
                    for j in js:
                        qb = 4 * sb + j
                        n = (qb + 1) * 128
                        sv(lambda: V.tensor_reduce(out=sm[:, s, 0, j:j + 1], in_=score[:, j, 0:n], axis=AX.X, op=ALU.max), extra_r=[scb[j]])
                        sv(lambda: V.tensor_reduce(out=sm[:, s, 1, j:j + 1], in_=score[:, j, 0:qb * 128], axis=AX.X, op=ALU.min), extra_r=[scb[j]])
                    sv(lambda: V.tensor_tensor(out=sm[:, s, 2, :], in0=sm[:, s, 0, :], in1=sm[:, s, 1, :], op=ALU.subtract))
                    sv(lambda: V.tensor_tensor(out=sm[:, s, 8:8 + IT, :], in0=g.pw3[:, 0:IT, :],
                                               in1=sm[:, s, 2:3, :].to_broadcast([128, IT, 4]), op=ALU.mult), extra_r=[g.constb])
                    for i in range(IT):
                        sv(lambda: V.tensor_tensor(out=sm[:, s, 5, :], in0=sm[:, s, 1, :], in1=sm[:, s, 8 + i, :], op=ALU.add))
                        if actj:
                            sv(lambda: V.scalar_tensor_tensor(out=smm[:, s, :], in0=sm[:, s, 1, :], scalar=-1.0, in1=sm[:, s, 8 + i, :],
                                                              op0=ALU.mult, op1=ALU.subtract), extra_w=[Bm])
                        for j in actj:
                            n = (4 * sb + j + 1) * 128
                            cx.op("act", lambda: nc.scalar.activation(
                                out=junkA[:, 0:n], in_=score[:, j, 0:n], func=AF.Sign, bias=smm[:, s, j:j + 1], scale=1.0,
                                accum_out=sma[:, s, j:j + 1]), reads=[Bm, scb[j]], writes=[Ba, junkAb])
                        for j in js:
                            if j in actj:
                                continue
                            n = (4 * sb + j + 1) * 128
                            sv(lambda: V.tensor_scalar(out=junk[:, 0:n], in0=score[:, j, 0:n], scalar1=sm[:, s, 5, j:j + 1],
                                                       scalar2=None, op0=ALU.is_ge, op1=ALU.add, accum_out=sm[:, s, 3, j:j + 1]),
                               extra_r=[scb[j]], extra_w=[junkb])
                        for j in actj:
                            sv(lambda: V.tensor_copy(out=sm[:, s, 3, j:j + 1], in_=sma[:, s, j:j + 1]), extra_r=[Ba])
                        sv(lambda: V.tensor_tensor(out=sm[:, s, 4, :], in0=sm[:, s, 3, :], in1=sm[:, s, 6, :], op=ALU.is_ge))
                        sv(lambda: V.tensor_tensor(out=sm[:, s, 4, :], in0=sm[:, s, 4, :], in1=sm[:, s, 8 + i, :], op=ALU.mult))
                        sv(lambda: V.tensor_tensor(out=sm[:, s, 1, :], in0=sm[:, s, 1, :], in1=sm[:, s, 4, :], op=ALU.add))
                        yield
                    for j in js:
                        n = (4 * sb + j + 1) * 128
                        cx.op("dve", lambda: V.tensor_scalar(out=bias[:, j, 0:n], in0=score[:, j, 0:n], scalar1=sm[:, s, 1, j:j + 1],
                                                             scalar2=NEG, op0=ALU.is_lt, op1=ALU.mult),
                              reads=[B, scb[j]], writes=[biasb[j]])
                    for j in range(4):
                        qb = 4 * sb + j
                        if qb < 2:
                            if qb > 0:
                                cx.op("dve", lambda: V.memset(bias[:, j, 0:qb * 128], 0.0), writes=[biasb[j]])
                            cx.op("dve", lambda: V.tensor_copy(out=bias[:, j, qb * 128:(qb + 1) * 128], in_=g.negb[:, :]),
                                  reads=[g.constb], writes=[biasb[j]])

                def make_biasT(sb):
                    par = sb % 2
                    for kb in range(4 * sb + 4):
                        j0 = max(0, kb - 4 * sb)
                        for j in range(j0, 4):
                            cx.op("pe", lambda: nc.tensor.matmul(
                                psB[:, j * 128:(j + 1) * 128], lhsT=bias[:, j, kb * 128:(kb + 1) * 128], rhs=g.ident[:, :],
                                start=(j == j0), stop=(j == 3), skip_group_check=True),
                                reads=[biasb[j], g.constb], writes=[psBb], sig=(j == 3))
                        cx.op("act", lambda: nc.scalar.copy(out=bT(par, kb, j0 * 128, TT), in_=psB[:, j0 * 128:TT]),
                              reads=[psBb], writes=[biasTb[par][kb]])

                def attention(sb):
                    par = sb % 2
                    nk = 4 * sb + 4
                    items = [(h, kb) for h in range(8) for kb in range(nk)]
                    slots = {}

                    def emit_sb(i):
                        h, kb = items[i]
                        c, hp = h // 2, (h % 2) * 64
                        qo = max(0, kb - 4 * sb) * 128
                        qs = slice(sb * TT + qo, (sb + 1) * TT)
                        ksl = slice(kb * 128, (kb + 1) * 128)
                        pS = st_["S"] % 2; st_["S"] += 1
                        slots[i] = pS
                        cx.op("pe", lambda: nc.tensor.matmul(
                            psS[:, pS, qo:TT], lhsT=kT2[:, h % 2, ksl], rhs=qT[:, c, qs],
                            start=True, stop=False), reads=[kT2b, qTb], writes=[psSb[pS]], sig=False)
                        cx.op("pe", lambda: nc.tensor.matmul(
                            psS[:, pS, qo:TT], lhsT=g.ident[:, :], rhs=bT(par, kb, qo, TT),
                            start=False, stop=True), reads=[biasTb[par][kb], g.constb], writes=[psSb[pS]])
                    emit_sb(0)
                    po = None
                    for i, (h, kb) in enumerate(items):
                        c = h // 2
                        if kb == 0:
                            po = st_["O"] % 2; st_["O"] += 1
                        qo = max(0, kb - 4 * sb) * 128
                        pS = slots.pop(i)
                        pt = st_["PT"] % 3; st_["PT"] += 1
                        cx.op("act", lambda: nc.scalar.activation(
                            out=PT[:, pt, qo:TT], in_=psS[:, pS, qo:TT], func=AF.Exp, scale=0.125),
                            reads=[psSb[pS]], writes=[PTb[pt]])
                        if i + 1 < len(items):
                            emit_sb(i + 1)
                        cx.op("pe", lambda: nc.tensor.matmul(
                            psO[:, po, qo:TT], lhsT=vaug2[:, kb, h % 2, :], rhs=PT[:, pt, qo:TT],
                            start=(kb == 0), stop=(kb == nk - 1), skip_group_check=True),
                            reads=[PTb[pt], vb], writes=[psOb[po]], sig=(kb == nk - 1))
                        if kb == nk - 1:
                            osl = slice(sb * TT, (sb + 1) * TT)
                            if h % 2 == 0:
                                num, den, dst_p = slice(0, 64), slice(64, 128), slice(0, 64)
                            else:
                                num, den, dst_p = slice(64, 128), slice(0, 64), slice(64, 128)
                            cx.op("act", lambda: nc.scalar.activation(out=rdn[dst_p, po, :], in_=psO[den, po, :], func=AF.Ln),
                                  reads=[psOb[po]], writes=[rdnb[po]])
                            cx.op("act", lambda: nc.scalar.activation(out=rdn[dst_p, po, :], in_=rdn[dst_p, po, :], func=AF.Exp, scale=-1.0),
                                  reads=[rdnb[po]], writes=[rdnb[po]])
                            cx.op("dve", lambda: V.tensor_tensor(out=attT[dst_p, c, osl], in0=psO[num, po, :], in1=rdn[dst_p, po, :],
                                                                 op=ALU.mult),
                                  reads=[psOb[po], rdnb[po]], writes=attTb[4 * sb:4 * sb + 4])
                        yield

                def run_all(gen):
                    for _ in gen:
                        pass

                def chain(gens):
                    for gg in gens:
                        for _ in gg:
                            yield

                def interleave(main, n_main, side, n_side):
                    done = 0
                    k = 0
                    for _ in main:
                        k += 1
                        tgt = (n_side * k) // max(n_main, 1)
                        while done < tgt:
                            next(side, None)
                            done += 1

                for sb in range(4):
                    qbs = [4 * sb + j for j in range(4) if 4 * sb + j >= 2]
                    n_units = sum(((qb + 1) * 128 + TT - 1) // TT * 8 for qb in qbs)
                    ig = chain([indexer(qb, qb - 4 * sb) for qb in qbs])
                    bg = bisect(sb)
                    if sb == 0:
                        run_all(ig)
                        run_all(bg)
                    else:
                        ag = attention(sb - 1)
                        n_items = 8 * 4 * sb
                        interleave(ig, n_units, ag, n_items // 2)
                        interleave(bg, IT, ag, n_items - n_items // 2)
                        run_all(ag)
                    make_biasT(sb)
                run_all(attention(3))
                cx.barrier_all()

        if getattr(g, "stop", 9) == 2:
            return
        rnnT = ms.enter_context(nc.sbuf_tensor(f"rnnT{l}", [128, 8, S], BF16))
        rnnb = [Buf() for _ in range(8)]
        with ExitStack() as st:
            T0, T1, T2, T3 = [g.x_sb[:, 2 * i_:2 * i_ + 2, :] for i_ in range(4)]
            gl32 = st.enter_context(nc.sbuf_tensor(f"gl32{l}", [128, 2, S], F32)); gl32b = [Buf(), Buf()]
            T0b, T1b, T2b, T3b = [[Buf(), Buf()] for _ in range(4)]
            xrb = st.enter_context(nc.sbuf_tensor(f"xrb{l}", [128, 2, S + 8], BF16)); xrbb = [Buf(), Buf()]
            xrb2 = st.enter_context(nc.sbuf_tensor(f"xrc{l}", [128, 2, S + 8], BF16)); xrb2b = [Buf(), Buf()]
            xcb = st.enter_context(nc.sbuf_tensor(f"xcb{l}", [128, 2, S], BF16)); xcbb = [Buf(), Buf()]
            wxr = st.enter_context(nc.sbuf_tensor(f"wxr{l}", [128, 2, 8, 128], BF16)); wxrb = [Buf(), Buf()]
            wgr = st.enter_context(nc.sbuf_tensor(f"wgr{l}", [128, 2, 8, 128], BF16)); wgrb = [Buf(), Buf()]
            dg = st.enter_context(nc.sbuf_tensor(f"dg{l}", [128, 2, 4, 128], BF16)); dgb = [Buf(), Buf()]
            bdA = st.enter_context(nc.sbuf_tensor(f"bdA{l}", [128, 8, 128], BF16)); bdAb = Buf()
            bdX = st.enter_context(nc.sbuf_tensor(f"bdX{l}", [128, 8, 128], BF16)); bdXb = Buf()
            nsp = st.enter_context(nc.sbuf_tensor(f"nsp{l}", [128, 3, 8], F32)); nspb = Buf()
            ps3 = st.enter_context(nc.psum_tensor(f"ps3{l}", [128, 2, TT], F32)); ps3b = [Buf(), Buf()]
            psc = st.enter_context(nc.psum_tensor(f"psc{l}", [128, 2, TT], F32)); pscb = [Buf(), Buf()]
            psa = st.enter_context(nc.psum_tensor(f"psa{l}", [128, 2, TT], F32)); psab = [Buf(), Buf()]
            psx = st.enter_context(nc.psum_tensor(f"psx{l}", [128, 2, TT], F32)); psxb = [Buf(), Buf()]
            V = nc.vector
            cx.op("dve", lambda: V.memset(bdA[:, :, :], 0.0), writes=[bdAb])
            cx.op("dve", lambda: V.memset(bdX[:, :, :], 0.0), writes=[bdXb])
            cx.op("dve", lambda: V.memset(xrb[:, :, 0:8], 0.0), writes=xrbb)
            cx.op("dve", lambda: V.memset(xrb2[:, :, 0:8], 0.0), writes=xrb2b)
            for half in range(2):
                hs = slice(half * 64, (half + 1) * 64)
                srcA = W["rg_wa"][l].rearrange("(c two) i j -> two i c j", two=2)[half]
                srcX = W["rg_wx"][l].rearrange("(c two) i j -> two i c j", two=2)[half]
                cx.dma("pool", bdA[hs, :, hs], srcA, writes=[bdAb])
                cx.dma("pool", bdX[hs, :, hs], srcX, writes=[bdXb])
            lam = g.prm_sb[:, pb + P_LAM:pb + P_LAM + 8]
            cx.op("act", lambda: nc.scalar.activation(out=nsp[:, 0, :], in_=lam, func=AF.Exp, scale=-1.0),
                  reads=[g.prmb], writes=[nspb])
            cx.op("act", lambda: nc.scalar.activation(out=nsp[:, 0, :], in_=nsp[:, 0, :], func=AF.Ln, scale=1.0, bias=g.one_c[:, 0:1]),
                  reads=[nspb, g.constb], writes=[nspb])
            cx.op("dve", lambda: V.tensor_scalar(out=nsp[:, 1, :], in0=nsp[:, 0, :], scalar1=-8.0, scalar2=None, op0=ALU.mult),
                  reads=[nspb], writes=[nspb])
            cx.op("dve", lambda: V.tensor_scalar(out=nsp[:, 2, :], in0=nsp[:, 0, :], scalar1=-16.0, scalar2=None, op0=ALU.mult),
                  reads=[nspb], writes=[nspb])

            def load3(c):
                s = c % 2
                cx.dma("pool", wxr[:, s, :, :], w_in_v[:, :, C_XR + c * 128:C_XR + (c + 1) * 128], writes=[wxrb[s]])
                cx.dma("pool", wgr[:, s, :, :], w_in_v[:, :, C_GR + c * 128:C_GR + (c + 1) * 128], writes=[wgrb[s]])
            k3 = [0]

            def stageA(c):
                s = c % 2
                pcol = lambda base: g.prm_sb[:, pb + base + c:pb + base + c + 1]
                for j in range(4):
                    cx.op("act", lambda j=j: nc.scalar.activation(
                        out=dg[:, s, j, :], in_=g.ident[:, :], func=AF.Identity,
                        scale=g.prm_sb[:, pb + P_CW + j * 8 + c:pb + P_CW + j * 8 + c + 1]),
                        reads=[g.constb, g.prmb], writes=[dgb[s]])
                for tt in range(NT):
                    tsl = slice(tt * TT, (tt + 1) * TT)
                    p = k3[0] % 2; k3[0] += 1
                    for kc in range(8):
                        cx.op("pe", lambda kc=kc: nc.tensor.matmul(
                            ps3[:, p, :], lhsT=wxr[:, s, kc, :], rhs=xn_sb[:, kc, tsl], start=(kc == 0), stop=(kc == 7)),
                            reads=[wxrb[s], xnb[kc][tt]], writes=[ps3b[p]], sig=(kc == 7))
                    cx.op("dve", lambda: V.tensor_copy(out=xrb[:, s, 4 + tt * TT:4 + (tt + 1) * TT], in_=ps3[:, p, :]),
                          reads=[ps3b[p]], writes=[xrbb[s]])
                    cx.op("dve", lambda: V.tensor_copy(out=xrb2[:, s, 5 + tt * TT:5 + (tt + 1) * TT], in_=ps3[:, p, :]),
                          reads=[ps3b[p]], writes=[xrb2b[s]])
                for tt in range(NT):
                    tsl = slice(tt * TT, (tt + 1) * TT)
                    p = tt % 2
                    for j in range(4):
                        if j % 2 == 1:
                            rhs_ap = xrb[:, s, tt * TT + 1 + j:tt * TT + 1 + j + TT]
                        else:
                            rhs_ap = xrb2[:, s, tt * TT + 2 + j:tt * TT + 2 + j + TT]
                        cx.op("pe", lambda j=j, rhs_ap=rhs_ap: nc.tensor.matmul(
                            psc[:, p, :], lhsT=dg[:, s, j, :], rhs=rhs_ap,
                            start=(j == 0), stop=(j == 3)), reads=[dgb[s], xrbb[s], xrb2b[s]], writes=[pscb[p]], sig=(j == 3))
                    cx.op("dve", lambda: V.tensor_scalar(out=T0[:, s, tsl], in0=psc[:, p, :], scalar1=pcol(P_CB),
                                                         scalar2=None, op0=ALU.add),
                          reads=[pscb[p], g.prmb], writes=[T0b[s]])
                    cx.op("dve", lambda: V.tensor_scalar(out=xcb[:, s, tsl], in0=psc[:, p, :], scalar1=pcol(P_CB),
                                                         scalar2=None, op0=ALU.add),
                          reads=[pscb[p], g.prmb], writes=[xcbb[s]])
            def stageA2(c):
                s = c % 2
                pcol = lambda base: g.prm_sb[:, pb + base + c:pb + base + c + 1]
                for tt in range(NT):
                    tsl = slice(tt * TT, (tt + 1) * TT)
                    p = tt % 2
                    cx.op("pe", lambda: nc.tensor.matmul(psa[:, p, :], lhsT=bdA[:, c, :], rhs=xcb[:, s, tsl], start=True, stop=True),
                          reads=[bdAb, xcbb[s]], writes=[psab[p]])
                    cx.op("act", lambda: nc.scalar.activation(out=T1[:, s, tsl], in_=psa[:, p, :], func=AF.Sigmoid,
                                                               bias=pcol(P_BA), scale=1.0),
                          reads=[psab[p], g.prmb], writes=[T1b[s]])
                    cx.op("pe", lambda: nc.tensor.matmul(psx[:, p, :], lhsT=bdX[:, c, :], rhs=xcb[:, s, tsl], start=True, stop=True),
                          reads=[bdXb, xcbb[s]], writes=[psxb[p]])
                    cx.op("act", lambda: nc.scalar.activation(out=T2[:, s, tsl], in_=psx[:, p, :], func=AF.Sigmoid,
                                                               bias=pcol(P_BX), scale=1.0),
                          reads=[psxb[p], g.prmb], writes=[T2b[s]])

            def stageG(c):
                s = c % 2
                for tt in range(NT):
                    tsl = slice(tt * TT, (tt + 1) * TT)
                    p = k3[0] % 2; k3[0] += 1
                    for kc in range(8):
                        cx.op("pe", lambda kc=kc: nc.tensor.matmul(
                            ps3[:, p, :], lhsT=wgr[:, s, kc, :], rhs=xn_sb[:, kc, tsl], start=(kc == 0), stop=(kc == 7)),
                            reads=[wgrb[s], xnb[kc][tt]], writes=[ps3b[p]], sig=(kc == 7))
                    cx.op("act", lambda: nc.scalar.activation(out=gl32[:, s, tsl], in_=ps3[:, p, :], func=AF.Gelu_apprx_tanh),
                          reads=[ps3b[p]], writes=[gl32b[s]])

            def stageB(c):
                s = c % 2
                A = nc.scalar.activation
                cx.op("act", lambda: A(out=T3[:, s, :], in_=T1[:, s, :], func=AF.Exp, scale=nsp[:, 1, c:c + 1]),
                      reads=[T1b[s], nspb], writes=[T3b[s]])
                cx.op("act", lambda: A(out=T1[:, s, :], in_=T1[:, s, :], func=AF.Exp, scale=nsp[:, 2, c:c + 1]),
                      reads=[T1b[s], nspb], writes=[T1b[s]])
                cx.op("act", lambda: A(out=T1[:, s, :], in_=T1[:, s, :], func=AF.Sqrt, scale=-1.0, bias=g.one_c[:, 0:1]),
                      reads=[T1b[s], g.constb], writes=[T1b[s]])
                cx.op("dve", lambda: V.tensor_tensor(out=T2[:, s, :], in0=T2[:, s, :], in1=T0[:, s, :], op=ALU.mult),
                      reads=[T2b[s], T0b[s]], writes=[T2b[s]])

            def stageB2(c):
                s = c % 2
                A = nc.scalar.activation
                cx.op("dve", lambda: V.tensor_tensor(out=T2[:, s, :], in0=T2[:, s, :], in1=T1[:, s, :], op=ALU.mult),
                      reads=[T2b[s], T1b[s]], writes=[T2b[s]])
                cx.op("dve", lambda: V.tensor_tensor_scan(out=T0[:, s, :], data0=T3[:, s, :], data1=T2[:, s, :], initial=0.0,
                                                          op0=ALU.mult, op1=ALU.add),
                      reads=[T3b[s], T2b[s]], writes=[T0b[s]])
                cx.op("dve", lambda: V.tensor_tensor(out=rnnT[:, c, :], in0=T0[:, s, :], in1=gl32[:, s, :], op=ALU.mult),
                      reads=[T0b[s], gl32b[s]], writes=[rnnb[c]])

            load3(0)
            load3(1)
            stageA(0)
            stageA2(0)
            stageG(0)
            for c in range(8):
                if c + 1 < 8:
                    stageA(c + 1)
                stageB(c)
                if c + 1 < 8:
                    stageA2(c + 1)
                    stageG(c + 1)
                stageB2(c)
                if c + 2 < 8:
                    load3(c + 2)
            cx.barrier_all()

        if getattr(g, "stop", 9) == 3:
            return
        with ExitStack() as st:
            merged = st.enter_context(nc.sbuf_tensor(f"merged{l}", [128, 8, S], BF16))
            mgb = _grid(8, NT)
            for c in range(8):
                cx.dma("sp" if c % 2 == 0 else "act", x_sb[:, c, :], xres_v[:, c, :],
                       reads=[g.xresb[c][tt] for tt in range(NT)], writes=[xb[c][tt] for tt in range(NT)])
            wga = st.enter_context(nc.sbuf_tensor(f"wga{l}", [128, 2, 8, 128], BF16)); wgab = [Buf(), Buf()]
            wgl = st.enter_context(nc.sbuf_tensor(f"wgl{l}", [128, 2, 8, 128], BF16)); wglb = [Buf(), Buf()]
            wpp = st.enter_context(nc.sbuf_tensor(f"wpp{l}", [128, 2, 4, 128], BF16)); wppb = [Buf(), Buf()]
            wrr = st.enter_context(nc.sbuf_tensor(f"wrr{l}", [128, 2, 8, 128], BF16)); wrrb = [Buf(), Buf()]
            sga = st.enter_context(nc.sbuf_tensor(f"sga{l}", [128, 2, TT], F32)); sgab = [Buf(), Buf()]
            sgl = st.enter_context(nc.sbuf_tensor(f"sgl{l}", [128, 2, TT], F32)); sglb = [Buf(), Buf()]
            wp_v = W["w_att_proj"][l].rearrange("(kc p) n -> p kc n", p=128)
            wr_v = W["w_rnn_proj"][l].rearrange("(kc p) n -> p kc n", p=128)
            wo_v = W["w_out"][l].rearrange("(kc p) n -> p kc n", p=128)
            with ExitStack() as s4:
                pga = s4.enter_context(nc.psum_tensor(f"pga{l}", [128, 2, TT], F32)); pgab = [Buf(), Buf()]
                pgl = s4.enter_context(nc.psum_tensor(f"pgl{l}", [128, 2, TT], F32)); pglb = [Buf(), Buf()]
                ppp = s4.enter_context(nc.psum_tensor(f"ppp{l}", [128, 2, TT], F32)); pppb = [Buf(), Buf()]
                prr = s4.enter_context(nc.psum_tensor(f"prr{l}", [128, 2, TT], F32)); prrb = [Buf(), Buf()]

                def load4(c):
                    s = c % 2
                    csl = slice(c * 128, (c + 1) * 128)
                    cx.dma("pool", wga[:, s, :, :], w_in_v[:, :, C_GA + c * 128:C_GA + (c + 1) * 128], writes=[wgab[s]])
                    cx.dma("pool", wgl[:, s, :, :], w_in_v[:, :, C_GL + c * 128:C_GL + (c + 1) * 128], writes=[wglb[s]])
                    cx.dma("pool", wpp[:, s, :, :], wp_v[:, :, csl], writes=[wppb[s]])
                    cx.dma("pool", wrr[:, s, :, :], wr_v[:, :, csl], writes=[wrrb[s]])
                load4(0)
                k4 = 0
                for c in range(8):
                    s = c % 2
                    if c + 1 < 8:
                        load4(c + 1)
                    for tt in range(NT):
                        tsl = slice(tt * TT, (tt + 1) * TT)
                        p = k4 % 2; k4 += 1
                        for kc in range(8):
                            cx.op("pe", lambda kc=kc: nc.tensor.matmul(
                                pga[:, p, :], lhsT=wga[:, s, kc, :], rhs=xn_sb[:, kc, tsl], start=(kc == 0), stop=(kc == 7)),
                                reads=[wgab[s], xnb[kc][tt]], writes=[pgab[p]], sig=(kc == 7))
                        cx.op("act", lambda: nc.scalar.activation(out=sga[:, p, :], in_=pga[:, p, :], func=AF.Sigmoid),
                              reads=[pgab[p]], writes=[sgab[p]])
                        for kc in range(4):
                            cx.op("pe", lambda kc=kc: nc.tensor.matmul(
                                ppp[:, p, :], lhsT=wpp[:, s, kc, :], rhs=attT[:, kc, tsl], start=(kc == 0), stop=(kc == 3)),
                                reads=[wppb[s]] + attTb[tt * 4:(tt + 1) * 4], writes=[pppb[p]], sig=(kc == 3))
                        cx.op("dve", lambda: nc.vector.tensor_tensor(out=sga[:, p, :], in0=ppp[:, p, :], in1=sga[:, p, :], op=ALU.mult),
                              reads=[pppb[p], sgab[p]], writes=[sgab[p]])
                        for kc in range(8):
                            cx.op("pe", lambda kc=kc: nc.tensor.matmul(
                                pgl[:, p, :], lhsT=wgl[:, s, kc, :], rhs=xn_sb[:, kc, tsl], start=(kc == 0), stop=(kc == 7)),
                                reads=[wglb[s], xnb[kc][tt]], writes=[pglb[p]], sig=(kc == 7))
                        cx.op("act", lambda: nc.scalar.activation(out=sgl[:, p, :], in_=pgl[:, p, :], func=AF.Sigmoid),
                              reads=[pglb[p]], writes=[sglb[p]])
                        for kc in range(8):
                            cx.op("pe", lambda kc=kc: nc.tensor.matmul(
                                prr[:, p, :], lhsT=wrr[:, s, kc, :], rhs=rnnT[:, kc, tsl], start=(kc == 0), stop=(kc == 7)),
                                reads=[wrrb[s]] + rnnb, writes=[prrb[p]], sig=(kc == 7))
                        cx.op("dve", lambda: nc.vector.tensor_tensor(out=sgl[:, p, :], in0=prr[:, p, :], in1=sgl[:, p, :], op=ALU.mult),
                              reads=[prrb[p], sglb[p]], writes=[sglb[p]])
                        cx.op("dve", lambda: nc.vector.tensor_tensor(out=merged[:, c, tsl], in0=sga[:, p, :], in1=sgl[:, p, :], op=ALU.add),
                              reads=[sgab[p], sglb[p]], writes=[mgb[c][tt]])
                cx.barrier_all()
            with ExitStack() as s5:
                wo = s5.enter_context(nc.sbuf_tensor(f"wo{l}", [128, 2, 8, 128], BF16)); wob = [Buf(), Buf()]
                pso = s5.enter_context(nc.psum_tensor(f"pso{l}", [128, 2, TT], F32)); psob = [Buf(), Buf()]

                def load5(c):
                    s = c % 2
                    cx.dma("pool", wo[:, s, :, :], wo_v[:, :, c * 128:(c + 1) * 128], writes=[wob[s]])
                load5(0)
                tiles5 = [(c, tt) for c in range(8) for tt in range(NT)]

                for k, (c, tt) in enumerate(tiles5):
                    s = c % 2
                    if tt == 0 and c + 1 < 8:
                        load5(c + 1)
                    tsl = slice(tt * TT, (tt + 1) * TT)
                    p = k % 2
                    for kc in range(8):
                        cx.op("pe", lambda kc=kc: nc.tensor.matmul(
                            pso[:, p, :], lhsT=wo[:, s, kc, :], rhs=merged[:, kc, tsl], start=(kc == 0), stop=(kc == 7)),
                            reads=[wob[s], mgb[kc][tt]], writes=[psob[p]], sig=(kc == 7))
                    cx.op("dve", lambda: nc.vector.tensor_tensor(out=x_sb[:, c, tsl], in0=pso[:, p, :], in1=x_sb[:, c, tsl], op=ALU.add),
                          reads=[psob[p], xb[c][tt]], writes=[xb[c][tt]])
                cx.barrier_all()


WNAMES = ["ffn1_wg", "ffn1_wu", "ffn1_wd", "w_in", "rg_wa", "rg_wx", "w_att_proj", "w_rnn_proj", "w_out",
          "ffn2_wg", "ffn2_wu", "ffn2_wd"]
WSHAPES = {"ffn1_wg": [DEPTH, D, DFF], "ffn1_wu": [DEPTH, D, DFF], "ffn1_wd": [DEPTH, DFF, D],
           "w_in": [DEPTH, D, N_IN], "rg_wa": [DEPTH, 16, 64, 64], "rg_wx": [DEPTH, 16, 64, 64],
           "w_att_proj": [DEPTH, 512, D], "w_rnn_proj": [DEPTH, D, D], "w_out": [DEPTH, D, D],
           "ffn2_wg": [DEPTH, D, DFF], "ffn2_wu": [DEPTH, D, DFF], "ffn2_wd": [DEPTH, DFF, D]}


def build_program(plan=None, stop=9):
    if plan is None:
        plan = ["ffn1_0", "mix_0", "ffn2_0+ffn1_1", "mix_1", "ffn2_1+final"]
    nc = bass.Bass("TRN2", target_bir_lowering=False)
    xT = nc.dram_tensor("xT", [D, S], F32, kind="ExternalInput").ap()
    prm = nc.dram_tensor("prm", [128, NP], F32, kind="ExternalInput").ap()
    W = {n: nc.dram_tensor(n, WSHAPES[n], F32, kind="ExternalInput").ap() for n in WNAMES}
    yT = nc.dram_tensor("yT", [D, S], F32, kind="ExternalOutput").ap()
    xres = nc.dram_tensor("xres", [D, S], F32, kind="Internal").ap()
    g = G()
    g.nc = nc
    g.stop = stop
    g.xres = xres
    g.xresb = _grid(8, NT)
    xT_v = xT.rearrange("(c p) t -> p c t", p=128)
    yT_v = yT.rearrange("(c p) t -> p c t", p=128)
    xres_v = xres.rearrange("(c p) t -> p c t", p=128)
    with ExitStack() as top:
        cx = Ctx(nc, top)
        g.cx = cx
        g.prm_sb = top.enter_context(nc.sbuf_tensor("prm_sb", [128, NP], F32)); g.prmb = Buf()
        g.ones_bf = top.enter_context(nc.sbuf_tensor("ones_bf", [128, 128], BF16))
        g.ident = top.enter_context(nc.sbuf_tensor("ident", [128, 128], BF16))
        g.negm = top.enter_context(nc.sbuf_tensor("negm", [128, 128], F32))
        g.negb = top.enter_context(nc.sbuf_tensor("negb", [128, 128], BF16))
        g.zer = top.enter_context(nc.sbuf_tensor("zer", [128, 128], F32))
        g.pw3 = top.enter_context(nc.sbuf_tensor("pw3", [128, 32, 4], F32))
        g.eps_c = top.enter_context(nc.sbuf_tensor("eps_c", [128, 1], F32))
        g.one_c = top.enter_context(nc.sbuf_tensor("one_c", [128, 1], F32))
        g.constb = Buf()
        P = nc.gpsimd
        cx.dma("sp", g.prm_sb[:, :], prm[:, :], writes=[g.prmb])
        cx.op("pool", lambda: P.memset(g.ones_bf[:, :], 1.0), writes=[g.constb])
        cx.op("pool", lambda: P.memset(g.zer[:, :], 0.0), writes=[g.constb])
        cx.op("pool", lambda: P.memset(g.eps_c[:, :], EPS), writes=[g.constb])
        cx.op("pool", lambda: P.memset(g.one_c[:, :], 1.0), writes=[g.constb])
        for i in range(IT):
            cx.op("pool", lambda i=i: P.memset(g.pw3[:, i, :], 2.0 ** -(i + 1)), writes=[g.constb])
        cx.op("pool", lambda: P.affine_select(out=g.ident[:, :], in_=g.ones_bf[:, :], pattern=[[1, 128]],
                                              compare_op=ALU.is_equal, fill=0.0, base=0, channel_multiplier=-1),
              reads=[g.constb], writes=[g.constb])
        cx.op("pool", lambda: P.affine_select(out=g.negm[:, :], in_=g.zer[:, :], pattern=[[-1, 128]],
                                              compare_op=ALU.is_ge, fill=-1e30, base=0, channel_multiplier=1),
              reads=[g.constb], writes=[g.constb])
        cx.op("pool", lambda: P.affine_select(out=g.negb[:, :], in_=g.zer[:, :], pattern=[[-1, 128]],
                                              compare_op=ALU.is_ge, fill=NEG, base=0, channel_multiplier=1),
              reads=[g.constb], writes=[g.constb])
        cx.barrier_all()

        def ffn_phase(src_v, steps, final, next_mix_l=None):
            with ExitStack() as st:
                g.uid = getattr(g, "uid", 0) + 1
                x_sb, xb, xn_sb, xnb = g.x_sb, g.xb, g.xn_sb, g.xnb
                fb = FfnBufs(g, st, f"p{g.uid}")
                nm = Normer(g, st, f"p{g.uid}")
                if src_v is not None:
                    for tt in range(NT):
                        tsl = slice(tt * TT, (tt + 1) * TT)
                        cx.dma("sp" if tt % 2 == 0 else "act", x_sb[:, :, tsl], src_v[:, :, tsl],
                               writes=[xb[c][tt] for c in range(8)])

                def gcol_of(l, which):
                    return l * PL + (P_F1 if which == 1 else P_F2)
                g0 = gcol_of(*steps[0])
                for tt in range(NT):
                    nm.tile(x_sb, xb, tt, norm_to_bf16(g, x_sb, xb, xn_sb, xnb, g0))

                def fin(tt, c, rs_ap, rs_buf):
                    tsl = slice(tt * TT, (tt + 1) * TT)
                    cx.op("dve", lambda: nc.vector.scalar_tensor_tensor(
                        out=x_sb[:, c, tsl], in0=x_sb[:, c, tsl], scalar=g.prm_sb[:, P_FIN + c:P_FIN + c + 1],
                        in1=rs_ap, op0=ALU.mult, op1=ALU.mult),
                        reads=[xb[c][tt], rs_buf, g.prmb], writes=[xb[c][tt]])
                    if c == 7:
                        cx.dma("sp" if tt % 2 == 0 else "act", yT_v[:, :, tsl], x_sb[:, :, tsl],
                               reads=[xb[cc][tt] for cc in range(8)])

                for si, (l, which) in enumerate(steps):
                    last = (si == len(steps) - 1)
                    if not last:
                        gn = gcol_of(*steps[si + 1])
                        hook = lambda tt, gn=gn: nm.tile(x_sb, xb, tt, norm_to_bf16(g, x_sb, xb, xn_sb, xnb, gn))
                    elif final:
                        hook = lambda tt: nm.tile(x_sb, xb, tt, fin)
                    else:
                        def hook(tt):
                            tsl = slice(tt * TT, (tt + 1) * TT)
                            cx.dma("sp" if tt % 2 == 0 else "act", xres_v[:, :, tsl], x_sb[:, :, tsl],
                                   reads=[xb[c][tt] for c in range(8)], writes=[g.xresb[c][tt] for c in range(8)])
                            if next_mix_l is not None:
                                gm = next_mix_l * PL + P_MX
                                nm.tile(x_sb, xb, tt, norm_to_bf16(g, x_sb, xb, xn_sb, xnb, gm))
                    emit_ffn(g, fb, x_sb, xb, xn_sb, xnb, W[f"ffn{which}_wg"][l], W[f"ffn{which}_wu"][l],
                             W[f"ffn{which}_wd"][l], after_tile=hook)
                cx.barrier_all()

        g.xn_sb = top.enter_context(nc.sbuf_tensor("xn_sb", [128, 8, S], BF16))
        g.xnb = _grid(8, NT)
        g.x_sb = top.enter_context(nc.sbuf_tensor("x_sb", [128, 8, S], F32))
        g.xb = _grid(8, NT)
        first = True
        have_xn = False
        for pi, ph in enumerate(plan):
            nxt = plan[pi + 1] if pi + 1 < len(plan) else None
            if ph.startswith("mix"):
                if first:
                    allx = [b for r in g.xb for b in r]
                    cx.dma("sp", g.x_sb[:, :, :], xT_v, writes=allx)
                    cx.dma("sp", xres_v, g.x_sb[:, :, :], reads=allx, writes=[b for r in g.xresb for b in r])
                    cx.barrier_all()
                emit_mixer(g, int(ph.split("_")[1]), W, have_xn=have_xn)
                have_xn = False
            else:
                steps = []
                final = False
                for part in ph.split("+"):
                    if part == "final":
                        final = True
                    else:
                        steps.append((int(part.split("_")[1]), 1 if part.startswith("ffn1") else 2))
                nml = int(nxt.split("_")[1]) if (nxt is not None and nxt.startswith("mix")) else None
                ffn_phase(xT_v if first else None, steps, final, next_mix_l=nml)
                have_xn = nml is not None
            first = False
        if not plan[-1].endswith("final"):
            cx.dma("sp", yT_v, g.x_sb[:, :, :], reads=[b for r in g.xb for b in r])
        cx.barrier_all()
        g.stats = (cx.n_ins, cx.n_wait)
    return nc


def pack_prm(inp):
    prm = np.zeros((128, NP), np.float32)

    def put(col, v):
        prm[:, col:col + 8] = np.asarray(v, np.float32).reshape(8, 128).T
    for l in range(DEPTH):
        b = l * PL
        put(b + P_F1, inp["ffn1_norm"][l]); put(b + P_MX, inp["mix_norm"][l]); put(b + P_F2, inp["ffn2_norm"][l])
        for j in range(4):
            put(b + P_CW + j * 8, inp["conv_w"][l][j])
        put(b + P_CB, inp["conv_b"][l]); put(b + P_BA, inp["rg_ba"][l]); put(b + P_BX, inp["rg_bx"][l])
        put(b + P_LAM, inp["rg_lam"][l])
    put(P_FIN, inp["final_norm"])
    return prm


def kernel(**inputs):
    inp = {k: np.asarray(v) for k, v in inputs.items()}
    x = inp["x"].astype(np.float32, copy=False)
    B = x.shape[0]
    nc = build_program()
    prm = pack_prm(inp)
    wmap = {n: np.ascontiguousarray(inp[n], dtype=np.float32) for n in WNAMES}
    in_maps = []
    for b in range(B):
        m = {"xT": np.ascontiguousarray(x[b].T), "prm": prm}
        m.update(wmap)
        in_maps.append(m)
    res = run_bass_kernel_spmd(nc, in_maps, core_ids=list(range(B)))
    out = np.stack([np.ascontiguousarray(res.results[b]["yT"].T) for b in range(B)], axis=0)
    return out.astype(np.float32, copy=False)
```

```python
import numpy as np
from contextlib import ExitStack
import concourse.bass as bass
import concourse.mybir as mybir
from concourse.bass_utils import run_bass_kernel_spmd

F32 = mybir.dt.float32
BF16 = mybir.dt.bfloat16
AF = mybir.ActivationFunctionType
ALU = mybir.AluOpType
AX = mybir.AxisListType

S = 2048
D = 1024
DFF = 2816
NT = 4
TT = 512
EPS = 1e-6
DEPTH = 2
N_IN = 5032
WARM_IDX = 1
WARM_BIS = 6
IT = 12
TOPK = 256
NEG = -30000.0
PL = 88
P_F1, P_MX, P_F2, P_CW, P_CB, P_BA, P_BX, P_LAM = 0, 8, 16, 24, 56, 64, 72, 80
P_FIN = DEPTH * PL
NP = P_FIN + 8
C_Q, C_K, C_V, C_QI, C_KI, C_WI, C_XR, C_GR, C_GA, C_GL = 0, 512, 576, 640, 896, 928, 936, 1960, 2984, 4008

SEM_ROLL = 30000
N_DMA_SEMS = 24


class Buf:
    __slots__ = ("name", "last_w", "readers")

    def __init__(self, name=""):
        self.name = name
        self.last_w = None
        self.readers = {}


class Ctx:
    def __init__(self, nc, stack):
        self.nc = nc
        self.stack = stack
        self.eng = {"pe": nc.tensor, "act": nc.scalar, "dve": nc.vector,
                    "pool": nc.gpsimd, "sp": nc.sync}
        self.sems = {}
        self.cur = {}
        self.cnt = {}
        self.gen = {}
        self.known = {e: {} for e in self.eng}
        for e in self.eng:
            self.gen[e] = 0
            self._new_sem(e)
        self.dma_sems = []
        for i in range(N_DMA_SEMS):
            key = f"dma{i}"
            self.sems[key] = stack.enter_context(nc.semaphore(key))
            self.dma_sems.append([key, 0])
        self.dma_rr = 0
        self.dma_rr2 = 0
        self.n_wait = 0
        self.n_ins = 0

    def _new_sem(self, e):
        key = f"{e}_{self.gen[e]}"
        self.gen[e] += 1
        self.sems[key] = self.stack.enter_context(self.nc.semaphore(key))
        self.cur[e] = key
        self.cnt[e] = 0

    def _wait(self, e, ev):
        key, val, src = ev
        if self.known[e].get(key, 0) >= val:
            return
        if not key.startswith("dma") and key == self.cur[src]:
            assert self.cnt[src] >= val, f"wait on future signal {key} {val} > {self.cnt[src]}"
        self.eng[e].wait_ge(self.sems[key], val)
        self.known[e][key] = val
        self.n_wait += 1

    def _deps(self, e, reads, writes):
        evs = []
        for b in reads:
            if b.last_w is not None:
                evs.append(b.last_w)
        for b in writes:
            if b.last_w is not None and b.last_w[2] != e:
                evs.append(b.last_w)
            for r, ev in b.readers.items():
                if r != e:
                    evs.append(ev)
        for ev in evs:
            self._wait(e, ev)

    def op(self, e, fn, reads=(), writes=(), sig=True):
        self._deps(e, reads, writes)
        ins = fn()
        self.n_ins += 1
        if sig:
            if self.cnt[e] >= SEM_ROLL:
                self._new_sem(e)
            self.cnt[e] += 1
            ins.then_inc(self.sems[self.cur[e]], 1)
            ev = (self.cur[e], self.cnt[e], e)
        else:
            ev = (self.cur[e], self.cnt[e] + 1, e)
        for b in reads:
            b.readers[e] = ev
        for b in writes:
            b.last_w = ev
            b.readers = {}
        return ins

    def dma(self, q, out, in_, reads=(), writes=(), **kw):
        self._deps(q, reads, writes)
        half = len(self.dma_sems) // 2
        if q == "pool":
            slot = self.dma_sems[self.dma_rr % half]
            self.dma_rr += 1
        else:
            slot = self.dma_sems[half + self.dma_rr2 % half]
            self.dma_rr2 += 1
        key, val = slot
        if val > 0:
            self._wait(q, (key, val, "dma"))
        ins = self.eng[q].dma_start(out=out, in_=in_, **kw)
        slot[1] = val + 16
        ins.then_inc(self.sems[key], 16)
        ev = (key, val + 16, "dma:" + q)
        for b in reads:
            b.readers["dma:" + q + key] = ev
        for b in writes:
            b.last_w = ev
            b.readers = {}
        self.n_ins += 1
        return ev

    def barrier_all(self):
        evs = [(self.cur[e], self.cnt[e], e) for e in self.eng if self.cnt[e] > 0]
        evs += [(key, val, "dma") for key, val in self.dma_sems if val > 0]
        for e in self.eng:
            for ev in evs:
                if ev[2] != e:
                    self._wait(e, ev)


class G:
    pass


def _grid(a, b):
    return [[Buf() for _ in range(b)] for _ in range(a)]


class Normer:
    def __init__(self, g, st, tag):
        nc = g.nc
        self.g = g
        self.sq = st.enter_context(nc.sbuf_tensor(f"sq{tag}", [128, 2, 8, TT], BF16))
        self.sqb = [Buf(), Buf()]
        self.rs = st.enter_context(nc.sbuf_tensor(f"rs{tag}", [128, 2, TT], F32))
        self.rsb = [Buf(), Buf()]
        self.ps = st.enter_context(nc.psum_tensor(f"psn{tag}", [128, 2, TT], F32))
        self.psb = [Buf(), Buf()]
        self.k = 0

    def tile(self, x_sb, xb, tt, out_fn):
        g = self.g
        cx, nc = g.cx, g.nc
        sq, rs, ps = self.sq, self.rs, self.ps
        s = self.k % 2
        self.k += 1
        tsl = slice(tt * TT, (tt + 1) * TT)
        for c in range(8):
            cx.op("act", lambda c=c: nc.scalar.activation(out=sq[:, s, c, :], in_=x_sb[:, c, tsl], func=AF.Square),
                  reads=[xb[c][tt]], writes=[self.sqb[s]], sig=(c == 7))
        for c in range(8):
            cx.op("pe", lambda c=c: nc.tensor.matmul(ps[:, s, :], lhsT=g.ones_bf[:, :], rhs=sq[:, s, c, :],
                                                      start=(c == 0), stop=(c == 7)),
                  reads=[self.sqb[s], g.constb], writes=[self.psb[s]], sig=(c == 7))
        cx.op("act", lambda: nc.scalar.activation(out=rs[:, s, :], in_=ps[:, s, :], func=AF.Sqrt,
                                                   scale=1.0 / D, bias=g.eps_c[:, 0:1]),
              reads=[self.psb[s], g.constb], writes=[self.rsb[s]])
        cx.op("dve", lambda: nc.vector.reciprocal(out=rs[:, s, :], in_=rs[:, s, :]),
              reads=[self.rsb[s]], writes=[self.rsb[s]])
        for c in range(8):
            out_fn(tt, c, rs[:, s, :], self.rsb[s])


def emit_norm(g, st, x_sb, xb, gcol, out_fn, tag):
    nm = Normer(g, st, tag)
    for tt in range(NT):
        nm.tile(x_sb, xb, tt, out_fn)


def norm_to_bf16(g, x_sb, xb, xn_sb, xnb, gcol):
    cx, nc = g.cx, g.nc

    def fn(tt, c, rs_ap, rs_buf):
        tsl = slice(tt * TT, (tt + 1) * TT)
        cx.op("dve", lambda: nc.vector.scalar_tensor_tensor(
            out=xn_sb[:, c, tsl], in0=x_sb[:, c, tsl], scalar=g.prm_sb[:, gcol + c:gcol + c + 1], in1=rs_ap,
            op0=ALU.mult, op1=ALU.mult),
            reads=[xb[c][tt], rs_buf, g.prmb], writes=[xnb[c][tt]])
    return fn


GROUPS = [(0, 6), (6, 12), (12, 17), (17, 22)]


class FfnBufs:
    def __init__(self, g, st, tag):
        nc = g.nc
        GM = 6
        self.wg_s = st.enter_context(nc.sbuf_tensor(f"wg_s{tag}", [128, 2, 8, GM * 128], BF16))
        self.wu_s = st.enter_context(nc.sbuf_tensor(f"wu_s{tag}", [128, 2, 8, GM * 128], BF16))
        self.wd_s = st.enter_context(nc.sbuf_tensor(f"wd_s{tag}", [128, 2, GM, D], BF16))
        self.wgb = [Buf(), Buf()]; self.wub = [Buf(), Buf()]; self.wdb = [Buf(), Buf()]
        self.a_s = st.enter_context(nc.sbuf_tensor(f"a_s{tag}", [128, 2, GM, TT], BF16))
        self.ab = [Buf(), Buf()]
        self.sg_s = st.enter_context(nc.sbuf_tensor(f"sg_s{tag}", [128, 2, TT], F32))
        self.sgb = [Buf(), Buf()]
        self.psg = st.enter_context(nc.psum_tensor(f"psg{tag}", [128, 2, TT], F32))
        self.psu = st.enter_context(nc.psum_tensor(f"psu{tag}", [128, 2, TT], F32))
        self.psy = st.enter_context(nc.psum_tensor(f"psy{tag}", [128, 2, TT], F32))
        self.psgb = [Buf(), Buf()]; self.psub = [Buf(), Buf()]; self.psyb = [Buf(), Buf()]
        self.gi = 0
        self.jj = 0
        self.yy = 0
        self.aa = 0


def emit_ffn(g, fb, x_sb, xb, xn_sb, xnb, wg, wu, wd, after_tile=None):
    cx, nc = g.cx, g.nc
    wg_s, wu_s, wd_s, a_s, sg_s, psg, psu, psy = fb.wg_s, fb.wu_s, fb.wd_s, fb.a_s, fb.sg_s, fb.psg, fb.psu, fb.psy
    wgb, wub, wdb, ab, sgb, psgb, psub, psyb = fb.wgb, fb.wub, fb.wdb, fb.ab, fb.sgb, fb.psgb, fb.psub, fb.psyb
    wgv = wg.rearrange("(kc p) n -> p kc n", p=128)
    wuv = wu.rearrange("(kc p) n -> p kc n", p=128)
    wdv = wd.rearrange("(j p) d -> p j d", p=128)
    slots = {}

    def load(i):
        c0, c1 = GROUPS[i]
        G_ = c1 - c0
        s = fb.gi % 2
        fb.gi += 1
        slots[i] = s
        cx.dma("pool", wg_s[:, s, :, 0:G_ * 128], wgv[:, :, c0 * 128:c1 * 128], writes=[wgb[s]])
        cx.dma("pool", wu_s[:, s, :, 0:G_ * 128], wuv[:, :, c0 * 128:c1 * 128], writes=[wub[s]])
        cx.dma("pool", wd_s[:, s, 0:G_, :], wdv[:, c0:c1, :], writes=[wdb[s]])

    load(0)
    for gi, (c0, c1) in enumerate(GROUPS):
        G_ = c1 - c0
        if gi + 1 < len(GROUPS):
            load(gi + 1)
        s = slots[gi]
        for tt in range(NT):
            tsl = slice(tt * TT, (tt + 1) * TT)
            sa = fb.aa % 2
            fb.aa += 1
            for j in range(G_):
                p = fb.jj % 2
                fb.jj += 1
                for kc in range(8):
                    cx.op("pe", lambda kc=kc: nc.tensor.matmul(
                        psg[:, p, :], lhsT=wg_s[:, s, kc, j * 128:(j + 1) * 128], rhs=xn_sb[:, kc, tsl],
                        start=(kc == 0), stop=(kc == 7)),
                        reads=[wgb[s], xnb[kc][tt]], writes=[psgb[p]], sig=(kc == 7))
                for kc in range(8):
                    cx.op("pe", lambda kc=kc: nc.tensor.matmul(
                        psu[:, p, :], lhsT=wu_s[:, s, kc, j * 128:(j + 1) * 128], rhs=xn_sb[:, kc, tsl],
                        start=(kc == 0), stop=(kc == 7)),
                        reads=[wub[s], xnb[kc][tt]], writes=[psub[p]], sig=(kc == 7))
                cx.op("act", lambda: nc.scalar.activation(out=sg_s[:, p, :], in_=psg[:, p, :], func=AF.Silu),
                      reads=[psgb[p]], writes=[sgb[p]])
                cx.op("dve", lambda: nc.vector.tensor_tensor(out=a_s[:, sa, j, :], in0=psu[:, p, :],
                                                             in1=sg_s[:, p, :], op=ALU.mult),
                      reads=[psub[p], sgb[p]], writes=[ab[sa]])
            for dc in range(8):
                p = fb.yy % 2
                fb.yy += 1
                for j in range(G_):
                    cx.op("pe", lambda j=j: nc.tensor.matmul(
                        psy[:, p, :], lhsT=wd_s[:, s, j, dc * 128:(dc + 1) * 128], rhs=a_s[:, sa, j, :],
                        start=(j == 0), stop=(j == G_ - 1)),
                        reads=[wdb[s], ab[sa]], writes=[psyb[p]], sig=(j == G_ - 1))
                cx.op("dve", lambda: nc.vector.scalar_tensor_tensor(
                    out=x_sb[:, dc, tsl], in0=psy[:, p, :], scalar=0.5, in1=x_sb[:, dc, tsl],
                    op0=ALU.mult, op1=ALU.add),
                    reads=[psyb[p], xb[dc][tt]], writes=[xb[dc][tt]])
            if after_tile is not None and gi == len(GROUPS) - 1:
                after_tile(tt)


def emit_mixer(g, l, W, have_xn=False):
    cx, nc = g.cx, g.nc
    pb = l * PL
    w_in_v = W["w_in"][l].rearrange("(kc p) n -> p kc n", p=128)
    xres_v = g.xres.rearrange("(c p) t -> p c t", p=128)
    with ExitStack() as ms:
        xn_sb, xnb = g.xn_sb, g.xnb
        x_sb, xb = g.x_sb, g.xb
        if not have_xn:
            with ExitStack() as st:
                for tt in range(NT):
                    cx.dma("sp", x_sb[:, :, tt * TT:(tt + 1) * TT], xres_v[:, :, tt * TT:(tt + 1) * TT],
                           reads=[g.xresb[c][tt] for c in range(8)], writes=[xb[c][tt] for c in range(8)])
                emit_norm(g, st, x_sb, xb, pb + P_MX, norm_to_bf16(g, x_sb, xb, xn_sb, xnb, pb + P_MX), f"m{l}")
        cx.barrier_all()
        if getattr(g, "stop", 9) == 0:
            return
        attT = ms.enter_context(nc.sbuf_tensor(f"attT{l}", [128, 4, S], BF16))
        attTb = [Buf() for _ in range(16)]
        with ExitStack() as st:
            qT = st.enter_context(nc.sbuf_tensor(f"qT{l}", [128, 4, S], BF16)); qTb = Buf()
            kT2 = st.enter_context(nc.sbuf_tensor(f"kT2{l}", [128, 2, S], BF16)); kT2b = Buf()
            qiT3 = st.enter_context(nc.sbuf_tensor(f"qiT3{l}", [128, 3, S], BF16)); qiT3b = Buf()
            kiT3 = st.enter_context(nc.sbuf_tensor(f"kiT3{l}", [128, 3, S], BF16)); kiT3b = Buf()
            vaug2 = st.enter_context(nc.sbuf_tensor(f"vaug{l}", [128, 16, 2, 128], BF16)); vb = Buf()
            wi_sb = st.enter_context(nc.sbuf_tensor(f"wi{l}", [128, 16, 8], F32)); wib = Buf()
            wabs = st.enter_context(nc.sbuf_tensor(f"wabs{l}", [128, 16, 8], F32)); wabsb = Buf()
            wsgn = st.enter_context(nc.sbuf_tensor(f"wsgn{l}", [128, 16, 8], F32)); wsgnb = Buf()
            with ExitStack() as s1:
                wt = s1.enter_context(nc.sbuf_tensor(f"m1w{l}", [128, 3, 8, 128], BF16))
                wtb = [Buf(), Buf(), Buf()]
                ps1 = s1.enter_context(nc.psum_tensor(f"m1ps{l}", [128, 2, TT], F32))
                ps1b = [Buf(), Buf()]
                psv = s1.enter_context(nc.psum_tensor(f"m1pv{l}", [128, 2, TT], F32))
                psvb = [Buf(), Buf()]
                chunks = []
                for c in range(4):
                    chunks.append(([(C_Q + c * 128, 128, 0)], 128, qT[:, c, :], qTb))
                chunks.append(([(C_K, 64, 0), (C_K, 64, 64)], 128, "K", kT2b))
                cx.op("dve", lambda: nc.vector.memset(kT2[:, :, :], 0.0), writes=[kT2b])
                chunks.append(([(C_QI, 96, 0)], 96, qiT3[0:96, 0, :], qiT3b))
                chunks.append(([(C_QI + 96, 96, 0)], 96, qiT3[0:96, 1, :], qiT3b))
                chunks.append(([(C_QI + 192, 64, 0)], 64, qiT3[0:64, 2, :], qiT3b))
                chunks.append(([(C_KI, 32, 0), (C_KI, 32, 32), (C_KI, 32, 64)], 96, "KI", kiT3b))
                cx.op("dve", lambda: nc.vector.memset(kiT3[:, :, :], 0.0), writes=[kiT3b])
                cx.op("dve", lambda: nc.vector.memset(qiT3[:, :, :], 0.0), writes=[qiT3b])
                chunks.append(([(C_V, 64, 0), (C_KI, 40, 64)], 104, None, None))

                def loadw(i):
                    cols, M, _, _ = chunks[i]
                    s = i % 3
                    for (c0, n, off) in cols:
                        cx.dma("pool", wt[:, s, :, off:off + n], w_in_v[:, :, c0:c0 + n], writes=[wtb[s]])
                loadw(0)
                loadw(1)
                pp = 0
                for i, (cols, M, dst, dstb) in enumerate(chunks):
                    s = i % 3
                    if i + 2 < len(chunks):
                        loadw(i + 2)
                    if dst is not None:
                        for tt in range(NT):
                            tsl = slice(tt * TT, (tt + 1) * TT)
                            p = pp % 2
                            pp += 1
                            for kc in range(8):
                                cx.op("pe", lambda kc=kc: nc.tensor.matmul(
                                    ps1[0:M, p, :], lhsT=wt[:, s, kc, 0:M], rhs=xn_sb[:, kc, tsl],
                                    start=(kc == 0), stop=(kc == 7)),
                                    reads=[wtb[s], xnb[kc][tt]], writes=[ps1b[p]], sig=(kc == 7))
                            if dst == "KI":
                                for v in range(3):
                                    cx.op("dve", lambda v=v: nc.vector.tensor_copy(out=kiT3[32 * v:32 * v + 32, v, tsl],
                                                                                   in_=ps1[32 * v:32 * v + 32, p, :]),
                                          reads=[ps1b[p]], writes=[dstb])
                            elif isinstance(dst, str):
                                cx.op("dve", lambda: nc.vector.tensor_copy(out=kT2[0:64, 0, tsl], in_=ps1[0:64, p, :]),
                                      reads=[ps1b[p]], writes=[dstb])
                                cx.op("dve", lambda: nc.vector.tensor_copy(out=kT2[64:128, 1, tsl], in_=ps1[64:128, p, :]),
                                      reads=[ps1b[p]], writes=[dstb])
                            elif pp % 2 == 0:
                                cx.op("act", lambda: nc.scalar.copy(out=dst[:, tsl], in_=ps1[0:M, p, :]),
                                      reads=[ps1b[p]], writes=[dstb])
                            else:
                                cx.op("dve", lambda: nc.vector.tensor_copy(out=dst[:, tsl], in_=ps1[0:M, p, :]),
                                      reads=[ps1b[p]], writes=[dstb])
                    else:
                        cx.op("dve", lambda: nc.vector.memset(vaug2[:, :, :, :], 1.0), writes=[vb])
                        for tb in range(16):
                            p = tb % 2
                            for kc in range(8):
                                cx.op("pe", lambda kc=kc: nc.tensor.matmul(
                                    psv[:, p, 0:104], lhsT=xn_sb[:, kc, tb * 128:(tb + 1) * 128], rhs=wt[:, s, kc, 0:104],
                                    start=(kc == 0), stop=(kc == 7)),
                                    reads=[wtb[s], xnb[kc][tb // 4]], writes=[psvb[p]], sig=(kc == 7))
                            cx.op("dve", lambda: nc.vector.tensor_copy(out=vaug2[:, tb, 0, 0:64], in_=psv[:, p, 0:64]),
                                  reads=[psvb[p]], writes=[vb])
                            cx.op("dve", lambda: nc.vector.tensor_copy(out=vaug2[:, tb, 1, 64:128], in_=psv[:, p, 0:64]),
                                  reads=[psvb[p]], writes=[vb])
                            cx.op("dve", lambda: nc.vector.tensor_copy(out=wi_sb[:, tb, :], in_=psv[:, p, 96:104]),
                                  reads=[psvb[p]], writes=[wib])
                        cx.op("dve", lambda: nc.vector.scalar_tensor_tensor(
                            out=wabs[:, :, :], in0=wi_sb[:, :, :], scalar=-1.0, in1=wi_sb[:, :, :],
                            op0=ALU.mult, op1=ALU.max), reads=[wib], writes=[wabsb])
                        cx.op("act", lambda: nc.scalar.activation(out=wsgn[:, :, :], in_=wi_sb[:, :, :], func=AF.Sign),
                              reads=[wib], writes=[wsgnb])
                cx.barrier_all()
            if getattr(g, "stop", 9) == 1:
                return
            with ExitStack() as s2:
                score = g.x_sb; scb = [Buf() for _ in range(4)]
                Rt = s2.enter_context(nc.sbuf_tensor(f"Rt{l}", [128, 3, TT], F32)); Rb = [Buf(), Buf(), Buf()]
                bias = s2.enter_context(nc.sbuf_tensor(f"bias{l}", [128, 4, S], BF16)); biasb = [Buf() for _ in range(4)]
                _bv = g.x_sb.bitcast(BF16)[:, 4:8, :].rearrange("p a (k q) -> p a k q", q=TT)

                def bT(par, kb, lo, hi):
                    return _bv[:, par * 2 + kb // 8, kb % 8, lo:hi]
                biasTb = [[Buf() for _ in range(16)] for _ in range(2)]
                PT = s2.enter_context(nc.sbuf_tensor(f"PT{l}", [128, 3, TT], BF16)); PTb = [Buf(), Buf(), Buf()]
                sm = s2.enter_context(nc.sbuf_tensor(f"sm{l}", [128, 2, 64, 4], F32)); smb = [Buf(), Buf()]
                rdn = s2.enter_context(nc.sbuf_tensor(f"rdn{l}", [128, 2, TT], F32)); rdnb = [Buf(), Buf()]
                smm = s2.enter_context(nc.sbuf_tensor(f"smm{l}", [128, 2, 4], F32)); smmb = [Buf(), Buf()]
                sma = s2.enter_context(nc.sbuf_tensor(f"sma{l}", [128, 2, 4], F32)); smab = [Buf(), Buf()]
                ACTJ = (0, 3)
                junk, junkb = bias[:, 0, :], biasb[0]
                junkA, junkAb = bias[:, 1, :], biasb[1]
                psL = s2.enter_context(nc.psum_tensor(f"psL{l}", [128, 2, TT], F32)); psLb = [Buf(), Buf()]
                accP = s2.enter_context(nc.psum_tensor(f"accP{l}", [128, TT], F32)); accPb = Buf()
                psS = s2.enter_context(nc.psum_tensor(f"psS{l}", [128, 2, TT], F32)); psSb = [Buf(), Buf()]
                psO = s2.enter_context(nc.psum_tensor(f"psO{l}", [128, 2, TT], F32)); psOb = [Buf(), Buf()]
                psB = s2.enter_context(nc.psum_tensor(f"psB{l}", [128, TT], F32)); psBb = Buf()
                st_ = {"L": 0, "R": 0, "S": 0, "PT": 0, "O": 0}
                V = nc.vector

                def warm(n):
                    for _ in range(n):
                        cx.op("pe", lambda: nc.tensor.matmul(psB[:, :], lhsT=g.ident[:, :], rhs=qT[:, 0, 0:TT], start=True, stop=True),
                              reads=[g.constb, qTb], writes=[psBb], sig=False)
                cx.op("dve", lambda: V.memset(sm[:, :, :, :], 0.0), writes=smb)

                def indexer(qb, j):
                    n = (qb + 1) * 128
                    qsl = slice(qb * 128, (qb + 1) * 128)
                    for kt in range((n + TT - 1) // TT):
                        w = min(TT, n - kt * TT)
                        for h in range(8):
                            jj, po = h // 3, (h % 3) * 32
                            pL = st_["L"] % 2; st_["L"] += 1
                            r = st_["R"] % 3; st_["R"] += 1
                            cx.op("pe", lambda: nc.tensor.matmul(
                                psL[:, pL, 0:w], lhsT=qiT3[:, jj, qsl], rhs=kiT3[:, h % 3, kt * TT:kt * TT + w],
                                start=True, stop=True), reads=[qiT3b, kiT3b], writes=[psLb[pL]])
                            warm(WARM_IDX)
                            cx.op("act", lambda: nc.scalar.activation(
                                out=Rt[:, r, 0:w], in_=psL[:, pL, 0:w], func=AF.Relu, scale=wabs[:, qb, h:h + 1]),
                                reads=[psLb[pL], wabsb], writes=[Rb[r]])
                            if h == 0:
                                cx.op("dve", lambda: V.tensor_scalar(
                                    out=accP[:, 0:w], in0=Rt[:, r, 0:w], scalar1=wsgn[:, qb, 0:1], scalar2=None,
                                    op0=ALU.mult), reads=[Rb[r], wsgnb], writes=[accPb])
                            elif h < 7:
                                cx.op("dve", lambda: V.scalar_tensor_tensor(
                                    out=accP[:, 0:w], in0=Rt[:, r, 0:w], scalar=wsgn[:, qb, h:h + 1], in1=accP[:, 0:w],
                                    op0=ALU.mult, op1=ALU.add), reads=[Rb[r], wsgnb, accPb], writes=[accPb])
                            else:
                                cx.op("dve", lambda: V.scalar_tensor_tensor(
                                    out=score[:, j, kt * TT:kt * TT + w], in0=Rt[:, r, 0:w], scalar=wsgn[:, qb, h:h + 1],
                                    in1=accP[:, 0:w], op0=ALU.mult, op1=ALU.add),
                                    reads=[Rb[r], wsgnb, accPb], writes=[scb[j]])
                            yield
                    cx.op("dve", lambda: V.tensor_tensor(
                        out=score[:, j, qsl], in0=score[:, j, qsl], in1=g.negm[:, :], op=ALU.add),
                        reads=[scb[j], g.constb], writes=[scb[j]])

                def bisect(sb):
                    s = sb % 2
                    B = smb[s]
                    Bm, Ba = smmb[s], smab[s]
                    js = [j for j in range(4) if 4 * sb + j >= 2]
                    actj = [j for j in js if j in ACTJ]

                    def sv(fn, extra_r=(), extra_w=()):
                        cx.op("dve", fn, reads=[B] + list(extra_r), writes=[B] + list(extra_w))
                    for j in range(4):
                        n = (4 * sb + j + 1) * 128
                        val = (2.0 * TOPK - 1.0 - n) if j in actj else (TOPK - 0.5)
                        sv(lambda: V.memset(sm[:, s, 6, j:j + 1], val))
                    for j in js:
                        qb = 4 * sb + j
                        n = (qb + 1) * 128
                        sv(lambda: V.tensor_reduce(out=sm[:, s, 0, j:j + 1], in_=score[:, j, 0:n], axis=AX.X, op=ALU.max), extra_r=[scb[j]])
                        sv(lambda: V.tensor_reduce(out=sm[:, s, 1, j:j + 1], in_=score[:, j, 0:qb * 128], axis=AX.X, op=ALU.min), extra_r=[scb[j]])
                    sv(lambda: V.tensor_tensor(out=sm[:, s, 2, :], in0=sm[:, s, 0, :], in1=sm[:, s, 1, :], op=ALU.subtract))
                    sv(lambda: V.tensor_tensor(out=sm[:, s, 8:8 + IT, :], in0=g.pw3[:, 0:IT, :],
                                               in1=sm[:, s, 2:3, :].to_broadcast([128, IT, 4]), op=ALU.mult), extra_r=[g.constb])
                    for i in range(IT):
                        sv(lambda: V.tensor_tensor(out=sm[:, s, 5, :], in0=sm[:, s, 1, :], in1=sm[:, s, 8 + i, :], op=ALU.add))
                        warm(WARM_BIS)
                        if actj:
                            sv(lambda: V.scalar_tensor_tensor(out=smm[:, s, :], in0=sm[:, s, 1, :], scalar=-1.0, in1=sm[:, s, 8 + i, :],
                                                              op0=ALU.mult, op1=ALU.subtract), extra_w=[Bm])
                        for j in actj:
                            n = (4 * sb + j + 1) * 128
                            cx.op("act", lambda: nc.scalar.activation(
                                out=junkA[:, 0:n], in_=score[:, j, 0:n], func=AF.Sign, bias=smm[:, s, j:j + 1], scale=1.0,
                                accum_out=sma[:, s, j:j + 1]), reads=[Bm, scb[j]], writes=[Ba, junkAb])
                        for j in js:
                            if j in actj:
                                continue
                            n = (4 * sb + j + 1) * 128
                            sv(lambda: V.tensor_scalar(out=junk[:, 0:n], in0=score[:, j, 0:n], scalar1=sm[:, s, 5, j:j + 1],
                                                       scalar2=None, op0=ALU.is_ge, op1=ALU.add, accum_out=sm[:, s, 3, j:j + 1]),
                               extra_r=[scb[j]], extra_w=[junkb])
                        for j in actj:
                            sv(lambda: V.tensor_copy(out=sm[:, s, 3, j:j + 1], in_=sma[:, s, j:j + 1]), extra_r=[Ba])
                        sv(lambda: V.tensor_tensor(out=sm[:, s, 4, :], in0=sm[:, s, 3, :], in1=sm[:, s, 6, :], op=ALU.is_ge))
                        sv(lambda: V.tensor_tensor(out=sm[:, s, 4, :], in0=sm[:, s, 4, :], in1=sm[:, s, 8 + i, :], op=ALU.mult))
                        sv(lambda: V.tensor_tensor(out=sm[:, s, 1, :], in0=sm[:, s, 1, :], in1=sm[:, s, 4, :], op=ALU.add))
                        yield
                    for j in js:
                        n = (4 * sb + j + 1) * 128
                        cx.op("dve", lambda: V.tensor_scalar(out=bias[:, j, 0:n], in0=score[:, j, 0:n], scalar1=sm[:, s, 1, j:j + 1],
                                                             scalar2=NEG, op0=ALU.is_lt, op1=ALU.mult),
                              reads=[B, scb[j]], writes=[biasb[j]])
                    for j in range(4):
                        qb = 4 * sb + j
                        if qb < 2:
                            if qb > 0:
                                cx.op("dve", lambda: V.memset(bias[:, j, 0:qb * 128], 0.0), writes=[biasb[j]])
                            cx.op("dve", lambda: V.tensor_copy(out=bias[:, j, qb * 128:(qb + 1) * 128], in_=g.negb[:, :]),
                                  reads=[g.constb], writes=[biasb[j]])

                def make_biasT(sb):
                    par = sb % 2
                    for kb in range(4 * sb + 4):
                        j0 = max(0, kb - 4 * sb)
                        for j in range(j0, 4):
                            cx.op("pe", lambda: nc.tensor.matmul(
                                psB[:, j * 128:(j + 1) * 128], lhsT=bias[:, j, kb * 128:(kb + 1) * 128], rhs=g.ident[:, :],
                                start=(j == j0), stop=(j == 3), skip_group_check=True),
                                reads=[biasb[j], g.constb], writes=[psBb], sig=(j == 3))
                        cx.op("act", lambda: nc.scalar.copy(out=bT(par, kb, j0 * 128, TT), in_=psB[:, j0 * 128:TT]),
                              reads=[psBb], writes=[biasTb[par][kb]])

                def attention(sb):
                    par = sb % 2
                    nk = 4 * sb + 4
                    items = [(h, kb) for h in range(8) for kb in range(nk)]
                    slots = {}

                    def emit_sb(i):
                        h, kb = items[i]
                        c, hp = h // 2, (h % 2) * 64
                        qo = max(0, kb - 4 * sb) * 128
                        qs = slice(sb * TT + qo, (sb + 1) * TT)
                        ksl = slice(kb * 128, (kb + 1) * 128)
                        pS = st_["S"] % 2; st_["S"] += 1
                        slots[i] = pS
                        cx.op("pe", lambda: nc.tensor.matmul(
                            psS[:, pS, qo:TT], lhsT=kT2[:, h % 2, ksl], rhs=qT[:, c, qs],
                            start=True, stop=False), reads=[kT2b, qTb], writes=[psSb[pS]], sig=False)
                        cx.op("pe", lambda: nc.tensor.matmul(
                            psS[:, pS, qo:TT], lhsT=g.ident[:, :], rhs=bT(par, kb, qo, TT),
                            start=False, stop=True), reads=[biasTb[par][kb], g.constb], writes=[psSb[pS]])
                    emit_sb(0)
                    po = None
                    for i, (h, kb) in enumerate(items):
                        c = h // 2
                        if kb == 0:
                            po = st_["O"] % 2; st_["O"] += 1
                        qo = max(0, kb - 4 * sb) * 128
                        pS = slots.pop(i)
                        pt = st_["PT"] % 3; st_["PT"] += 1
                        cx.op("act", lambda: nc.scalar.activation(
                            out=PT[:, pt, qo:TT], in_=psS[:, pS, qo:TT], func=AF.Exp, scale=0.125),
                            reads=[psSb[pS]], writes=[PTb[pt]])
                        if i + 1 < len(items):
                            emit_sb(i + 1)
                        cx.op("pe", lambda: nc.tensor.matmul(
                            psO[:, po, qo:TT], lhsT=vaug2[:, kb, h % 2, :], rhs=PT[:, pt, qo:TT],
                            start=(kb == 0), stop=(kb == nk - 1), skip_group_check=True),
                            reads=[PTb[pt], vb], writes=[psOb[po]], sig=(kb == nk - 1))
                        if kb == nk - 1:
                            osl = slice(sb * TT, (sb + 1) * TT)
                            if h % 2 == 0:
                                num, den, dst_p = slice(0, 64), slice(64, 128), slice(0, 64)
                            else:
                                num, den, dst_p = slice(64, 128), slice(0, 64), slice(64, 128)
                            cx.op("act", lambda: nc.scalar.activation(out=rdn[dst_p, po, :], in_=psO[den, po, :], func=AF.Ln),
                                  reads=[psOb[po]], writes=[rdnb[po]])
                            cx.op("act", lambda: nc.scalar.activation(out=rdn[dst_p, po, :], in_=rdn[dst_p, po, :], func=AF.Exp, scale=-1.0),
                                  reads=[rdnb[po]], writes=[rdnb[po]])
                            cx.op("dve", lambda: V.tensor_tensor(out=attT[dst_p, c, osl], in0=psO[num, po, :], in1=rdn[dst_p, po, :],
                                                                 op=ALU.mult),
                                  reads=[psOb[po], rdnb[po]], writes=attTb[4 * sb:4 * sb + 4])
                        yield

                def run_all(gen):
                    for _ in gen:
                        pass

                def chain(gens):
                    for gg in gens:
                        for _ in gg:
                            yield

                def interleave(main, n_main, side, n_side):
                    done = 0
                    k = 0
                    for _ in main:
                        k += 1
                        tgt = (n_side * k) // max(n_main, 1)
                        while done < tgt:
                            next(side, None)
                            done += 1

                for sb in range(4):
                    qbs = [4 * sb + j for j in range(4) if 4 * sb + j >= 2]
                    n_units = sum(((qb + 1) * 128 + TT - 1) // TT * 8 for qb in qbs)
                    ig = chain([indexer(qb, qb - 4 * sb) for qb in qbs])
                    bg = bisect(sb)
                    if sb == 0:
                        run_all(ig)
                        run_all(bg)
                    else:
                        ag = attention(sb - 1)
                        n_items = 8 * 4 * sb
                        interleave(ig, n_units, ag, n_items // 2)
                        interleave(bg, IT, ag, n_items - n_items // 2)
                        run_all(ag)
                    make_biasT(sb)
                run_all(attention(3))
                cx.barrier_all()

        if getattr(g, "stop", 9) == 2:
            return
        rnnT = ms.enter_context(nc.sbuf_tensor(f"rnnT{l}", [128, 8, S], BF16))
        rnnb = [Buf() for _ in range(8)]
        with ExitStack() as st:
            T0, T1, T2, T3 = [g.x_sb[:, 2 * i_:2 * i_ + 2, :] for i_ in range(4)]
            gl32 = st.enter_context(nc.sbuf_tensor(f"gl32{l}", [128, 2, S], F32)); gl32b = [Buf(), Buf()]
            T0b, T1b, T2b, T3b = [[Buf(), Buf()] for _ in range(4)]
            xrb = st.enter_context(nc.sbuf_tensor(f"xrb{l}", [128, 2, S + 8], BF16)); xrbb = [Buf(), Buf()]
            xrb2 = st.enter_context(nc.sbuf_tensor(f"xrc{l}", [128, 2, S + 8], BF16)); xrb2b = [Buf(), Buf()]
            xcb = st.enter_context(nc.sbuf_tensor(f"xcb{l}", [128, 2, S], BF16)); xcbb = [Buf(), Buf()]
            wxr = st.enter_context(nc.sbuf_tensor(f"wxr{l}", [128, 2, 8, 128], BF16)); wxrb = [Buf(), Buf()]
            wgr = st.enter_context(nc.sbuf_tensor(f"wgr{l}", [128, 2, 8, 128], BF16)); wgrb = [Buf(), Buf()]
            dg = st.enter_context(nc.sbuf_tensor(f"dg{l}", [128, 2, 4, 128], BF16)); dgb = [Buf(), Buf()]
            bdA = st.enter_context(nc.sbuf_tensor(f"bdA{l}", [128, 8, 128], BF16)); bdAb = Buf()
            bdX = st.enter_context(nc.sbuf_tensor(f"bdX{l}", [128, 8, 128], BF16)); bdXb = Buf()
            nsp = st.enter_context(nc.sbuf_tensor(f"nsp{l}", [128, 3, 8], F32)); nspb = Buf()
            ps3 = st.enter_context(nc.psum_tensor(f"ps3{l}", [128, 2, TT], F32)); ps3b = [Buf(), Buf()]
            psc = st.enter_context(nc.psum_tensor(f"psc{l}", [128, 2, TT], F32)); pscb = [Buf(), Buf()]
            psa = st.enter_context(nc.psum_tensor(f"psa{l}", [128, 2, TT], F32)); psab = [Buf(), Buf()]
            psx = st.enter_context(nc.psum_tensor(f"psx{l}", [128, 2, TT], F32)); psxb = [Buf(), Buf()]
            V = nc.vector
            cx.op("dve", lambda: V.memset(bdA[:, :, :], 0.0), writes=[bdAb])
            cx.op("dve", lambda: V.memset(bdX[:, :, :], 0.0), writes=[bdXb])
            cx.op("dve", lambda: V.memset(xrb[:, :, 0:8], 0.0), writes=xrbb)
            cx.op("dve", lambda: V.memset(xrb2[:, :, 0:8], 0.0), writes=xrb2b)
            for half in range(2):
                hs = slice(half * 64, (half + 1) * 64)
                srcA = W["rg_wa"][l].rearrange("(c two) i j -> two i c j", two=2)[half]
                srcX = W["rg_wx"][l].rearrange("(c two) i j -> two i c j", two=2)[half]
                cx.dma("pool", bdA[hs, :, hs], srcA, writes=[bdAb])
                cx.dma("pool", bdX[hs, :, hs], srcX, writes=[bdXb])
            lam = g.prm_sb[:, pb + P_LAM:pb + P_LAM + 8]
            cx.op("act", lambda: nc.scalar.activation(out=nsp[:, 0, :], in_=lam, func=AF.Exp, scale=-1.0),
                  reads=[g.prmb], writes=[nspb])
            cx.op("act", lambda: nc.scalar.activation(out=nsp[:, 0, :], in_=nsp[:, 0, :], func=AF.Ln, scale=1.0, bias=g.one_c[:, 0:1]),
                  reads=[nspb, g.constb], writes=[nspb])
            cx.op("dve", lambda: V.tensor_scalar(out=nsp[:, 1, :], in0=nsp[:, 0, :], scalar1=-8.0, scalar2=None, op0=ALU.mult),
                  reads=[nspb], writes=[nspb])
            cx.op("dve", lambda: V.tensor_scalar(out=nsp[:, 2, :], in0=nsp[:, 0, :], scalar1=-16.0, scalar2=None, op0=ALU.mult),
                  reads=[nspb], writes=[nspb])

            def load3(c):
                s = c % 2
                cx.dma("pool", wxr[:, s, :, :], w_in_v[:, :, C_XR + c * 128:C_XR + (c + 1) * 128], writes=[wxrb[s]])
                cx.dma("pool", wgr[:, s, :, :], w_in_v[:, :, C_GR + c * 128:C_GR + (c + 1) * 128], writes=[wgrb[s]])
            k3 = [0]

            def stageA(c):
                s = c % 2
                pcol = lambda base: g.prm_sb[:, pb + base + c:pb + base + c + 1]
                for j in range(4):
                    cx.op("act", lambda j=j: nc.scalar.activation(
                        out=dg[:, s, j, :], in_=g.ident[:, :], func=AF.Identity,
                        scale=g.prm_sb[:, pb + P_CW + j * 8 + c:pb + P_CW + j * 8 + c + 1]),
                        reads=[g.constb, g.prmb], writes=[dgb[s]])
                for tt in range(NT):
                    tsl = slice(tt * TT, (tt + 1) * TT)
                    p = k3[0] % 2; k3[0] += 1
                    for kc in range(8):
                        cx.op("pe", lambda kc=kc: nc.tensor.matmul(
                            ps3[:, p, :], lhsT=wxr[:, s, kc, :], rhs=xn_sb[:, kc, tsl], start=(kc == 0), stop=(kc == 7)),
                            reads=[wxrb[s], xnb[kc][tt]], writes=[ps3b[p]], sig=(kc == 7))
                    cx.op("dve", lambda: V.tensor_copy(out=xrb[:, s, 4 + tt * TT:4 + (tt + 1) * TT], in_=ps3[:, p, :]),
                          reads=[ps3b[p]], writes=[xrbb[s]])
                    cx.op("dve", lambda: V.tensor_copy(out=xrb2[:, s, 5 + tt * TT:5 + (tt + 1) * TT], in_=ps3[:, p, :]),
                          reads=[ps3b[p]], writes=[xrb2b[s]])
                for tt in range(NT):
                    tsl = slice(tt * TT, (tt + 1) * TT)
                    p = tt % 2
                    for j in range(4):
                        if j % 2 == 1:
                            rhs_ap = xrb[:, s, tt * TT + 1 + j:tt * TT + 1 + j + TT]
                        else:
                            rhs_ap = xrb2[:, s, tt * TT + 2 + j:tt * TT + 2 + j + TT]
                        cx.op("pe", lambda j=j, rhs_ap=rhs_ap: nc.tensor.matmul(
                            psc[:, p, :], lhsT=dg[:, s, j, :], rhs=rhs_ap,
                            start=(j == 0), stop=(j == 3)), reads=[dgb[s], xrbb[s], xrb2b[s]], writes=[pscb[p]], sig=(j == 3))
                    cx.op("dve", lambda: V.tensor_scalar(out=T0[:, s, tsl], in0=psc[:, p, :], scalar1=pcol(P_CB),
                                                         scalar2=None, op0=ALU.add),
                          reads=[pscb[p], g.prmb], writes=[T0b[s]])
                    cx.op("dve", lambda: V.tensor_scalar(out=xcb[:, s, tsl], in0=psc[:, p, :], scalar1=pcol(P_CB),
                                                         scalar2=None, op0=ALU.add),
                          reads=[pscb[p], g.prmb], writes=[xcbb[s]])
            def stageA2(c):
                s = c % 2
                pcol = lambda base: g.prm_sb[:, pb + base + c:pb + base + c + 1]
                for tt in range(NT):
                    tsl = slice(tt * TT, (tt + 1) * TT)
                    p = tt % 2
                    cx.op("pe", lambda: nc.tensor.matmul(psa[:, p, :], lhsT=bdA[:, c, :], rhs=xcb[:, s, tsl], start=True, stop=True),
                          reads=[bdAb, xcbb[s]], writes=[psab[p]])
                    cx.op("act", lambda: nc.scalar.activation(out=T1[:, s, tsl], in_=psa[:, p, :], func=AF.Sigmoid,
                                                               bias=pcol(P_BA), scale=1.0),
                          reads=[psab[p], g.prmb], writes=[T1b[s]])
                    cx.op("pe", lambda: nc.tensor.matmul(psx[:, p, :], lhsT=bdX[:, c, :], rhs=xcb[:, s, tsl], start=True, stop=True),
                          reads=[bdXb, xcbb[s]], writes=[psxb[p]])
                    cx.op("act", lambda: nc.scalar.activation(out=T2[:, s, tsl], in_=psx[:, p, :], func=AF.Sigmoid,
                                                               bias=pcol(P_BX), scale=1.0),
                          reads=[psxb[p], g.prmb], writes=[T2b[s]])

            def stageG(c):
                s = c % 2
                for tt in range(NT):
                    tsl = slice(tt * TT, (tt + 1) * TT)
                    p = k3[0] % 2; k3[0] += 1
                    for kc in range(8):
                        cx.op("pe", lambda kc=kc: nc.tensor.matmul(
                            ps3[:, p, :], lhsT=wgr[:, s, kc, :], rhs=xn_sb[:, kc, tsl], start=(kc == 0), stop=(kc == 7)),
                            reads=[wgrb[s], xnb[kc][tt]], writes=[ps3b[p]], sig=(kc == 7))
                    cx.op("act", lambda: nc.scalar.activation(out=gl32[:, s, tsl], in_=ps3[:, p, :], func=AF.Gelu_apprx_tanh),
                          reads=[ps3b[p]], writes=[gl32b[s]])

            def stageB(c):
                s = c % 2
                A = nc.scalar.activation
                cx.op("act", lambda: A(out=T3[:, s, :], in_=T1[:, s, :], func=AF.Exp, scale=nsp[:, 1, c:c + 1]),
                      reads=[T1b[s], nspb], writes=[T3b[s]])
                cx.op("act", lambda: A(out=T1[:, s, :], in_=T1[:, s, :], func=AF.Exp, scale=nsp[:, 2, c:c + 1]),
                      reads=[T1b[s], nspb], writes=[T1b[s]])
                cx.op("act", lambda: A(out=T1[:, s, :], in_=T1[:, s, :], func=AF.Sqrt, scale=-1.0, bias=g.one_c[:, 0:1]),
                      reads=[T1b[s], g.constb], writes=[T1b[s]])
                cx.op("dve", lambda: V.tensor_tensor(out=T2[:, s, :], in0=T2[:, s, :], in1=T0[:, s, :], op=ALU.mult),
                      reads=[T2b[s], T0b[s]], writes=[T2b[s]])

            def stageB2(c):
                s = c % 2
                A = nc.scalar.activation
                cx.op("dve", lambda: V.tensor_tensor(out=T2[:, s, :], in0=T2[:, s, :], in1=T1[:, s, :], op=ALU.mult),
                      reads=[T2b[s], T1b[s]], writes=[T2b[s]])
                cx.op("dve", lambda: V.tensor_tensor_scan(out=T0[:, s, :], data0=T3[:, s, :], data1=T2[:, s, :], initial=0.0,
                                                          op0=ALU.mult, op1=ALU.add),
                      reads=[T3b[s], T2b[s]], writes=[T0b[s]])
                cx.op("dve", lambda: V.tensor_tensor(out=rnnT[:, c, :], in0=T0[:, s, :], in1=gl32[:, s, :], op=ALU.mult),
                      reads=[T0b[s], gl32b[s]], writes=[rnnb[c]])

            load3(0)
            load3(1)
            stageA(0)
            stageA2(0)
            stageG(0)
            for c in range(8):
                if c + 1 < 8:
                    stageA(c + 1)
                stageB(c)
                if c + 1 < 8:
                    stageA2(c + 1)
                    stageG(c + 1)
                stageB2(c)
                if c + 2 < 8:
                    load3(c + 2)
            cx.barrier_all()

        if getattr(g, "stop", 9) == 3:
            return
        with ExitStack() as st:
            merged = st.enter_context(nc.sbuf_tensor(f"merged{l}", [128, 8, S], BF16))
            mgb = _grid(8, NT)
            for c in range(8):
                cx.dma("sp" if c % 2 == 0 else "act", x_sb[:, c, :], xres_v[:, c, :],
                       reads=[g.xresb[c][tt] for tt in range(NT)], writes=[xb[c][tt] for tt in range(NT)])
            wga = st.enter_context(nc.sbuf_tensor(f"wga{l}", [128, 2, 8, 128], BF16)); wgab = [Buf(), Buf()]
            wgl = st.enter_context(nc.sbuf_tensor(f"wgl{l}", [128, 2, 8, 128], BF16)); wglb = [Buf(), Buf()]
            wpp = st.enter_context(nc.sbuf_tensor(f"wpp{l}", [128, 2, 4, 128], BF16)); wppb = [Buf(), Buf()]
            wrr = st.enter_context(nc.sbuf_tensor(f"wrr{l}", [128, 2, 8, 128], BF16)); wrrb = [Buf(), Buf()]
            sga = st.enter_context(nc.sbuf_tensor(f"sga{l}", [128, 2, TT], F32)); sgab = [Buf(), Buf()]
            sgl = st.enter_context(nc.sbuf_tensor(f"sgl{l}", [128, 2, TT], F32)); sglb = [Buf(), Buf()]
            wp_v = W["w_att_proj"][l].rearrange("(kc p) n -> p kc n", p=128)
            wr_v = W["w_rnn_proj"][l].rearrange("(kc p) n -> p kc n", p=128)
            wo_v = W["w_out"][l].rearrange("(kc p) n -> p kc n", p=128)
            with ExitStack() as s4:
                pga = s4.enter_context(nc.psum_tensor(f"pga{l}", [128, 2, TT], F32)); pgab = [Buf(), Buf()]
                pgl = s4.enter_context(nc.psum_tensor(f"pgl{l}", [128, 2, TT], F32)); pglb = [Buf(), Buf()]
                ppp = s4.enter_context(nc.psum_tensor(f"ppp{l}", [128, 2, TT], F32)); pppb = [Buf(), Buf()]
                prr = s4.enter_context(nc.psum_tensor(f"prr{l}", [128, 2, TT], F32)); prrb = [Buf(), Buf()]

                def load4(c):
                    s = c % 2
                    csl = slice(c * 128, (c + 1) * 128)
                    cx.dma("pool", wga[:, s, :, :], w_in_v[:, :, C_GA + c * 128:C_GA + (c + 1) * 128], writes=[wgab[s]])
                    cx.dma("pool", wgl[:, s, :, :], w_in_v[:, :, C_GL + c * 128:C_GL + (c + 1) * 128], writes=[wglb[s]])
                    cx.dma("pool", wpp[:, s, :, :], wp_v[:, :, csl], writes=[wppb[s]])
                    cx.dma("pool", wrr[:, s, :, :], wr_v[:, :, csl], writes=[wrrb[s]])
                load4(0)
                k4 = 0
                for c in range(8):
                    s = c % 2
                    if c + 1 < 8:
                        load4(c + 1)
                    for tt in range(NT):
                        tsl = slice(tt * TT, (tt + 1) * TT)
                        p = k4 % 2; k4 += 1
                        for kc in range(8):
                            cx.op("pe", lambda kc=kc: nc.tensor.matmul(
                                pga[:, p, :], lhsT=wga[:, s, kc, :], rhs=xn_sb[:, kc, tsl], start=(kc == 0), stop=(kc == 7)),
                                reads=[wgab[s], xnb[kc][tt]], writes=[pgab[p]], sig=(kc == 7))
                        cx.op("act", lambda: nc.scalar.activation(out=sga[:, p, :], in_=pga[:, p, :], func=AF.Sigmoid),
                              reads=[pgab[p]], writes=[sgab[p]])
                        for kc in range(4):
                            cx.op("pe", lambda kc=kc: nc.tensor.matmul(
                                ppp[:, p, :], lhsT=wpp[:, s, kc, :], rhs=attT[:, kc, tsl], start=(kc == 0), stop=(kc == 3)),
                                reads=[wppb[s]] + attTb[tt * 4:(tt + 1) * 4], writes=[pppb[p]], sig=(kc == 3))
                        cx.op("dve", lambda: nc.vector.tensor_tensor(out=sga[:, p, :], in0=ppp[:, p, :], in1=sga[:, p, :], op=ALU.mult),
                              reads=[pppb[p], sgab[p]], writes=[sgab[p]])
                        for kc in range(8):
                            cx.op("pe", lambda kc=kc: nc.tensor.matmul(
                                pgl[:, p, :], lhsT=wgl[:, s, kc, :], rhs=xn_sb[:, kc, tsl], start=(kc == 0), stop=(kc == 7)),
                                reads=[wglb[s], xnb[kc][tt]], writes=[pglb[p]], sig=(kc == 7))
                        cx.op("act", lambda: nc.scalar.activation(out=sgl[:, p, :], in_=pgl[:, p, :], func=AF.Sigmoid),
                              reads=[pglb[p]], writes=[sglb[p]])
                        for kc in range(8):
                            cx.op("pe", lambda kc=kc: nc.tensor.matmul(
                                prr[:, p, :], lhsT=wrr[:, s, kc, :], rhs=rnnT[:, kc, tsl], start=(kc == 0), stop=(kc == 7)),
                                reads=[wrrb[s]] + rnnb, writes=[prrb[p]], sig=(kc == 7))
                        cx.op("dve", lambda: nc.vector.tensor_tensor(out=sgl[:, p, :], in0=prr[:, p, :], in1=sgl[:, p, :], op=ALU.mult),
                              reads=[prrb[p], sglb[p]], writes=[sglb[p]])
                        cx.op("dve", lambda: nc.vector.tensor_tensor(out=merged[:, c, tsl], in0=sga[:, p, :], in1=sgl[:, p, :], op=ALU.add),
                              reads=[sgab[p], sglb[p]], writes=[mgb[c][tt]])
                cx.barrier_all()
            with ExitStack() as s5:
                wo = s5.enter_context(nc.sbuf_tensor(f"wo{l}", [128, 2, 8, 128], BF16)); wob = [Buf(), Buf()]
                pso = s5.enter_context(nc.psum_tensor(f"pso{l}", [128, 2, TT], F32)); psob = [Buf(), Buf()]

                def load5(c):
                    s = c % 2
                    cx.dma("pool", wo[:, s, :, :], wo_v[:, :, c * 128:(c + 1) * 128], writes=[wob[s]])
                load5(0)
                tiles5 = [(c, tt) for c in range(8) for tt in range(NT)]

                for k, (c, tt) in enumerate(tiles5):
                    s = c % 2
                    if tt == 0 and c + 1 < 8:
                        load5(c + 1)
                    tsl = slice(tt * TT, (tt + 1) * TT)
                    p = k % 2
                    for kc in range(8):
                        cx.op("pe", lambda kc=kc: nc.tensor.matmul(
                            pso[:, p, :], lhsT=wo[:, s, kc, :], rhs=merged[:, kc, tsl], start=(kc == 0), stop=(kc == 7)),
                            reads=[wob[s], mgb[kc][tt]], writes=[psob[p]], sig=(kc == 7))
                    cx.op("dve", lambda: nc.vector.tensor_tensor(out=x_sb[:, c, tsl], in0=pso[:, p, :], in1=x_sb[:, c, tsl], op=ALU.add),
                          reads=[psob[p], xb[c][tt]], writes=[xb[c][tt]])
                cx.barrier_all()


WNAMES = ["ffn1_wg", "ffn1_wu", "ffn1_wd", "w_in", "rg_wa", "rg_wx", "w_att_proj", "w_rnn_proj", "w_out",
          "ffn2_wg", "ffn2_wu", "ffn2_wd"]
WSHAPES = {"ffn1_wg": [DEPTH, D, DFF], "ffn1_wu": [DEPTH, D, DFF], "ffn1_wd": [DEPTH, DFF, D],
           "w_in": [DEPTH, D, N_IN], "rg_wa": [DEPTH, 16, 64, 64], "rg_wx": [DEPTH, 16, 64, 64],
           "w_att_proj": [DEPTH, 512, D], "w_rnn_proj": [DEPTH, D, D], "w_out": [DEPTH, D, D],
           "ffn2_wg": [DEPTH, D, DFF], "ffn2_wu": [DEPTH, D, DFF], "ffn2_wd": [DEPTH, DFF, D]}


def build_program(plan=None, stop=9):
    if plan is None:
        plan = ["ffn1_0", "mix_0", "ffn2_0+ffn1_1", "mix_1", "ffn2_1+final"]
    nc = bass.Bass("TRN2", target_bir_lowering=False)
    xT = nc.dram_tensor("xT", [D, S], F32, kind="ExternalInput").ap()
    prm = nc.dram_tensor("prm", [128, NP], F32, kind="ExternalInput").ap()
    W = {n: nc.dram_tensor(n, WSHAPES[n], F32, kind="ExternalInput").ap() for n in WNAMES}
    yT = nc.dram_tensor("yT", [D, S], F32, kind="ExternalOutput").ap()
    xres = nc.dram_tensor("xres", [D, S], F32, kind="Internal").ap()
    g = G()
    g.nc = nc
    g.stop = stop
    g.xres = xres
    g.xresb = _grid(8, NT)
    xT_v = xT.rearrange("(c p) t -> p c t", p=128)
    yT_v = yT.rearrange("(c p) t -> p c t", p=128)
    xres_v = xres.rearrange("(c p) t -> p c t", p=128)
    with ExitStack() as top:
        cx = Ctx(nc, top)
        g.cx = cx
        g.prm_sb = top.enter_context(nc.sbuf_tensor("prm_sb", [128, NP], F32)); g.prmb = Buf()
        g.ones_bf = top.enter_context(nc.sbuf_tensor("ones_bf", [128, 128], BF16))
        g.ident = top.enter_context(nc.sbuf_tensor("ident", [128, 128], BF16))
        g.negm = top.enter_context(nc.sbuf_tensor("negm", [128, 128], F32))
        g.negb = top.enter_context(nc.sbuf_tensor("negb", [128, 128], BF16))
        g.zer = top.enter_context(nc.sbuf_tensor("zer", [128, 128], F32))
        g.pw3 = top.enter_context(nc.sbuf_tensor("pw3", [128, 32, 4], F32))
        g.eps_c = top.enter_context(nc.sbuf_tensor("eps_c", [128, 1], F32))
        g.one_c = top.enter_context(nc.sbuf_tensor("one_c", [128, 1], F32))
        g.constb = Buf()
        P = nc.gpsimd
        cx.dma("sp", g.prm_sb[:, :], prm[:, :], writes=[g.prmb])
        cx.op("pool", lambda: P.memset(g.ones_bf[:, :], 1.0), writes=[g.constb])
        cx.op("pool", lambda: P.memset(g.zer[:, :], 0.0), writes=[g.constb])
        cx.op("pool", lambda: P.memset(g.eps_c[:, :], EPS), writes=[g.constb])
        cx.op("pool", lambda: P.memset(g.one_c[:, :], 1.0), writes=[g.constb])
        for i in range(IT):
            cx.op("pool", lambda i=i: P.memset(g.pw3[:, i, :], 2.0 ** -(i + 1)), writes=[g.constb])
        cx.op("pool", lambda: P.affine_select(out=g.ident[:, :], in_=g.ones_bf[:, :], pattern=[[1, 128]],
                                              compare_op=ALU.is_equal, fill=0.0, base=0, channel_multiplier=-1),
              reads=[g.constb], writes=[g.constb])
        cx.op("pool", lambda: P.affine_select(out=g.negm[:, :], in_=g.zer[:, :], pattern=[[-1, 128]],
                                              compare_op=ALU.is_ge, fill=-1e30, base=0, channel_multiplier=1),
              reads=[g.constb], writes=[g.constb])
        cx.op("pool", lambda: P.affine_select(out=g.negb[:, :], in_=g.zer[:, :], pattern=[[-1, 128]],
                                              compare_op=ALU.is_ge, fill=NEG, base=0, channel_multiplier=1),
              reads=[g.constb], writes=[g.constb])
        cx.barrier_all()

        def ffn_phase(src_v, steps, final, next_mix_l=None):
            with ExitStack() as st:
                g.uid = getattr(g, "uid", 0) + 1
                x_sb, xb, xn_sb, xnb = g.x_sb, g.xb, g.xn_sb, g.xnb
                fb = FfnBufs(g, st, f"p{g.uid}")
                nm = Normer(g, st, f"p{g.uid}")
                if src_v is not None:
                    for tt in range(NT):
                        tsl = slice(tt * TT, (tt + 1) * TT)
                        cx.dma("sp" if tt % 2 == 0 else "act", x_sb[:, :, tsl], src_v[:, :, tsl],
                               writes=[xb[c][tt] for c in range(8)])

                def gcol_of(l, which):
                    return l * PL + (P_F1 if which == 1 else P_F2)
                g0 = gcol_of(*steps[0])
                for tt in range(NT):
                    nm.tile(x_sb, xb, tt, norm_to_bf16(g, x_sb, xb, xn_sb, xnb, g0))

                def fin(tt, c, rs_ap, rs_buf):
                    tsl = slice(tt * TT, (tt + 1) * TT)
                    cx.op("dve", lambda: nc.vector.scalar_tensor_tensor(
                        out=x_sb[:, c, tsl], in0=x_sb[:, c, tsl], scalar=g.prm_sb[:, P_FIN + c:P_FIN + c + 1],
                        in1=rs_ap, op0=ALU.mult, op1=ALU.mult),
                        reads=[xb[c][tt], rs_buf, g.prmb], writes=[xb[c][tt]])
                    if c == 7:
                        cx.dma("sp" if tt % 2 == 0 else "act", yT_v[:, :, tsl], x_sb[:, :, tsl],
                               reads=[xb[cc][tt] for cc in range(8)])

                for si, (l, which) in enumerate(steps):
                    last = (si == len(steps) - 1)
                    if not last:
                        gn = gcol_of(*steps[si + 1])
                        hook = lambda tt, gn=gn: nm.tile(x_sb, xb, tt, norm_to_bf16(g, x_sb, xb, xn_sb, xnb, gn))
                    elif final:
                        hook = lambda tt: nm.tile(x_sb, xb, tt, fin)
                    else:
                        def hook(tt):
                            tsl = slice(tt * TT, (tt + 1) * TT)
                            cx.dma("sp" if tt % 2 == 0 else "act", xres_v[:, :, tsl], x_sb[:, :, tsl],
                                   reads=[xb[c][tt] for c in range(8)], writes=[g.xresb[c][tt] for c in range(8)])
                            if next_mix_l is not None:
                                gm = next_mix_l * PL + P_MX
                                nm.tile(x_sb, xb, tt, norm_to_bf16(g, x_sb, xb, xn_sb, xnb, gm))
                    emit_ffn(g, fb, x_sb, xb, xn_sb, xnb, W[f"ffn{which}_wg"][l], W[f"ffn{which}_wu"][l],
                             W[f"ffn{which}_wd"][l], after_tile=hook)
                cx.barrier_all()

        g.xn_sb = top.enter_context(nc.sbuf_tensor("xn_sb", [128, 8, S], BF16))
        g.xnb = _grid(8, NT)
        g.x_sb = top.enter_context(nc.sbuf_tensor("x_sb", [128, 8, S], F32))
        g.xb = _grid(8, NT)
        first = True
        have_xn = False
        for pi, ph in enumerate(plan):
            nxt = plan[pi + 1] if pi + 1 < len(plan) else None
            if ph.startswith("mix"):
                if first:
                    allx = [b for r in g.xb for b in r]
                    cx.dma("sp", g.x_sb[:, :, :], xT_v, writes=allx)
                    cx.dma("sp", xres_v, g.x_sb[:, :, :], reads=allx, writes=[b for r in g.xresb for b in r])
                    cx.barrier_all()
                emit_mixer(g, int(ph.split("_")[1]), W, have_xn=have_xn)
                have_xn = False
            else:
                steps = []
                final = False
                for part in ph.split("+"):
                    if part == "final":
                        final = True
                    else:
                        steps.append((int(part.split("_")[1]), 1 if part.startswith("ffn1") else 2))
                nml = int(nxt.split("_")[1]) if (nxt is not None and nxt.startswith("mix")) else None
                ffn_phase(xT_v if first else None, steps, final, next_mix_l=nml)
                have_xn = nml is not None
            first = False
        if not plan[-1].endswith("final"):
            cx.dma("sp", yT_v, g.x_sb[:, :, :], reads=[b for r in g.xb for b in r])
        cx.barrier_all()
        g.stats = (cx.n_ins, cx.n_wait)
    return nc


def pack_prm(inp):
    prm = np.zeros((128, NP), np.float32)

    def put(col, v):
        prm[:, col:col + 8] = np.asarray(v, np.float32).reshape(8, 128).T
    for l in range(DEPTH):
        b = l * PL
        put(b + P_F1, inp["ffn1_norm"][l]); put(b + P_MX, inp["mix_norm"][l]); put(b + P_F2, inp["ffn2_norm"][l])
        for j in range(4):
            put(b + P_CW + j * 8, inp["conv_w"][l][j])
        put(b + P_CB, inp["conv_b"][l]); put(b + P_BA, inp["rg_ba"][l]); put(b + P_BX, inp["rg_bx"][l])
        put(b + P_LAM, inp["rg_lam"][l])
    put(P_FIN, inp["final_norm"])
    return prm


def kernel(**inputs):
    inp = {k: np.asarray(v) for k, v in inputs.items()}
    x = inp["x"].astype(np.float32, copy=False)
    B = x.shape[0]
    nc = build_program()
    prm = pack_prm(inp)
    wmap = {n: np.ascontiguousarray(inp[n], dtype=np.float32) for n in WNAMES}
    in_maps = []
    for b in range(B):
        m = {"xT": np.ascontiguousarray(x[b].T), "prm": prm}
        m.update(wmap)
        in_maps.append(m)
    res = run_bass_kernel_spmd(nc, in_maps, core_ids=list(range(B)))
    out = np.stack([np.ascontiguousarray(res.results[b]["yT"].T) for b in range(B)], axis=0)
    return out.astype(np.float32, copy=False)
```
